# Optimizing a Trainium2 kernel written in Bass

```python
import math
import jax, jax.numpy as jnp
from jax import lax
import numpy as np

D_MODEL = 1024
BATCH = 2
SEQ = 16384
DEPTH = 1
DEC_BATCH = 32
DEC_SEQ = 32
PAST_LEN = 2048

CHUNK = 64
SB_HEADS = 8
SB_DIM = 64
SB_WIDTH = SB_HEADS * SB_DIM
Q_BLOCK = 128
DN_HEADS = 4
DN_DIM = 128
DN_WIDTH = DN_HEADS * DN_DIM
CONV_W = 4
DN_CONV_CH = 3 * DN_WIDTH
PEER_HEADS = 8
N_KEYS = 128
N_EXPERTS = N_KEYS * N_KEYS
PEER_KEY_DIM = 256
PEER_HALF = PEER_KEY_DIM // 2
PEER_TOPK = 16
PEER_BLOCK = 256
LN_EPS = 1e-5
RMS_EPS = 1e-6
DEEPNORM_ALPHA = (2 * DEPTH) ** 0.25
DEEPNORM_BETA = (8 * DEPTH) ** -0.25
OFF_DN = 3 * SB_WIDTH
OFF_Z = OFF_DN + DN_CONV_CH
OFF_B = OFF_Z + DN_WIDTH
OFF_A = OFF_B + DN_HEADS
OFF_G = OFF_A + DN_HEADS
IN_COLS = OFF_G + 2 * D_MODEL

kernel_name = 'stickbreak_gdn_peer_streaming_encoder'


def layer_norm(x, g, b):
    xf = x.astype(jnp.float32)
    mu = jnp.mean(xf, axis=-1, keepdims=True)
    var = jnp.mean(jnp.square(xf - mu), axis=-1, keepdims=True)
    return ((xf - mu) * lax.rsqrt(var + LN_EPS) * g + b).astype(x.dtype)


def l2norm(x):
    return x * lax.rsqrt(jnp.sum(jnp.square(x), axis=-1, keepdims=True) + RMS_EPS)


def sb_block(q, k, v, q_pos, k_pos):
    z = jnp.einsum('bqhd,bkhd->bhqk', q, k, preferred_element_type=jnp.float32) / math.sqrt(SB_DIM)
    mask = k_pos[None, :] < q_pos[:, None]
    log_beta = jax.nn.log_sigmoid(z)
    log_1mb = jnp.where(mask, log_beta - z, 0.0)
    later = lax.cumsum(log_1mb, axis=3, reverse=True) - log_1mb
    w = jnp.where(mask, jnp.exp(log_beta + later), 0.0)
    return jnp.einsum('bhqk,bkhd->bqhd', w.astype(v.dtype), v)


def stick_breaking_prompt(q, k, v):
    B, S = q.shape[:2]
    nb = S // Q_BLOCK
    qb = jnp.moveaxis(q.reshape(B, nb, Q_BLOCK, SB_HEADS, SB_DIM), 1, 0)
    k_pos = jnp.arange(S, dtype=jnp.int32)
    q_pos = k_pos.reshape(nb, Q_BLOCK)
    out = lax.map(lambda a: sb_block(a[0], k, v, a[1], k_pos), (qb, q_pos))
    return jnp.moveaxis(out, 0, 1).reshape(B, S, SB_HEADS, SB_DIM)


def stick_breaking_sample(q, k_all, v_all):
    total = k_all.shape[1]
    T = q.shape[1]
    k_pos = jnp.arange(total, dtype=jnp.int32)
    q_pos = jnp.arange(total - T, total, dtype=jnp.int32)
    return sb_block(q, k_all, v_all, q_pos, k_pos)


def gated_delta_rule(q, k, v, g, beta, S0):
    B, T, H, dk = q.shape
    dv = v.shape[-1]
    c = min(CHUNK, T)
    n = T // c

    def blocks(a):
        return jnp.moveaxis(a.reshape((B, n, c, H) + a.shape[3:]), (1, 3), (0, 2))

    qc, kc, vc, bc = blocks(q), blocks(k), blocks(v), blocks(beta)
    gc = jnp.cumsum(blocks(g), axis=-1)
    idx = jnp.arange(c)
    lower_strict = idx[:, None] > idx[None, :]
    lower_incl = idx[:, None] >= idx[None, :]
    decay = jnp.exp(jnp.where(lower_incl, gc[..., :, None] - gc[..., None, :], -jnp.inf))
    kb = kc * bc[..., None]
    A = jnp.where(lower_strict, jnp.einsum('nbhid,nbhjd->nbhij', kb, kc) * decay, 0.0)
    eye = jnp.eye(c, dtype=A.dtype)
    Tinv = lax.linalg.triangular_solve(eye + A, jnp.broadcast_to(eye, A.shape),
                                       left_side=True, lower=True, unit_diagonal=True)
    u = Tinv @ (vc * bc[..., None])
    w = Tinv @ (kb * jnp.exp(gc)[..., None])
    qk = jnp.einsum('nbhid,nbhjd->nbhij', qc, kc) * decay

    def step(S, xs):
        q_i, k_i, u_i, w_i, g_i, qk_i = xs
        v_new = u_i - w_i @ S
        o = (q_i * jnp.exp(g_i)[..., None]) @ S + qk_i @ v_new
        g_last = g_i[..., -1]
        k_dec = k_i * jnp.exp(g_last[..., None] - g_i)[..., None]
        S = S * jnp.exp(g_last)[..., None, None] + jnp.einsum('bhcd,bhce->bhde', k_dec, v_new)
        return S, o

    S_final, o = lax.scan(step, S0, (qc, kc, u, w, gc, qk))
    o = jnp.moveaxis(o, (0, 2), (1, 3)).reshape(B, T, H, dv)
    return o, S_final


def delta_branch(proj, conv_buf, w_conv, a_log, dt_bias, norm_w, S0):
    B, T = proj.shape[:2]
    f32 = jnp.float32
    x_in = proj[..., OFF_DN:OFF_Z]
    xpad = jnp.concatenate([conv_buf.astype(x_in.dtype), x_in], axis=1)
    xc = sum(xpad[:, i:i + T] * w_conv[i] for i in range(CONV_W))
    xc = jax.nn.silu(xc.astype(f32))
    q, k, v = jnp.split(xc, 3, axis=-1)
    q = l2norm(q.reshape(B, T, DN_HEADS, DN_DIM)) * (DN_DIM ** -0.5)
    k = l2norm(k.reshape(B, T, DN_HEADS, DN_DIM))
    v = v.reshape(B, T, DN_HEADS, DN_DIM)
    beta = jax.nn.sigmoid(proj[..., OFF_B:OFF_A].astype(f32))
    g = -jnp.exp(a_log.astype(f32)) * jax.nn.softplus(proj[..., OFF_A:OFF_G].astype(f32) + dt_bias.astype(f32))
    o, S_new = gated_delta_rule(q, k, v, g, beta, S0.astype(f32))
    z = proj[..., OFF_Z:OFF_B].reshape(B, T, DN_HEADS, DN_DIM).astype(f32)
    o = o * lax.rsqrt(jnp.mean(jnp.square(o), axis=-1, keepdims=True) + RMS_EPS) * norm_w * jax.nn.silu(z)
    return o.reshape(B, T, DN_WIDTH).astype(proj.dtype), S_new, xpad[:, -(CONV_W - 1):]


def peer(x2d, w_q, sub_keys, u_tab, v_tab):
    N = x2d.shape[0]
    nb = -(-N // PEER_BLOCK)
    xp = jnp.pad(x2d, ((0, nb * PEER_BLOCK - N), (0, 0))).reshape(nb, PEER_BLOCK, D_MODEL)

    def one(xb):
        q = (xb @ w_q).reshape(PEER_BLOCK, PEER_HEADS, 2, PEER_HALF)
        s = jnp.einsum('nhpd,hpkd->nhpk', q, sub_keys, preferred_element_type=jnp.float32)
        s1, i1 = lax.top_k(s[:, :, 0], PEER_TOPK)
        s2, i2 = lax.top_k(s[:, :, 1], PEER_TOPK)
        cand = (s1[..., :, None] + s2[..., None, :]).reshape(PEER_BLOCK, PEER_HEADS, PEER_TOPK * PEER_TOPK)
        cidx = (i1[..., :, None] * N_KEYS + i2[..., None, :]).reshape(PEER_BLOCK, PEER_HEADS, PEER_TOPK * PEER_TOPK)
        top_s, pos = lax.top_k(cand, PEER_TOPK)
        eidx = jnp.take_along_axis(cidx, pos, axis=-1)
        gate = jax.nn.softmax(top_s, axis=-1)
        u = u_tab[eidx]
        act = jax.nn.gelu(jnp.einsum('nd,nhed->nhe', xb, u, preferred_element_type=jnp.float32), approximate=False)
        v = v_tab[eidx]
        return jnp.einsum('nhe,nhed->nd', (gate * act).astype(v.dtype), v)

    return lax.map(one, xp).reshape(nb * PEER_BLOCK, D_MODEL)[:N]


def encoder_layer(x, past, p):
    (w_in, b_gate, w_conv, a_log, dt_bias, dn_norm_w, w_up_sb, w_up_dn, w_out,
     ln1_g, ln1_b, peer_wq, peer_keys, peer_u, peer_v, ln2_g, ln2_b) = p
    B, T, _ = x.shape
    proj = x @ w_in
    q_sb = proj[..., 0:SB_WIDTH].reshape(B, T, SB_HEADS, SB_DIM)
    k_sb = proj[..., SB_WIDTH:2 * SB_WIDTH].reshape(B, T, SB_HEADS, SB_DIM)
    v_sb = proj[..., 2 * SB_WIDTH:3 * SB_WIDTH].reshape(B, T, SB_HEADS, SB_DIM)
    if past is None:
        o_sb = stick_breaking_prompt(q_sb, k_sb, v_sb)
        conv_buf = jnp.zeros((B, CONV_W - 1, DN_CONV_CH), x.dtype)
        S0 = jnp.zeros((B, DN_HEADS, DN_DIM, DN_DIM), jnp.float32)
    else:
        k_past, v_past, conv_buf, S0 = past
        o_sb = stick_breaking_sample(q_sb, jnp.concatenate([k_past, k_sb], axis=1),
                                     jnp.concatenate([v_past, v_sb], axis=1))
    o_dn, S_new, conv_new = delta_branch(proj, conv_buf, w_conv, a_log, dt_bias, dn_norm_w, S0)
    gates = jax.nn.sigmoid(proj[..., OFF_G:] + b_gate).reshape(B, T, 2, D_MODEL)
    merged = (gates[:, :, 0] * (o_sb.reshape(B, T, SB_WIDTH) @ w_up_sb)
              + gates[:, :, 1] * (o_dn @ w_up_dn))
    h = layer_norm(DEEPNORM_ALPHA * x + merged @ w_out, ln1_g, ln1_b)
    ffn = peer(h.reshape(B * T, D_MODEL), peer_wq, peer_keys, peer_u, peer_v).reshape(B, T, D_MODEL)
    y = layer_norm(DEEPNORM_ALPHA * h + ffn, ln2_g, ln2_b)
    return y, k_sb, v_sb, S_new, conv_new


def setup_inputs(seed: int = 0) -> dict:
    key = jax.random.key(seed)
    ks = jax.random.split(key, 24)
    f32 = jnp.float32

    def nrm(k, shape, scale):
        return jax.random.normal(k, shape, f32) * scale

    col_scale = jnp.ones((IN_COLS,), f32)
    col_scale = col_scale.at[2 * SB_WIDTH:3 * SB_WIDTH].set(DEEPNORM_BETA)
    col_scale = col_scale.at[OFF_DN + 2 * DN_WIDTH:OFF_Z].set(DEEPNORM_BETA)
    return {
        'x_prompt': nrm(ks[0], (BATCH, SEQ, D_MODEL), 1.0),
        'x_sample': nrm(ks[1], (DEC_BATCH, DEC_SEQ, D_MODEL), 1.0),
        'cache_sb_k': nrm(ks[2], (DEPTH, DEC_BATCH, PAST_LEN, SB_HEADS, SB_DIM), 1.0),
        'cache_sb_v': nrm(ks[3], (DEPTH, DEC_BATCH, PAST_LEN, SB_HEADS, SB_DIM), DEEPNORM_BETA),
        'state_dn_ssm': nrm(ks[4], (DEPTH, DEC_BATCH, DN_HEADS, DN_DIM, DN_DIM), 0.1),
        'state_dn_conv': nrm(ks[5], (DEPTH, DEC_BATCH, CONV_W - 1, DN_CONV_CH), 1.0),
        'w_in': nrm(ks[6], (DEPTH, D_MODEL, IN_COLS), D_MODEL ** -0.5) * col_scale,
        'b_gate': nrm(ks[7], (DEPTH, 2 * D_MODEL), 0.02),
        'w_conv': nrm(ks[8], (DEPTH, CONV_W, DN_CONV_CH), CONV_W ** -0.5),
        'a_log': jnp.log(jax.random.uniform(ks[9], (DEPTH, DN_HEADS), f32, 1.0, 16.0)),
        'dt_bias': nrm(ks[10], (DEPTH, DN_HEADS), 0.1),
        'dn_norm_w': 1.0 + nrm(ks[11], (DEPTH, DN_DIM), 0.02),
        'w_up_sb': nrm(ks[12], (DEPTH, SB_WIDTH, D_MODEL), DEEPNORM_BETA * SB_WIDTH ** -0.5),
        'w_up_dn': nrm(ks[13], (DEPTH, DN_WIDTH, D_MODEL), DEEPNORM_BETA * DN_WIDTH ** -0.5),
        'w_out': nrm(ks[14], (DEPTH, D_MODEL, D_MODEL), DEEPNORM_BETA * D_MODEL ** -0.5),
        'ln1_g': 1.0 + nrm(ks[15], (DEPTH, D_MODEL), 0.02),
        'ln1_b': nrm(ks[16], (DEPTH, D_MODEL), 0.02),
        'peer_wq': nrm(ks[17], (DEPTH, D_MODEL, PEER_HEADS * PEER_KEY_DIM), D_MODEL ** -0.5),
        'peer_keys': nrm(ks[18], (DEPTH, PEER_HEADS, 2, N_KEYS, PEER_HALF), PEER_HALF ** -0.5),
        'peer_u': nrm(ks[19], (DEPTH, N_EXPERTS, D_MODEL), D_MODEL ** -0.5),
        'peer_v': nrm(ks[20], (DEPTH, N_EXPERTS, D_MODEL), DEEPNORM_BETA * PEER_HEADS ** -0.5),
        'ln2_g': 1.0 + nrm(ks[21], (DEPTH, D_MODEL), 0.02),
        'ln2_b': nrm(ks[22], (DEPTH, D_MODEL), 0.02),
    }


def reference(x_prompt, x_sample, cache_sb_k, cache_sb_v, state_dn_ssm, state_dn_conv,
              w_in, b_gate, w_conv, a_log, dt_bias, dn_norm_w, w_up_sb, w_up_dn, w_out,
              ln1_g, ln1_b, peer_wq, peer_keys, peer_u, peer_v, ln2_g, ln2_b):
    y_prompt, y_sample = x_prompt, x_sample
    kp, vp, sp, cp, ksm, vsm, ssm, csm = [], [], [], [], [], [], [], []
    for l in range(DEPTH):
        p = (w_in[l], b_gate[l], w_conv[l], a_log[l], dt_bias[l], dn_norm_w[l], w_up_sb[l], w_up_dn[l],
             w_out[l], ln1_g[l], ln1_b[l], peer_wq[l], peer_keys[l], peer_u[l], peer_v[l], ln2_g[l], ln2_b[l])
        y_prompt, k1, v1, s1, c1 = encoder_layer(y_prompt, None, p)
        y_sample, k2, v2, s2, c2 = encoder_layer(
            y_sample, (cache_sb_k[l], cache_sb_v[l], state_dn_conv[l], state_dn_ssm[l]), p)
        kp.append(k1); vp.append(v1); sp.append(s1); cp.append(c1)
        ksm.append(k2); vsm.append(v2); ssm.append(s2); csm.append(c2)
    return (y_prompt, y_sample, jnp.stack(kp), jnp.stack(vp), jnp.stack(ksm), jnp.stack(vsm),
            jnp.stack(sp), jnp.stack(ssm), jnp.stack(cp), jnp.stack(csm))
```

```python
import numpy as np
import ml_dtypes
import concourse.bass as bass
import concourse.mybir as mybir
from concourse.bass_utils import run_bass_kernel_spmd

F32 = mybir.dt.float32
BF16 = mybir.dt.bfloat16
I32 = mybir.dt.int32
U32 = mybir.dt.uint32
AF = mybir.ActivationFunctionType
ALU = mybir.AluOpType

D = 1024
SEQ = 16384
NCORE = 8
WP = 256
NT_FULL = SEQ // WP
PAST = 2048
OFF_DN = 1536
OFF_Z = 3072
OFF_B = 3584
OFF_G = 3592
ALPHA = 2 ** 0.25
LN_EPS = 1e-5
RMS_EPS = 1e-6


class Buf:
    def __init__(self, S, t, name, dma=False):
        self.t = t
        self.name = name
        self.writers = {}
        self.readers = {}
        self.dsem = None
        self.dcnt = 0
        if dma:
            self.dsem = S.nc.alloc_semaphore("d_" + name)
            S.semh[("d", name)] = self.dsem
            S.dmabufs.append(self)

    def __getitem__(self, idx):
        return self.t[idx]


class Sched:
    def __init__(self, nc):
        self.nc = nc
        self.eng = {"pe": nc.tensor, "act": nc.scalar, "dve": nc.vector, "pool": nc.gpsimd, "sp": nc.sync}
        self.semh = {}
        self.cnt = {}
        self.seen = {}
        for k in self.eng:
            self.semh[k] = nc.alloc_semaphore("e_" + k)
            self.cnt[k] = 0
            self.seen[k] = {}
        self.nbuf = 0
        self.dmabufs = []
        self.gbufs = []
        self.n_ins = 0
        self.n_wait = 0

    def sb(self, shape, dtype, name=None, dma=False):
        self.nbuf += 1
        name = "s_" + (name or f"b{self.nbuf}")
        t = self.nc.alloc_sbuf_tensor(name, list(shape), dtype)
        return Buf(self, t, name, dma)

    def ps(self, shape, dtype=F32, name=None):
        self.nbuf += 1
        name = name or f"p{self.nbuf}"
        t = self.nc.alloc_psum_tensor(name, list(shape), dtype)
        return Buf(self, t, name, False)

    def dram(self, name, shape, dtype, kind="Internal"):
        t = self.nc.dram_tensor(name, list(shape), dtype, kind=kind)
        return Buf(self, t, name, False)

    def _wait(self, e, key, val):
        if self.seen[e].get(key, 0) >= val:
            return
        self.eng[e].wait_ge(self.semh[key], val)
        self.seen[e][key] = val
        self.n_wait += 1

    def _deps(self, e, r, w):
        for b in r:
            for (k, v) in b.writers.items():
                if not (k == "pe" and e == "pe"):
                    self._wait(e, k, v)
        for b in w:
            for (k, v) in b.writers.items():
                if not (k == "pe" and e == "pe"):
                    self._wait(e, k, v)
            for (k, v) in b.readers.items():
                if not (k == "pe" and e == "pe"):
                    self._wait(e, k, v)

    def _mark(self, key, val, r, w):
        for b in r:
            if b not in w:
                b.readers[key] = val
        for b in w:
            b.writers[key] = val

    def op(self, e, fn, r=(), w=()):
        r = list(r)
        w = list(w)
        self._deps(e, r, w)
        ins = fn(self.eng[e])
        self.cnt[e] += 1
        ins.then_inc(self.semh[e], 1)
        self._mark(e, self.cnt[e], r, w)
        self.n_ins += 1
        return ins

    def dma(self, e, out, in_, sbuf_buf, r=(), w=(), indirect=None, **kw):
        r = list(r)
        w = list(w)
        self._deps(e, r, w)
        b = sbuf_buf
        if e == "pool":
            if not hasattr(b, "gsem"):
                b.gsem = self.nc.alloc_semaphore("g_" + b.name)
                b.gcnt = 0
                self.semh[("g", b.name)] = b.gsem
                self.gbufs.append(b)
            key = ("g", b.name)
            if b.gcnt > 0:
                self._wait(e, key, 16 * b.gcnt)
            if indirect is None:
                ins = self.eng[e].dma_start(out=out, in_=in_, **kw)
            else:
                ins = self.eng[e].indirect_dma_start(out=out, in_=in_, **indirect)
            b.gcnt += 1
            ins.then_inc(b.gsem, 16)
            self._mark(key, 16 * b.gcnt, r, w)
            self.n_ins += 1
            return ins
        key = ("d", b.name)
        if b.dcnt > 0:
            self._wait(e, key, 16 * b.dcnt)
        ins = self.eng[e].dma_start(out=out, in_=in_, **kw)
        b.dcnt += 1
        ins.then_inc(b.dsem, 16)
        self._mark(key, 16 * b.dcnt, r, w)
        self.n_ins += 1
        return ins

    def finish(self, bufs):
        for b in bufs:
            for (k, v) in list(b.writers.items()) + list(b.readers.items()):
                self._wait("sp", k, v)
        for b in self.dmabufs:
            if b.dcnt > 0:
                self._wait("sp", ("d", b.name), 16 * b.dcnt)
        for b in self.gbufs:
            if b.gcnt > 0:
                self._wait("sp", ("g", b.name), 16 * b.gcnt)
        for k in ("pe", "act", "dve", "pool"):
            if self.cnt[k] > 0:
                self._wait("sp", k, self.cnt[k])


def make_consts():
    c = {}
    i = np.arange(128)
    c["ident"] = np.eye(128, dtype=np.float32)
    c["ones"] = np.ones((128, 128), np.float32)
    c["tri_b"] = (i[:, None] >= i[None, :]).astype(np.float32).astype(ml_dtypes.bfloat16)
    c["ones_b"] = np.ones((128, 128), np.float32).astype(ml_dtypes.bfloat16)
    t = np.arange(WP)
    mp = np.zeros((128, WP // 128, WP), np.float32)
    for o in range(WP // 128):
        mp[:, o, :] = ((128 * o + i[:, None]) < t[None, :])
    c["mask_p"] = mp
    j32 = np.arange(32)
    ms = np.zeros((128, 8, 32), np.float32)
    ms[:32] = (j32[:, None, None] < j32[None, None, :])
    c["mask_s"] = ms.reshape(128, 256)
    m = np.arange(64)
    dn = np.zeros((64, 6, 4, 64), np.float32)
    dn[:, 0] = (m[:, None] <= m[None, :])[:, None, :]
    dn[:, 1] = (m[:, None] > m[None, :])[:, None, :]
    dn[:, 2] = (m[None, :] > m[:, None])[:, None, :]
    dn[:, 3] = (m[None, :] >= m[:, None])[:, None, :]
    dn[:, 4] = np.eye(64)[:, None, :]
    dnp = np.zeros((128, 6 * 4 * 64), np.float32)
    dnp[:64] = dn.reshape(64, -1)
    c["dn"] = dnp
    return c


def build_program(n_ptiles=NT_FULL, do_sample=True, debug=False, stop=None):
    nc = bass.Bass("TRN2", target_bir_lowering=False)
    S = Sched(nc)
    EI = "ExternalInput"
    EO = "ExternalOutput"
    do_prompt = n_ptiles > 0
    n_own = (n_ptiles + 3) // 4 if do_prompt else 0

    din = {}

    def inp(name, shape, dt=F32):
        din[name] = S.dram(name, shape, dt, kind=EI)
        return din[name]

    dout = {}

    def outp(name, shape, dt=F32):
        dout[name] = S.dram(name, shape, dt, kind=EO)
        return dout[name]

    if do_prompt:
        xT_p = inp("xT_p", [D, n_ptiles * WP])
    if do_sample:
        xT_s = inp("xT_s", [D, 128])
        kTc = inp("kTc", [4, 64, 8, PAST])
        vc = inp("vc", [4, PAST, 512])
        S0in = inp("S0", [4, 128, 4, 128])
        convT = inp("convT", [128, 12, 4, 3])
    w_in = inp("w_in", [D, 5640])
    wconvT = inp("wconvT", [128, 12, 4])
    bgT = inp("bgT", [128, 16])
    alog = inp("alog", [128, 4])
    dtb = inp("dtb", [128, 4])
    normw = inp("normw", [128, 1])
    w_up_sb = inp("w_up_sb", [64, 8, D])
    w_up_dn = inp("w_up_dn", [128, 4, D])
    w_out = inp("w_out", [128, 8, D])
    ln1g = inp("ln1g", [128, 8])
    ln1b = inp("ln1b", [128, 8])
    peer_wq = inp("peer_wq", [128, 8, 2048])
    keysT = inp("keysT", [128, 16, 128])
    NEXP = 8 if stop == 'dn' else 16384
    peer_u = inp("peer_u", [NEXP, D])
    peer_v = inp("peer_v", [NEXP, D])
    ln2g = inp("ln2g", [128, D])
    ln2b = inp("ln2b", [128, D])
    c_ident = inp("ident", [128, 128])
    c_ones = inp("ones", [128, 128])
    c_tri_b = inp("tri_b", [128, 128], BF16)
    c_ones_b = inp("ones_b", [128, 128], BF16)
    c_mask_p = inp("mask_p", [128, WP // 128, WP])
    c_mask_s = inp("mask_s", [128, 256])
    c_dn = inp("dn", [128, 6 * 4 * 64])
    kval_in = inp("kval", [128, 4])

    if do_prompt:
        NOWN = n_own
        y_p = outp("y_p", [NOWN * WP, D])
        kn_p = outp("kn_p", [NOWN * WP, 512])
        vn_p = outp("vn_p", [NOWN * WP, 512])
        S_p = outp("S_p", [128, 4, 128])
        cv_p = outp("cv_p", [3, 1536])
    if do_sample:
        y_s = outp("y_s", [128, D])
        kn_s = outp("kn_s", [128, 512])
        vn_s = outp("vn_s", [128, 512])
        S_s = outp("S_s", [4, 128, 4, 128])
        cv_s = outp("cv_s", [4, 3, 1536])
    dbg = {}
    if debug:
        for nm, shp in [("d_osb", [64, 8, 128]), ("d_odn", [128, 4, 128]), ("d_h", [128, 8, 128]), ("d_ffn", [128, D]),
                        ("d_qkv", [128, 12, 128]), ("d_bg", [32, 4, 8]), ("d_mrg", [128, 8, 128])]:
            dbg[nm] = outp(nm, shp, F32)

    wsc = S.dram("wsc", [128, 8, 5640], BF16)
    wsc_pq = S.dram("wsc_pq", [128, 8, 2048], BF16)
    wsc_out = S.dram("wsc_out", [128, 8, D], BF16)
    wsc_udn = S.dram("wsc_udn", [128, 4, D], BF16)
    wsc_usb = S.dram("wsc_usb", [64, 8, D], BF16)
    if do_prompt:
        KTs = [S.dram(f"KTs{i}", [64, 8, WP], BF16) for i in range(n_ptiles)]
        Vs = [S.dram(f"Vs{i}", [128, WP // 128, 512], BF16) for i in range(n_ptiles)]

    ident = S.sb([128, 128], F32, "ident", dma=True)
    ones_f = S.sb([128, 128], F32, "ones_f", dma=True)
    tri_b = S.sb([128, 128], BF16, "tri_b", dma=True)
    ones_b = S.sb([128, 128], BF16, "ones_b", dma=True)
    mask_p = S.sb([128, WP // 128, WP], F32, "mask_p", dma=True)
    mask_s = S.sb([128, 256], F32, "mask_s", dma=True)
    dnc = S.sb([128, 6, 4, 64], F32, "dnc", dma=True)
    wcv = S.sb([128, 12, 4], F32, "wcv", dma=True)
    bg_sb = S.sb([128, 16], F32, "bg_sb", dma=True)
    nA = S.sb([128, 4], F32, "nA", dma=True)
    dtb_sb = S.sb([128, 4], F32, "dtb_sb", dma=True)
    normw_sb = S.sb([128, 1], F32, "normw_sb", dma=True)
    l1g = S.sb([128, 8], F32, "l1g", dma=True)
    l1b = S.sb([128, 8], F32, "l1b", dma=True)
    keys_b = S.sb([128, 16, 128], BF16, "keys_b")
    l2g = S.sb([128, D], F32, "l2g", dma=True)
    l2b = S.sb([128, D], F32, "l2b", dma=True)
    kval = S.sb([128, 4], F32, "kval", dma=True)
    S.dma("sp", kval[:], kval_in[:], kval, r=[kval_in], w=[kval])
    for (sbuf, src, q) in [(ident, c_ident, "sp"), (ones_f, c_ones, "act"), (tri_b, c_tri_b, "sp"), (ones_b, c_ones_b, "act"),
                           (mask_p, c_mask_p, "sp"), (mask_s, c_mask_s, "act"), (wcv, wconvT, "sp"), (bg_sb, bgT, "act"),
                           (nA, alog, "sp"), (dtb_sb, dtb, "act"), (normw_sb, normw, "sp"), (l1g, ln1g, "act"),
                           (l1b, ln1b, "sp"), (l2g, ln2g, "sp"), (l2b, ln2b, "act")]:
        S.dma(q, sbuf[:], src[:], sbuf, r=[src], w=[sbuf])
    S.dma("sp", dnc[:].rearrange("p a h c -> p (a h c)"), c_dn[:], dnc, r=[c_dn], w=[dnc])
    S.op("act", lambda e: e.activation(nA[:], nA[:], AF.Exp), r=[nA], w=[nA])
    S.op("dve", lambda e: e.tensor_scalar(nA[:], nA[:], -1.0, None, ALU.mult), r=[nA], w=[nA])

    banks = [S.ps([128, 512], F32, f"bank{i}") for i in range(8)]
    gp_state = [0]

    def gp():
        b = banks[gp_state[0] % 6]
        gp_state[0] += 1
        return b

    bank_acc = banks[6]
    bank_acc2 = banks[7]

    ubuf = [S.sb([128, D], F32, f"ubuf{i}", dma=True) for i in range(3)]
    stg = ubuf[0:2]
    stgb = [S.sb([128, 1024], BF16, f"stgb{i}", dma=True) for i in range(2)]
    cv_i = [0]

    def convert(src_ap, dst_ap, srcbuf, dstbuf, npart, n):
        i = cv_i[0] % 2
        cv_i[0] += 1
        q = "sp" if i == 0 else "act"
        S.dma(q, stg[i][:npart, :n], src_ap, stg[i], r=[srcbuf], w=[stg[i]])
        eng = "dve" if i == 0 else "pool"
        S.op(eng, lambda e: e.tensor_copy(stgb[i][:npart, :n], stg[i][:npart, :n]), r=[stg[i]], w=[stgb[i]])
        S.dma(q, dst_ap, stgb[i][:npart, :n], stgb[i], r=[stgb[i]], w=[dstbuf])

    w_in_v = w_in[:].rearrange("(k p) c -> p k c", p=128)
    for kc in range(8):
        for c0 in range(0, 5640, 1024):
            n = min(1024, 5640 - c0)
            convert(w_in_v[:, kc, c0:c0 + n], wsc[:, kc, c0:c0 + n], w_in, wsc, 128, n)
    for kc in range(8):
        for c0 in range(0, 2048, 1024):
            convert(peer_wq[:, kc, c0:c0 + 1024], wsc_pq[:, kc, c0:c0 + 1024], peer_wq, wsc_pq, 128, 1024)
    for kc in range(8):
        convert(w_out[:, kc, :], wsc_out[:, kc, :], w_out, wsc_out, 128, 1024)
        convert(w_up_sb[:, kc, :], wsc_usb[:, kc, :], w_up_sb, wsc_usb, 64, 1024)
    for kc in range(4):
        convert(w_up_dn[:, kc, :], wsc_udn[:, kc, :], w_up_dn, wsc_udn, 128, 1024)
    for hp0 in range(0, 16, 8):
        S.dma("sp", stg[0][:, :], keysT[:, hp0:hp0 + 8, :].rearrange("p a k -> p (a k)"), stg[0], r=[keysT], w=[stg[0]])
        S.op("dve", lambda e: e.tensor_copy(keys_b[:, hp0:hp0 + 8, :].rearrange("p a k -> p (a k)"), stg[0][:, :]), r=[stg[0]], w=[keys_b])

    Wba = S.sb([128, 8, 8], BF16, "Wba", dma=True)
    S.dma("sp", Wba[:], wsc[:, :, OFF_B:OFF_B + 8], Wba, r=[wsc], w=[Wba])
    wslots = [S.sb([128, 8, 512], BF16, f"wslot{i}", dma=True) for i in range(2)]
    ws_i = [0]

    def wload(src_ap, srcbuf, npart=128, nk=8):
        i = ws_i[0] % 2
        ws_i[0] += 1
        q = ["sp", "act"][i]
        S.dma(q, wslots[i][:npart, :nk, :], src_ap, wslots[i], r=[srcbuf], w=[wslots[i]])
        return wslots[i]

    WM = WP if do_prompt else 128
    xTf = S.sb([128, 8, WM], F32, "xTf", dma=True)
    xTb = S.sb([128, 8, WM], BF16, "xTb")
    KTcur = S.sb([64, 8, WM], BF16, "KTcur", dma=True)
    Vcur = S.sb([128, 4, 512], BF16, "Vcur", dma=True)
    kvf = [S.sb([128, 512], F32, f"kvf{i}", dma=True) for i in range(2)]
    xin_flat = [S.sb([128, WM + 12], F32, f"xin{i}", dma=True) for i in range(2)]
    hal = S.sb([128, 12, 3], F32, "hal", dma=True)
    qkv = S.sb([128, 12, WM], F32, "qkv", dma=True)
    cvo = S.sb([4, 512], F32, "cvo", dma=True)
    betag = S.sb([64, 4, 8], F32, "betag", dma=True)
    tmp48 = S.sb([64, 8], F32, "tmp48")
    Sst = S.sb([128, 4, 128], F32, "Sst", dma=True)
    qTb = S.sb([64, 8, WM], BF16, "qTb")
    zsT = S.sb([128, 4, WM], BF16, "zsT")
    o_dnT = S.sb([128, 4, WM], BF16, "o_dnT", dma=True)
    oT_sb = S.sb([64, 8, WM], BF16, "oT_sb", dma=True)
    k_tok = S.sb([64, 4, 128], F32, "k_tok")
    v_tok = S.sb([64, 4, 128], F32, "v_tok")
    keg = S.sb([64, 4, 128], F32, "keg")
    sm = S.sb([128, 32], F32, "sm")
    trig = S.sb([64, 4, 64], F32, "trig")
    decT = S.sb([64, 4, 64], F32, "decT")
    decTs = S.sb([64, 4, 64], F32, "decTs")
    qkt = S.sb([64, 4, 64], F32, "qkt")
    Qm = [S.sb([64, 4, 64], F32, f"Qm{i}") for i in range(2)]
    QmT = [S.sb([64, 4, 64], F32, f"QmT{i}") for i in range(2)]
    FmT = [S.sb([64, 4, 64], F32, f"FmT{i}") for i in range(2)]
    Xb = [S.sb([64, 4, 256], F32, f"Xb{i}") for i in range(2)]
    xvb = S.sb([64, 4, 128], F32, "xvb")
    kdec = xvb
    xwT = S.sb([128, 4, 64], F32, "xwT")
    v_new = S.sb([64, 4, 128], F32, "v_new")
    o1s = keg
    o_tok = S.sb([64, 4, 128], F32, "o_tok")
    junk64 = S.sb([64, 128], F32, "junk64")
    KTblk = [S.sb([64, 8, 256], BF16, f"KTblk{i}", dma=True) for i in range(2)]
    Vblk = [S.sb([128, 2, 512], BF16, f"Vblk{i}", dma=True) for i in range(2)]
    kvstg = [ubuf[0], ubuf[1]]
    ZM = 512 if do_prompt else 256
    Eb = [S.sb([128, ZM], F32, f"Eb{i}") for i in range(2)]
    Pb = [S.sb([128, ZM], BF16, f"Pb{i}") for i in range(2)]
    Gb = [S.sb([128, ZM], BF16, f"Gb{i}") for i in range(2)]
    wTb = [S.sb([128, ZM], BF16, f"wTb{i}") for i in range(2)]
    Pacc = S.sb([128, ZM], BF16, "Pacc")
    WL = 8 if stop == 'dn' else WM
    gt = [S.sb([128, WM], F32, f"gt{i}") for i in range(2)]
    sqb, nrm = gt[0], gt[1]
    mtmp = S.sb([128, WL], F32, "mtmp")
    mrgA = qkv
    mrgT = S.sb([128, 8, WL], BF16, "mrgT")
    pre = mrgA
    hTf = xTf
    hTb = xTb
    mean = nrm
    rstd = S.sb([128, WL], F32, "rstd")
    qpT = S.sb([128, 16, WL], BF16, "qpT")
    sc = S.sb([128, 4, 128], F32, "sc")
    scw = S.sb([128, 128], F32, "scw")
    tv = S.sb([128, 16, 16], F32, "tv")
    ti = S.sb([128, 16, 16], U32, "ti")
    tif = S.sb([128, 16, 16], F32, "tif")
    cand = S.sb([128, 16, 16], F32, "cand")
    candw = S.sb([128, 256], F32, "candw")
    cidx = S.sb([128, 16, 16], F32, "cidx")
    tsv = S.sb([128, 8, 16], F32, "tsv")
    eidf = S.sb([128, 128], F32, "eidf")
    eidi = S.sb([128, 128], I32, "eidi")
    gate = S.sb([128, 8, 16], F32, "gate")
    psm = S.sb([128, 32], F32, "psm")
    junkp = S.sb([128, 256], F32, "junkp")
    h_tok = S.sb([128, D], F32, "h_tok")
    junku = stgb[0]
    actv = S.sb([128, 128], F32, "actv")
    coef = S.sb([128, 128], F32, "coef")
    facc = S.sb([128, D], F32, "facc", dma=True)
    ybuf = facc

    triT = lambda C: dnc[:C, 0, 0, :C]
    ustr = lambda C: dnc[:C, 1, 0, :C]
    maskS = lambda C: dnc[:C, 2, :, :C]
    maskI = lambda C: dnc[:C, 3, :, :C]
    identR = lambda C: dnc[:C, 4, :, :C]

    def mm(out, lhsT, rhs, r, w, start=True, stop=True):
        S.op("pe", lambda e: e.matmul(out, lhsT, rhs, start=start, stop=stop), r=r, w=w)

    def tr(out, in_, idn, r, w):
        S.op("pe", lambda e: e.transpose(out, in_, idn), r=r, w=w)

    def act(out, in_, func, r, w, **kw):
        S.op("act", lambda e: e.activation(out, in_, func, **kw), r=r, w=w)

    def proj_fm(wbuf, wcol0, ncc, evac):
        for cc in range(ncc):
            ps = gp()
            for kc in range(8):
                mm(ps[:, :W_], wbuf[:, kc, wcol0 + cc * 128: wcol0 + (cc + 1) * 128], xTb[:, kc, :W_], [wbuf, xTb], [ps], start=(kc == 0), stop=(kc == 7))
            evac(cc, ps)


    cur_scope = [None]

    def scope(name):
        if cur_scope[0] is not None:
            nc.pop_named_scope(cur_scope[0])
        cur_scope[0] = name
        if name is not None:
            nc.push_named_scope(name)

    def tile(*a, **k):
        tile_(*a, **k)
        scope(None)

    def tile_(W, nseq, Wseq, C, xsrc_ap, xsrc_buf, owned, first, last, halo_src, kv_past_blocks, out_row0, sample, ti_idx):
        nonlocal W_
        W_ = W
        y_out, kn_out, vn_out = (y_s, kn_s, vn_s) if sample else (y_p, kn_p, vn_p)
        y_buf, kn_buf, vn_buf = y_out, kn_out, vn_out
        nch = W // C
        if stop == 'pro':
            return
        scope('kvproj')
        S.dma("sp", xTf[:, 0:4, :W], xsrc_ap(0), xTf, r=[xsrc_buf], w=[xTf])
        S.dma("act", xTf[:, 4:8, :W], xsrc_ap(1), xTf, r=[xsrc_buf], w=[xTf])
        S.op("pool", lambda e: e.tensor_copy(xTb[:, :, :W], xTf[:, :, :W]), r=[xTf], w=[xTb])
        if stop == 'x':
            return
        def proj_heads(wbuf, dst):
            for h in range(8):
                ps = gp()
                for kc in range(8):
                    mm(ps[:64, :W], wbuf[:, kc, h * 64:(h + 1) * 64], xTb[:, kc, :W], [wbuf, xTb], [ps], start=(kc == 0), stop=(kc == 7))
                act(dst[:, h, :W], ps[:64, :W], AF.Copy, [ps], [dst])
        Wk = wload(wsc[:, :, 512:1024], wsc)
        proj_heads(Wk, KTcur)
        if not sample:
            S.dma("sp", KTs[ti_idx][:, :, :], KTcur[:, :, :W], KTcur, r=[KTcur], w=[KTs[ti_idx]])
        if stop == 'kt':
            return

        def tokmajor_out(wt, ob_, kb_, q_):
            for g in range(W // 128):
                ps = gp()
                for kc in range(8):
                    mm(ps[:, :], xTb[:, kc, g * 128:(g + 1) * 128], wt[:, kc, :], [xTb, wt], [ps], start=(kc == 0), stop=(kc == 7))
                act(kb_[:, :], ps[:, :], AF.Copy, [ps], [kb_])
                S.dma(q_, ob_[out_row0 + g * 128: out_row0 + (g + 1) * 128, :], kb_[:, :], kb_, r=[kb_], w=[ob_])
        if owned:
            tokmajor_out(Wk, kn_out, kvf[1], "act")
        Wv = wload(wsc[:, :, 1024:1536], wsc)
        gs = 32 if sample else 128
        ng = W // gs
        for g in range(ng):
            ps = gp()
            for kc in range(8):
                mm(ps[:gs, :], xTb[:, kc, g * gs:(g + 1) * gs], Wv[:, kc, :], [xTb, Wv], [ps], start=(kc == 0), stop=(kc == 7))
            S.op("dve", lambda e: e.tensor_copy(Vcur[:gs, g, :], ps[:gs, :]), r=[ps], w=[Vcur])
        if owned:
            tokmajor_out(Wv, vn_out, kvf[0], "sp")
        if not sample:
            S.dma("act", Vs[ti_idx][:, :, :], Vcur[:, :ng, :], Vcur, r=[Vcur], w=[Vs[ti_idx]])
        if stop in ('kv', 'kv1', 'kv2'):
            return
        scope('dnproj')
        if first and not sample:
            S.op("pool", lambda e: e.memset(hal[:], 0.0), r=[], w=[hal])
        for piece in range(3):
            wb = wload(wsc[:, :, OFF_DN + piece * 512: OFF_DN + (piece + 1) * 512], wsc)
            for c4 in range(4):
                cc = piece * 4 + c4
                xbuf_ = xin_flat[cc % 2]
                xb_ = xbuf_[:, :nseq * (Wseq + 3)].rearrange("p (s w) -> p s w", s=nseq)
                ps = gp()
                for kc in range(8):
                    mm(ps[:, :W], wb[:, kc, c4 * 128:(c4 + 1) * 128], xTb[:, kc, :W], [wb, xTb], [ps], start=(kc == 0), stop=(kc == 7))
                act(xb_[:, :nseq, 3:3 + Wseq], ps[:, :W].rearrange("p (s w) -> p s w", s=nseq), AF.Copy, [ps], [xbuf_])
                if sample:
                    S.dma("sp", xb_[:, :nseq, 0:3], convT[:, cc, :, :], xbuf_, r=[convT], w=[xbuf_])
                else:
                    S.op("pool", lambda e: e.tensor_copy(xb_[:, 0, 0:3], hal[:, cc, :]), r=[hal], w=[xbuf_])
                    S.op("pool", lambda e: e.tensor_copy(hal[:, cc, :], xb_[:, 0, Wseq:Wseq + 3]), r=[xbuf_], w=[hal])
                qv = qkv[:, cc, :W].rearrange("p (s w) -> p s w", s=nseq)
                S.op("dve", lambda e: e.tensor_scalar(qv, xb_[:, :nseq, 0:Wseq], wcv[:, cc, 0:1], None, ALU.mult), r=[xbuf_, wcv], w=[qkv])
                for i in range(1, 4):
                    S.op("dve", lambda e: e.scalar_tensor_tensor(qv, xb_[:, :nseq, i:i + Wseq], wcv[:, cc, i:i + 1], qv, ALU.mult, ALU.add), r=[xbuf_, wcv, qkv], w=[qkv])
                act(qkv[:, cc, :W], qkv[:, cc, :W], AF.Silu, [qkv], [qkv])
            if sample or last:
                for s_ in range(nseq):
                    ps = gp()
                    t1 = (s_ + 1) * Wseq
                    for kc in range(8):
                        mm(ps[:3, :], xTb[:, kc, t1 - 3:t1], wb[:, kc, :], [xTb, wb], [ps], start=(kc == 0), stop=(kc == 7))
                    S.op("dve", lambda e: e.tensor_copy(cvo[:3, :], ps[:3, :]), r=[ps], w=[cvo])
                    if sample:
                        S.dma("sp", cv_s[s_, :, piece * 512:(piece + 1) * 512], cvo[:3, :], cvo, r=[cvo], w=[cv_s])
                    else:
                        S.dma("sp", cv_p[:, piece * 512:(piece + 1) * 512], cvo[:3, :], cvo, r=[cvo], w=[cv_p])
        if stop == 'conv':
            return
        for cc in range(8):
            S.op("pool", lambda e: e.tensor_tensor(sqb[:, :W], qkv[:, cc, :W], qkv[:, cc, :W], ALU.mult), r=[qkv], w=[sqb])
            ps = gp()
            mm(ps[:, :W], ones_f[:, :], sqb[:, :W], [ones_f, sqb], [ps])
            act(nrm[:, :W], ps[:, :W], AF.Sqrt, [ps], [nrm], bias=RMS_EPS)
            S.op("dve", lambda e: e.reciprocal(nrm[:, :W], nrm[:, :W]), r=[nrm], w=[nrm])
            sc_ = (128 ** -0.5) if cc < 4 else 1.0
            S.op("dve", lambda e: e.scalar_tensor_tensor(qkv[:, cc, :W], qkv[:, cc, :W], sc_, nrm[:, :W], ALU.mult, ALU.mult), r=[qkv, nrm], w=[qkv])
        if debug and sample:
            S.dma("sp", dbg["d_qkv"][:], qkv[:, :, :128], qkv, r=[qkv], w=[dbg["d_qkv"]])
        for j in range(nch):
            ps = gp()
            for kc in range(8):
                mm(ps[:C, 0:8], xTb[:, kc, j * C:(j + 1) * C], Wba[:, kc, :], [xTb, Wba], [ps], start=(kc == 0), stop=(kc == 7))
            act(betag[:C, j, 0:4], ps[:C, 0:4], AF.Sigmoid, [ps], [betag])
            S.op("dve", lambda e: e.tensor_tensor(tmp48[:C, 0:4], ps[:C, 4:8], dtb_sb[:C, :], ALU.add), r=[ps, dtb_sb], w=[tmp48])
            act(tmp48[:C, 0:4], tmp48[:C, 0:4], AF.Exp, [tmp48], [tmp48])
            act(tmp48[:C, 0:4], tmp48[:C, 0:4], AF.Ln, [tmp48], [tmp48], bias=1.0)
            S.op("dve", lambda e: e.tensor_tensor(betag[:C, j, 4:8], tmp48[:C, 0:4], nA[:C, :], ALU.mult), r=[tmp48, nA], w=[betag])
        if debug and sample:
            S.dma("sp", dbg["d_bg"][:], betag[:32, :, :], betag, r=[betag], w=[dbg["d_bg"]])
        if stop == 'bg':
            return
        scope('qz')
        if owned:
            wb = wload(wsc[:, :, 0:512], wsc)
            proj_heads(wb, qTb)
            wb = wload(wsc[:, :, OFF_Z:OFF_Z + 512], wsc)
            def ev_z(cc, ps):
                act(zsT[:, cc, :W], ps[:, :W], AF.Silu, [ps], [zsT])
            proj_fm(wb, 0, 4, ev_z)
        scope('dnchunks')
        nlev = 6 if C == 64 else 5
        for j in range(nch):
            tc_ = slice(j * C, (j + 1) * C)
            if sample:
                S.dma("sp", Sst[:], S0in[j], Sst, r=[S0in], w=[Sst])
            elif first and j == 0:
                S.op("pool", lambda e: e.memset(Sst[:], 0.0), r=[], w=[Sst])
            ps = gp()
            for h in range(4):
                tr(ps[:C, h * 128:(h + 1) * 128], qkv[:, 4 + h, tc_], ident[:, :], [qkv, ident], [ps])
            act(k_tok[:C].rearrange("p h d -> p (h d)"), ps[:C, :], AF.Copy, [ps], [k_tok])
            ps = gp()
            for h in range(4):
                tr(ps[:C, h * 128:(h + 1) * 128], qkv[:, 8 + h, tc_], ident[:, :], [qkv, ident], [ps])
            S.op("dve", lambda e: e.tensor_copy(v_tok[:C].rearrange("p h d -> p (h d)"), ps[:C, :]), r=[ps], w=[v_tok])
            bgj = betag[:C, j, :]
            ps = gp()
            mm(ps[:C, 0:4], triT(C), betag[:C, j, 4:8], [dnc, betag], [ps])
            mm(ps[:, 8:12], ones_f[:C, :], betag[:C, j, 4:8], [ones_f, betag], [ps])
            S.op("dve", lambda e: e.tensor_copy(sm[:C, 0:4], ps[:C, 0:4]), r=[ps], w=[sm])
            act(sm[:C, 4:8], ps[:C, 0:4], AF.Exp, [ps], [sm])
            act(sm[:, 16:20], ps[:, 8:12], AF.Exp, [ps], [sm])
            S.op("dve", lambda e: e.tensor_tensor(sm[:C, 8:12], ps[:C, 8:12], sm[:C, 0:4], ALU.subtract), r=[ps, sm], w=[sm])
            act(sm[:C, 8:12], sm[:C, 8:12], AF.Exp, [sm], [sm])
            S.op("dve", lambda e: e.tensor_scalar(sm[:C, 12:16], betag[:C, j, 0:4], -1.0, None, ALU.mult), r=[betag], w=[sm])
            for h in range(4):
                S.op("pool", lambda e: e.tensor_scalar(trig[:C, h, :C], triT(C), betag[:C, j, 4 + h:5 + h], None, ALU.mult), r=[dnc, betag], w=[trig])
            ps = gp()
            for h in range(4):
                mm(ps[:C, h * C:(h + 1) * C], ustr(C), trig[:C, h, :C], [dnc, trig], [ps])
            act(decT[:C, :, :C], ps[:C, :4 * C].rearrange("p (h c) -> p h c", h=4), AF.Exp, [ps], [decT])
            S.op("pool", lambda e: e.tensor_tensor(decTs[:C, :, :C], decT[:C, :, :C], maskS(C), ALU.mult), r=[decT, dnc], w=[decTs])
            S.op("pool", lambda e: e.tensor_tensor(decT[:C, :, :C], decT[:C, :, :C], maskI(C), ALU.mult), r=[decT, dnc], w=[decT])
            psK = gp()
            for h in range(4):
                mm(psK[:C, h * C:(h + 1) * C], qkv[:, 4 + h, tc_], qkv[:, 4 + h, tc_], [qkv], [psK])
            psQ = gp()
            for h in range(4):
                mm(psQ[:C, h * C:(h + 1) * C], qkv[:, 4 + h, tc_], qkv[:, h, tc_], [qkv], [psQ])
            S.op("dve", lambda e: e.tensor_tensor(qkt[:C, :, :C], psQ[:C, :4 * C].rearrange("p (h c) -> p h c", h=4), decT[:C, :, :C], ALU.mult), r=[psQ, decT], w=[qkt])
            for h in range(4):
                S.op("dve", lambda e: e.scalar_tensor_tensor(QmT[0][:C, h, :C], psK[:C, h * C:(h + 1) * C], sm[:C, 12 + h:13 + h], decTs[:C, h, :C], ALU.mult, ALU.mult), r=[psK, sm, decTs], w=[QmT[0]])
            ps = gp()
            for h in range(4):
                tr(ps[:C, h * C:(h + 1) * C], QmT[0][:C, h, :C], ident[:C, :C], [QmT[0], ident], [ps])
            act(Qm[0][:C, :, :C], ps[:C, :4 * C].rearrange("p (h c) -> p h c", h=4), AF.Copy, [ps], [Qm[0]])
            S.op("pool", lambda e: e.tensor_tensor(FmT[0][:C, :, :C], QmT[0][:C, :, :C], identR(C), ALU.add), r=[QmT[0], dnc], w=[FmT[0]])
            for h in range(4):
                S.op("pool", lambda e: e.tensor_scalar(keg[:C, h, :], k_tok[:C, h, :], sm[:C, 4 + h:5 + h], None, ALU.mult), r=[k_tok, sm], w=[keg])
            for lv in range(nlev):
                a, b_ = lv % 2, (lv + 1) % 2
                lastlv = (lv == nlev - 1)
                if not lastlv:
                    for hh in range(2):
                        ps = gp()
                        for h2 in range(2):
                            h = hh * 2 + h2
                            if lv == 0:
                                mm(ps[:C, h2 * 256: h2 * 256 + 128], FmT[a][:C, h, :C], v_tok[:C, h, :], [FmT[a], v_tok], [ps])
                                mm(ps[:C, h2 * 256 + 128: h2 * 256 + 256], FmT[a][:C, h, :C], keg[:C, h, :], [FmT[a], keg], [ps])
                            else:
                                mm(ps[:C, h2 * 256:(h2 + 1) * 256], FmT[a][:C, h, :C], Xb[a][:C, h, :], [FmT[a], Xb[a]], [ps])
                        eng = "act" if hh == 0 else "dve"
                        if eng == "act":
                            act(Xb[b_][:C, hh * 2:hh * 2 + 2, :].rearrange("p h d -> p (h d)"), ps[:C, :], AF.Copy, [ps], [Xb[b_]])
                        else:
                            S.op("dve", lambda e: e.tensor_copy(Xb[b_][:C, hh * 2:hh * 2 + 2, :].rearrange("p h d -> p (h d)"), ps[:C, :]), r=[ps], w=[Xb[b_]])
                    ps1 = gp()
                    for h in range(4):
                        mm(ps1[:C, h * C:(h + 1) * C], QmT[a][:C, h, :C], Qm[a][:C, h, :C], [QmT[a], Qm[a]], [ps1])
                    ps2 = gp()
                    for h in range(4):
                        mm(ps2[:C, h * C:(h + 1) * C], Qm[a][:C, h, :C], QmT[a][:C, h, :C], [QmT[a], Qm[a]], [ps2])
                    act(Qm[b_][:C, :, :C], ps1[:C, :4 * C].rearrange("p (h c) -> p h c", h=4), AF.Copy, [ps1], [Qm[b_]])
                    S.op("dve", lambda e: e.tensor_copy(QmT[b_][:C, :, :C], ps2[:C, :4 * C].rearrange("p (h c) -> p h c", h=4)), r=[ps2], w=[QmT[b_]])
                    S.op("pool", lambda e: e.tensor_tensor(FmT[b_][:C, :, :C], QmT[b_][:C, :, :C], identR(C), ALU.add), r=[QmT[b_], dnc], w=[FmT[b_]])
                else:
                    psv = gp()
                    for h in range(4):
                        mm(psv[:C, h * 128:(h + 1) * 128], FmT[a][:C, h, :C], Xb[a][:C, h, 0:128], [FmT[a], Xb[a]], [psv])
                    psw = gp()
                    for h in range(4):
                        mm(psw[:, h * C:(h + 1) * C], Xb[a][:C, h, 128:256], FmT[a][:C, h, :C], [FmT[a], Xb[a]], [psw])
                    for h in range(4):
                        S.op("dve", lambda e: e.tensor_scalar(xvb[:C, h, :], psv[:C, h * 128:(h + 1) * 128], betag[:C, j, h:h + 1], None, ALU.mult), r=[psv, betag], w=[xvb])
                    act(xwT[:, :, :C], psw[:, :4 * C].rearrange("p (h c) -> p h c", h=4), AF.Copy, [psw], [xwT])
            psW = gp()
            for h in range(4):
                mm(psW[:C, h * 128:(h + 1) * 128], xwT[:, h, :C], Sst[:, h, :], [xwT, Sst], [psW])
            for h in range(4):
                S.op("dve", lambda e: e.scalar_tensor_tensor(v_new[:C, h, :], psW[:C, h * 128:(h + 1) * 128], sm[:C, 12 + h:13 + h], xvb[:C, h, :], ALU.mult, ALU.add), r=[psW, sm, xvb], w=[v_new])
            if owned:
                psO1 = gp()
                for h in range(4):
                    mm(psO1[:C, h * 128:(h + 1) * 128], qkv[:, h, tc_], Sst[:, h, :], [qkv, Sst], [psO1])
                for h in range(4):
                    act(o1s[:C, h, :], psO1[:C, h * 128:(h + 1) * 128], AF.Copy, [psO1, sm], [o1s], scale=sm[:C, 4 + h:5 + h])
                psO2 = gp()
                for h in range(4):
                    mm(psO2[:C, h * 128:(h + 1) * 128], qkt[:C, h, :C], v_new[:C, h, :], [qkt, v_new], [psO2])
                S.op("dve", lambda e: e.tensor_tensor(o_tok[:C].rearrange("p h d -> p (h d)"), o1s[:C].rearrange("p h d -> p (h d)"), psO2[:C, :], ALU.add), r=[o1s, psO2], w=[o_tok])
            for h in range(4):
                S.op("pool", lambda e: e.tensor_scalar(kdec[:C, h, :], k_tok[:C, h, :], sm[:C, 8 + h:9 + h], None, ALU.mult), r=[k_tok, sm], w=[kdec])
            for hh in range(2):
                psS = gp()
                for h2 in range(2):
                    h = hh * 2 + h2
                    mm(psS[:, h2 * 128:(h2 + 1) * 128], kdec[:C, h, :], v_new[:C, h, :], [kdec, v_new], [psS])
                for h2 in range(2):
                    h = hh * 2 + h2
                    S.op("dve", lambda e: e.scalar_tensor_tensor(Sst[:, h, :], Sst[:, h, :], sm[:, 16 + h:17 + h], psS[:, h2 * 128:(h2 + 1) * 128], ALU.mult, ALU.add), r=[Sst, sm, psS], w=[Sst])
            if sample:
                S.dma("sp", S_s[j], Sst[:], Sst, r=[Sst], w=[S_s])
            elif last and j == nch - 1:
                S.dma("sp", S_p[:], Sst[:], Sst, r=[Sst], w=[S_p])
            if owned:
                for h in range(4):
                    act(junk64[:C, :], o_tok[:C, h, :], AF.Square, [o_tok], [junk64, sm], accum_out=sm[:C, 28 + h:29 + h])
                act(sm[:C, 24:28], sm[:C, 28:32], AF.Sqrt, [sm], [sm], scale=1.0 / 128, bias=RMS_EPS)
                S.op("dve", lambda e: e.reciprocal(sm[:C, 24:28], sm[:C, 24:28]), r=[sm], w=[sm])
                for h in range(4):
                    S.op("pool", lambda e: e.tensor_scalar(o_tok[:C, h, :], o_tok[:C, h, :], sm[:C, 24 + h:25 + h], None, ALU.mult), r=[o_tok, sm], w=[o_tok])
                ps = gp()
                for h in range(4):
                    tr(ps[:, h * C:(h + 1) * C], o_tok[:C, h, :], ident[:C, :C], [o_tok, ident], [ps])
                S.op("dve", lambda e: e.scalar_tensor_tensor(o_dnT[:, :, tc_], ps[:, :4 * C].rearrange("p (h c) -> p h c", h=4), normw_sb[:, 0:1], zsT[:, :, tc_], ALU.mult, ALU.mult), r=[ps, normw_sb, zsT], w=[o_dnT])
        if not owned or stop == 'dn':
            return
        if debug and sample:
            S.op("pool", lambda e: e.tensor_copy(mrgA[:, 0:4, :128], o_dnT[:, :, :128]), r=[o_dnT], w=[mrgA])
            S.dma("sp", dbg["d_odn"][:], mrgA[:, 0:4, :128], mrgA, r=[mrgA], w=[dbg["d_odn"]])
        if stop == 'dbgodn':
            return
        scope('attn')
        if sample:
            for s_ in range(4):
                grp = [(h, h * 32, s_ * 32, 32) for h in range(8)]
                attention_stream(grp, 256, s_, sample=True)
        else:
            for h in range(0, 8, 2):
                grp = [(h, 0, 0, W), (h + 1, W, 0, W)]
                attention_stream(grp, 2 * W, ti_idx, sample=False)
        if debug and sample:
            S.op("pool", lambda e: e.tensor_copy(mrgA[:64, 0:8, :128], oT_sb[:, :, :128]), r=[oT_sb], w=[mrgA])
            S.dma("sp", dbg["d_osb"][:], mrgA[:64, 0:8, :128], mrgA, r=[mrgA], w=[dbg["d_osb"]])
        if stop in ('attn', 'attn1', 'attn2') or (stop or '').startswith('al'):
            return
        scope('merge')
        for half in range(2):
            for pc in range(2):
                if half == 0:
                    wu = wload(wsc_usb[:, :, pc * 512:(pc + 1) * 512], wsc_usb, npart=64, nk=8)
                else:
                    wu = wload(wsc_udn[:, :, pc * 512:(pc + 1) * 512], wsc_udn, npart=128, nk=4)
                wg = wload(wsc[:, :, OFF_G + half * 1024 + pc * 512: OFF_G + half * 1024 + (pc + 1) * 512], wsc)
                for c4 in range(4):
                    cc = pc * 4 + c4
                    psm_ = gp()
                    if half == 0:
                        for h in range(8):
                            mm(psm_[:, :W], wu[:64, h, c4 * 128:(c4 + 1) * 128], oT_sb[:, h, :W], [wu, oT_sb], [psm_], start=(h == 0), stop=(h == 7))
                    else:
                        for f in range(4):
                            mm(psm_[:, :W], wu[:, f, c4 * 128:(c4 + 1) * 128], o_dnT[:, f, :W], [wu, o_dnT], [psm_], start=(f == 0), stop=(f == 3))
                    psg = gp()
                    for kc in range(8):
                        mm(psg[:, :W], wg[:, kc, c4 * 128:(c4 + 1) * 128], xTb[:, kc, :W], [wg, xTb], [psg], start=(kc == 0), stop=(kc == 7))
                    g_ = gt[cc % 2]
                    act(g_[:, :W], psg[:, :W], AF.Sigmoid, [psg, bg_sb], [g_], bias=bg_sb[:, half * 8 + cc: half * 8 + cc + 1])
                    if half == 0:
                        S.op("dve", lambda e: e.tensor_tensor(mrgA[:, cc, :W], g_[:, :W], psm_[:, :W], ALU.mult), r=[g_, psm_], w=[mrgA])
                    else:
                        S.op("dve", lambda e: e.tensor_tensor(mtmp[:, :W], g_[:, :W], psm_[:, :W], ALU.mult), r=[g_, psm_], w=[mtmp])
                        S.op("pool", lambda e: e.tensor_tensor(mrgT[:, cc, :W], mtmp[:, :W], mrgA[:, cc, :W], ALU.add), r=[mtmp, mrgA], w=[mrgT])
        if debug and sample:
            S.dma("sp", dbg["d_mrg"][:], mrgA[:, 0:8, :128], mrgA, r=[mrgA], w=[dbg["d_mrg"]])
        if stop == 'merge':
            return
        scope('ln1')
        for pc in range(2):
            wo = wload(wsc_out[:, :, pc * 512:(pc + 1) * 512], wsc_out)
            for c4 in range(4):
                dmc = pc * 4 + c4
                ps = gp()
                for cc in range(8):
                    mm(ps[:, :W], wo[:, cc, c4 * 128:(c4 + 1) * 128], mrgT[:, cc, :W], [wo, mrgT], [ps], start=(cc == 0), stop=(cc == 7))
                S.op("dve", lambda e: e.scalar_tensor_tensor(pre[:, dmc, :W], xTf[:, dmc, :W], ALPHA, ps[:, :W], ALU.mult, ALU.add), r=[xTf, ps], w=[pre])
        pss = bank_acc
        for dmc in range(8):
            mm(pss[:, :W], ones_f[:, :], pre[:, dmc, :W], [ones_f, pre], [pss], start=(dmc == 0), stop=(dmc == 7))
        S.op("dve", lambda e: e.tensor_scalar(mean[:, :W], pss[:, :W], 1.0 / D, None, ALU.mult), r=[pss], w=[mean])
        for dmc in range(8):
            S.op("pool", lambda e: e.tensor_tensor(pre[:, dmc, :W], pre[:, dmc, :W], mean[:, :W], ALU.subtract), r=[pre, mean], w=[pre])
        psq = bank_acc2
        for dmc in range(8):
            S.op("pool", lambda e: e.tensor_tensor(sqb[:, :W], pre[:, dmc, :W], pre[:, dmc, :W], ALU.mult), r=[pre], w=[sqb])
            mm(psq[:, :W], ones_f[:, :], sqb[:, :W], [ones_f, sqb], [psq], start=(dmc == 0), stop=(dmc == 7))
        act(rstd[:, :W], psq[:, :W], AF.Sqrt, [psq], [rstd], scale=1.0 / D, bias=LN_EPS)
        S.op("dve", lambda e: e.reciprocal(rstd[:, :W], rstd[:, :W]), r=[rstd], w=[rstd])
        for dmc in range(8):
            S.op("dve", lambda e: e.tensor_tensor(pre[:, dmc, :W], pre[:, dmc, :W], rstd[:, :W], ALU.mult), r=[pre, rstd], w=[pre])
            S.op("dve", lambda e: e.tensor_scalar(hTf[:, dmc, :W], pre[:, dmc, :W], l1g[:, dmc:dmc + 1], l1b[:, dmc:dmc + 1], ALU.mult, ALU.add), r=[pre, l1g, l1b], w=[hTf])
        S.op("pool", lambda e: e.tensor_copy(hTb[:, :, :W], hTf[:, :, :W]), r=[hTf], w=[hTb])
        if debug and sample:
            S.dma("sp", dbg["d_h"][:], hTf[:, :, :128], hTf, r=[hTf], w=[dbg["d_h"]])
        if stop == 'ln1':
            return
        scope('peerq')
        for pc in range(4):
            wq_ = wload(wsc_pq[:, :, pc * 512:(pc + 1) * 512], wsc_pq)
            for c4 in range(4):
                cq = pc * 4 + c4
                ps = gp()
                for kc in range(8):
                    mm(ps[:, :W], wq_[:, kc, c4 * 128:(c4 + 1) * 128], hTb[:, kc, :W], [wq_, hTb], [ps], start=(kc == 0), stop=(kc == 7))
                act(qpT[:, cq, :W], ps[:, :W], AF.Copy, [ps], [qpT])
        for a in range(W // 128):
            scope('peer_topk')
            ta = slice(a * 128, (a + 1) * 128)
            for g4 in range(4):
                ps = gp()
                for q4 in range(4):
                    hp = g4 * 4 + q4
                    mm(ps[:, q4 * 128:(q4 + 1) * 128], qpT[:, hp, ta], keys_b[:, hp, :], [qpT, keys_b], [ps])
                act(sc[:, :, :].rearrange("p a k -> p (a k)"), ps[:, :], AF.Copy, [ps], [sc])
                for q4 in range(4):
                    hp = g4 * 4 + q4
                    S.op("dve", lambda e: e.max(tv[:, hp, 0:8], sc[:, q4, :]), r=[sc], w=[tv])
                    S.op("dve", lambda e: e.max_index(ti[:, hp, 0:8], tv[:, hp, 0:8], sc[:, q4, :]), r=[sc, tv], w=[ti])
                    S.op("dve", lambda e: e.match_replace(scw[:, :], tv[:, hp, 0:8], sc[:, q4, :], -1e30), r=[sc, tv], w=[scw])
                    S.op("dve", lambda e: e.max(tv[:, hp, 8:16], scw[:, :]), r=[scw], w=[tv])
                    S.op("dve", lambda e: e.max_index(ti[:, hp, 8:16], tv[:, hp, 8:16], scw[:, :]), r=[scw, tv], w=[ti])
            S.op("dve", lambda e: e.tensor_copy(tif[:], ti[:]), r=[ti], w=[tif])
            for h in range(8):
                S.op("dve", lambda e: e.tensor_tensor(cand[:], tv[:, 2 * h, :].unsqueeze(2).broadcast_to([128, 16, 16]),
                                                      tv[:, 2 * h + 1, :].unsqueeze(1).broadcast_to([128, 16, 16]), ALU.add), r=[tv], w=[cand])
                S.op("dve", lambda e: e.scalar_tensor_tensor(cidx[:], tif[:, 2 * h, :].unsqueeze(2).broadcast_to([128, 16, 16]), 128.0,
                                                             tif[:, 2 * h + 1, :].unsqueeze(1).broadcast_to([128, 16, 16]), ALU.mult, ALU.add), r=[tif], w=[cidx])
                cf = cand[:].rearrange("p a b -> p (a b)")
                xf = cidx[:].rearrange("p a b -> p (a b)")
                S.op("dve", lambda e: e.max(tsv[:, h, 0:8], cf), r=[cand], w=[tsv])
                S.op("dve", lambda e: e.match_replace(candw[:, :], tsv[:, h, 0:8], cf, -1e30), r=[cand, tsv], w=[candw])
                S.op("dve", lambda e: e.max(tsv[:, h, 8:16], candw[:, :]), r=[candw], w=[tsv])
                for k in range(16):
                    S.op("dve", lambda e: e.scalar_tensor_tensor(junkp[:, :], cf, tsv[:, h, k:k + 1], xf, ALU.is_equal, ALU.mult,
                                                                 accum_out=eidf[:, h * 16 + k: h * 16 + k + 1]), r=[cand, cidx, tsv], w=[junkp, eidf])
                S.op("dve", lambda e: e.tensor_scalar(psm[:, h:h + 1], tsv[:, h, 0:1], -1.0, None, ALU.mult), r=[tsv], w=[psm])
                act(gate[:, h, :], tsv[:, h, :], AF.Exp, [tsv, psm], [gate, psm], bias=psm[:, h:h + 1], accum_out=psm[:, 8 + h:9 + h])
            S.op("dve", lambda e: e.reciprocal(psm[:, 16:24], psm[:, 8:16]), r=[psm], w=[psm])
            for h in range(8):
                S.op("dve", lambda e: e.tensor_scalar(gate[:, h, :], gate[:, h, :], psm[:, 16 + h:17 + h], None, ALU.mult), r=[gate, psm], w=[gate])
            S.op("dve", lambda e: e.tensor_scalar(eidf[:], eidf[:], 16383.0, None, ALU.min), r=[eidf], w=[eidf])
            S.op("dve", lambda e: e.tensor_copy(eidi[:], eidf[:]), r=[eidf], w=[eidi])
            scope('peer_htok')
            for hh in range(2):
                ps = gp()
                for k4 in range(4):
                    kc = hh * 4 + k4
                    tr(ps[:, k4 * 128:(k4 + 1) * 128], hTf[:, kc, ta], ident[:, :], [hTf, ident], [ps])
                act(h_tok[:, hh * 512:(hh + 1) * 512], ps[:, :], AF.Copy, [ps], [h_tok])
            scope('peer_u')
            for s_ in range(128):
                ub = ubuf[s_ % 3]
                S.dma("pool", ub[:], peer_u[:, :], ub, r=[peer_u, eidi], w=[ub],
                      indirect=dict(out_offset=None, in_offset=bass.IndirectOffsetOnAxis(ap=eidi[:, s_:s_ + 1], axis=0)))
                S.op("dve", lambda e: e.scalar_tensor_tensor(junku[:], ub[:], 1.0, h_tok[:], ALU.mult, ALU.mult, accum_out=actv[:, s_:s_ + 1]), r=[ub, h_tok], w=[junku, actv])
            scope('peer_v')
            act(coef[:], actv[:], AF.Gelu, [actv], [coef])
            S.op("dve", lambda e: e.tensor_tensor(coef[:], coef[:], gate[:].rearrange("p h k -> p (h k)"), ALU.mult), r=[coef, gate], w=[coef])
            for s_ in range(128):
                ub = ubuf[s_ % 3]
                S.dma("pool", ub[:], peer_v[:, :], ub, r=[peer_v, eidi], w=[ub],
                      indirect=dict(out_offset=None, in_offset=bass.IndirectOffsetOnAxis(ap=eidi[:, s_:s_ + 1], axis=0)))
                if s_ == 0:
                    S.op("dve", lambda e: e.tensor_scalar(facc[:], ub[:], coef[:, 0:1], None, ALU.mult), r=[ub, coef], w=[facc])
                else:
                    S.op("dve", lambda e: e.scalar_tensor_tensor(facc[:], ub[:], coef[:, s_:s_ + 1], facc[:], ALU.mult, ALU.add), r=[ub, coef, facc], w=[facc])
            if debug and sample:
                S.dma("sp", dbg["d_ffn"][:], facc[:], facc, r=[facc], w=[dbg["d_ffn"]])
            scope('ln2')
            S.op("dve", lambda e: e.scalar_tensor_tensor(facc[:], h_tok[:], ALPHA, facc[:], ALU.mult, ALU.add), r=[h_tok, facc], w=[facc])
            act(junku[:], facc[:], AF.Copy, [facc], [junku, psm], accum_out=psm[:, 24:25])
            S.op("dve", lambda e: e.tensor_scalar(psm[:, 24:25], psm[:, 24:25], -1.0 / D, None, ALU.mult), r=[psm], w=[psm])
            S.op("dve", lambda e: e.tensor_scalar(facc[:], facc[:], psm[:, 24:25], None, ALU.add), r=[facc, psm], w=[facc])
            act(junku[:], facc[:], AF.Square, [facc], [junku, psm], accum_out=psm[:, 25:26])
            act(psm[:, 26:27], psm[:, 25:26], AF.Sqrt, [psm], [psm], scale=1.0 / D, bias=LN_EPS)
            S.op("dve", lambda e: e.reciprocal(psm[:, 26:27], psm[:, 26:27]), r=[psm], w=[psm])
            S.op("dve", lambda e: e.scalar_tensor_tensor(ybuf[:], facc[:], psm[:, 26:27], l2g[:], ALU.mult, ALU.mult), r=[facc, psm, l2g], w=[ybuf])
            S.op("pool", lambda e: e.tensor_tensor(ybuf[:], ybuf[:], l2b[:], ALU.add), r=[ybuf, l2b], w=[ybuf])
            S.dma("sp", y_out[out_row0 + a * 128: out_row0 + (a + 1) * 128, :], ybuf[:], ybuf, r=[ybuf], w=[y_buf])

    def attention_stream(grp, ZW, idx, sample):
        blocks = []
        loaders = []
        if sample:
            s_ = idx
            blocks.append(dict(ktbuf=KTcur, kt=(lambda cc: KTcur[:, cc, s_ * 32:(s_ + 1) * 32]), vbuf=Vcur,
                               v=(lambda h: Vcur[:32, s_, h * 64:(h + 1) * 64]), nk=32, mask=(mask_s[:32, :256], mask_s), load=None))
            for n_, g8 in enumerate(range(7, -1, -1)):
                i2 = n_ % 2

                def load(i2=i2, g8=g8):
                    for hh in range(2):
                        S.dma("sp", kvstg[0][:64, :].rearrange("p (c k) -> p c k", c=4), kTc[s_, :, hh * 4:(hh + 1) * 4, g8 * 256:(g8 + 1) * 256], kvstg[0], r=[kTc], w=[kvstg[0]])
                        S.op("pool", lambda e: e.tensor_copy(KTblk[i2][:, hh * 4:(hh + 1) * 4, :].rearrange("p c k -> p (c k)"), kvstg[0][:64, :]), r=[kvstg[0]], w=[KTblk[i2]])
                    S.dma("act", kvstg[1][:, :].rearrange("p (b c) -> p b c", b=2), vc[s_, g8 * 256:(g8 + 1) * 256, :].rearrange("(b p) c -> p b c", p=128), kvstg[1], r=[vc], w=[kvstg[1]])
                    S.op("dve", lambda e: e.tensor_copy(Vblk[i2][:].rearrange("p b c -> p (b c)"), kvstg[1][:, :]), r=[kvstg[1]], w=[Vblk[i2]])
                for b4 in range(1, -1, -1):
                    blocks.append(dict(ktbuf=KTblk[i2], kt=(lambda cc, i2=i2, b4=b4: KTblk[i2][:, cc, b4 * 128:(b4 + 1) * 128]), vbuf=Vblk[i2],
                                       v=(lambda h, i2=i2, b4=b4: Vblk[i2][:, b4, h * 64:(h + 1) * 64]), nk=128, mask=None,
                                       load=(load if b4 == 1 else None)))
        else:
            i = idx
            nsub = WP // 128
            for o in range(nsub - 1, -1, -1):
                blocks.append(dict(ktbuf=KTcur, kt=(lambda cc, o=o: KTcur[:, cc, o * 128:(o + 1) * 128]), vbuf=Vcur,
                                   v=(lambda h, o=o: Vcur[:, o, h * 64:(h + 1) * 64]), nk=128, mask=(mask_p[:, o, :], mask_p), load=None))
            n_ = 0
            pt = i - 1
            while pt >= 0:
                i2 = n_ % 2
                n_ += 1
                tiles_ = [pt]

                def load(i2=i2, tiles_=tiles_):
                    for u_, t_ in enumerate(tiles_):
                        S.dma("sp", KTblk[i2][:, :, u_ * WP:(u_ + 1) * WP], KTs[t_][:, :, :], KTblk[i2], r=[KTs[t_]], w=[KTblk[i2]])
                        S.dma("act", Vblk[i2][:, u_ * nsub:(u_ + 1) * nsub, :], Vs[t_][:, :, :], Vblk[i2], r=[Vs[t_]], w=[Vblk[i2]])
                firstb = True
                for u_, t_ in enumerate(tiles_):
                    for o in range(nsub - 1, -1, -1):
                        blocks.append(dict(ktbuf=KTblk[i2], kt=(lambda cc, i2=i2, u_=u_, o=o: KTblk[i2][:, cc, u_ * WP + o * 128: u_ * WP + (o + 1) * 128]),
                                           vbuf=Vblk[i2], v=(lambda h, i2=i2, u_=u_, o=o: Vblk[i2][:, u_ * nsub + o, h * 64:(h + 1) * 64]),
                                           nk=128, mask=None, load=(load if firstb else None), kvalid=(t_ if t_ < 3 else None)))
                        firstb = False
                pt -= 1
        if stop == 'attn1' or (stop or '').startswith('al'):
            blocks = blocks[:1]
        if stop == 'attn2':
            blocks = blocks[:3]
        attention_run(grp, ZW, blocks)

    alvl = int(stop[2:]) if (stop or '').startswith('al') else 99

    def attention_run(groups, ZW, blocks):
        po = bank_acc
        nb_ = len(blocks)
        pk = 0
        for bi, blk in enumerate(blocks):
            if blk.get("load") is not None:
                blk["load"]()
            nk = blk["nk"]
            i2 = bi % 2
            zp = gp()
            for (h, zc, qc, Wg) in groups:
                mm(zp[:nk, zc:zc + Wg], blk["kt"](h), qTb[:, h, qc:qc + Wg], [blk["ktbuf"], qTb], [zp])
            if alvl <= 0:
                continue
            act(Eb[i2][:nk, :ZW], zp[:nk, :ZW], AF.Exp, [zp], [Eb[i2]], scale=0.125)
            if alvl <= 1:
                continue
            if blk["mask"] is not None:
                mk, mkb = blk["mask"]
                mw = mk.shape[-1]
                for c0 in range(0, ZW, mw):
                    S.op("pool", lambda e: e.tensor_tensor(Eb[i2][:nk, c0:c0 + mw], Eb[i2][:nk, c0:c0 + mw], mk, ALU.mult), r=[Eb[i2], mkb], w=[Eb[i2]])
            if blk.get("kvalid") is not None:
                kvc = blk["kvalid"]
                S.op("pool", lambda e: e.tensor_scalar(Eb[i2][:nk, :ZW], Eb[i2][:nk, :ZW], kval[:nk, kvc:kvc + 1], None, ALU.mult), r=[Eb[i2], kval], w=[Eb[i2]])
            if alvl <= 2:
                continue
            act(Pb[i2][:nk, :ZW], Eb[i2][:nk, :ZW], AF.Ln, [Eb[i2]], [Pb[i2]], bias=1.0)
            if alvl <= 3:
                continue
            cp = gp()
            mm(cp[:nk, :ZW], tri_b[:nk, :nk], Pb[i2][:nk, :ZW], [tri_b, Pb[i2]], [cp], start=True, stop=(bi == 0))
            if bi > 0:
                mm(cp[:nk, :ZW], ones_b[:pk, :nk], Pacc[:pk, :ZW], [ones_b, Pacc], [cp], start=False, stop=True)
            if alvl <= 4:
                continue
            act(Gb[i2][:nk, :ZW], cp[:nk, :ZW], AF.Exp, [cp], [Gb[i2]], scale=-1.0)
            if alvl <= 5:
                continue
            S.op("dve", lambda e: e.tensor_tensor(wTb[i2][:nk, :ZW], Eb[i2][:nk, :ZW], Gb[i2][:nk, :ZW], ALU.mult), r=[Eb[i2], Gb[i2]], w=[wTb[i2]])
            if alvl <= 6:
                continue
            if bi == 0:
                pk = 128
                if nk < 128:
                    S.op("pool", lambda e: e.memset(Pacc[:, :ZW], 0.0), r=[], w=[Pacc])
                S.op("pool", lambda e: e.tensor_copy(Pacc[:nk, :ZW], Pb[i2][:nk, :ZW]), r=[Pb[i2]], w=[Pacc])
            elif bi < nb_ - 1:
                S.op("pool", lambda e: e.tensor_tensor(Pacc[:nk, :ZW], Pacc[:nk, :ZW], Pb[i2][:nk, :ZW], ALU.add), r=[Pb[i2], Pacc], w=[Pacc])
            if alvl <= 7:
                continue
            for gi_, (h, zc, qc, Wg) in enumerate(groups):
                S.op("pe", lambda e: e.matmul(po[:64, zc:zc + Wg], blk["v"](h), wTb[i2][:nk, zc:zc + Wg], start=(bi == 0 and gi_ == 0), stop=(bi == nb_ - 1),
                                              skip_group_check=True), r=[blk["vbuf"], wTb[i2]], w=[po])
        if alvl <= 8:
            return
        for (h, zc, qc, Wg) in groups:
            S.op("dve", lambda e: e.tensor_copy(oT_sb[:, h, qc:qc + Wg], po[:64, zc:zc + Wg]), r=[po], w=[oT_sb])

    W_ = 128
    if do_sample:
        xsv = xT_s[:].rearrange("(k p) t -> p k t", p=128)
        tile(128, 4, 32, 32, (lambda hf: xsv[:, hf * 4:(hf + 1) * 4, :]), xT_s, True, True, True, None, None, 0, True, 0)
    if do_prompt:
        xpv = xT_p[:].rearrange("(k p) t -> p k t", p=128)
        for p in range(n_ptiles):
            tile(WP, 1, WP, 64, (lambda hf, p=p: xpv[:, hf * 4:(hf + 1) * 4, p * WP:(p + 1) * WP]), xT_p, (p % 4 == 3), (p == 0), (p == n_ptiles - 1),
                 None, None, (p // 4) * WP, False, p)
    S.finish(list(dout.values()))
    return nc, S


def _shared_inputs(w_in, b_gate, w_conv, a_log, dt_bias, dn_norm_w, w_up_sb, w_up_dn, w_out, ln1_g, ln1_b,
                   peer_wq, peer_keys, peer_u, peer_v, ln2_g, ln2_b):
    c = make_consts()
    f = np.ascontiguousarray
    d = dict(c)
    d["w_in"] = f(w_in[0])
    d["wconvT"] = f(w_conv[0].reshape(4, 12, 128).transpose(2, 1, 0))
    d["bgT"] = f(b_gate[0].reshape(16, 128).T)
    d["alog"] = f(np.broadcast_to(a_log[0][None, :], (128, 4)))
    d["dtb"] = f(np.broadcast_to(dt_bias[0][None, :], (128, 4)))
    d["normw"] = f(dn_norm_w[0].reshape(128, 1))
    d["w_up_sb"] = f(w_up_sb[0].reshape(8, 64, D).transpose(1, 0, 2))
    d["w_up_dn"] = f(w_up_dn[0].reshape(4, 128, D).transpose(1, 0, 2))
    d["w_out"] = f(w_out[0].reshape(8, 128, D).transpose(1, 0, 2))
    d["ln1g"] = f(ln1_g[0].reshape(8, 128).T)
    d["ln1b"] = f(ln1_b[0].reshape(8, 128).T)
    d["peer_wq"] = f(peer_wq[0].reshape(8, 128, 2048).transpose(1, 0, 2))
    d["keysT"] = f(peer_keys[0].reshape(16, 128, 128).transpose(2, 0, 1))
    d["peer_u"] = f(peer_u[0])
    d["peer_v"] = f(peer_v[0])
    d["ln2g"] = f(np.broadcast_to(ln2_g[0][None, :], (128, D)))
    d["ln2b"] = f(np.broadcast_to(ln2_b[0][None, :], (128, D)))
    return d


def _sample_inputs(c, x_sample, cache_sb_k, cache_sb_v, state_dn_ssm, state_dn_conv):
    f = np.ascontiguousarray
    sl = slice(4 * c, 4 * c + 4)
    d = {}
    d["xT_s"] = f(x_sample[sl].reshape(128, D).T)
    d["kTc"] = f(cache_sb_k[0, sl].transpose(0, 3, 2, 1))
    d["vc"] = f(cache_sb_v[0, sl].reshape(4, PAST, 512))
    d["S0"] = f(state_dn_ssm[0, sl].transpose(0, 2, 1, 3))
    d["convT"] = f(state_dn_conv[0, sl].reshape(4, 3, 12, 128).transpose(3, 2, 0, 1))
    return d


_PROG = {}
STOP = None


def kernel(x_prompt, x_sample, cache_sb_k, cache_sb_v, state_dn_ssm, state_dn_conv,
           w_in, b_gate, w_conv, a_log, dt_bias, dn_norm_w, w_up_sb, w_up_dn, w_out,
           ln1_g, ln1_b, peer_wq, peer_keys, peer_u, peer_v, ln2_g, ln2_b):
    args = [np.asarray(a, dtype=np.float32) for a in (x_prompt, x_sample, cache_sb_k, cache_sb_v, state_dn_ssm, state_dn_conv,
            w_in, b_gate, w_conv, a_log, dt_bias, dn_norm_w, w_up_sb, w_up_dn, w_out,
            ln1_g, ln1_b, peer_wq, peer_keys, peer_u, peer_v, ln2_g, ln2_b)]
    (x_prompt, x_sample, cache_sb_k, cache_sb_v, state_dn_ssm, state_dn_conv,
     w_in, b_gate, w_conv, a_log, dt_bias, dn_norm_w, w_up_sb, w_up_dn, w_out,
     ln1_g, ln1_b, peer_wq, peer_keys, peer_u, peer_v, ln2_g, ln2_b) = args
    if "nc" not in _PROG:
        _PROG["nc"] = build_program(stop=STOP)[0]
    nc = _PROG["nc"]
    shared = _shared_inputs(w_in, b_gate, w_conv, a_log, dt_bias, dn_norm_w, w_up_sb, w_up_dn, w_out, ln1_g, ln1_b,
                            peer_wq, peer_keys, peer_u, peer_v, ln2_g, ln2_b)
    in_maps = []
    for c in range(NCORE):
        b, r = c // 4, c % 4
        d = dict(shared)
        d.update(_sample_inputs(c, x_sample, cache_sb_k, cache_sb_v, state_dn_ssm, state_dn_conv))
        sh = (3 - r) * WP
        xt = np.zeros((D, SEQ), np.float32)
        xt[:, sh:] = x_prompt[b, :SEQ - sh].T
        d["xT_p"] = xt
        kv = np.ones((128, 4), np.float32)
        kv[:, :3 - r] = 0.0
        d["kval"] = kv
        if STOP == 'dn':
            d["peer_u"] = d["peer_u"][:8]
            d["peer_v"] = d["peer_v"][:8]
        in_maps.append(d)
    res = run_bass_kernel_spmd(nc, in_maps, core_ids=list(range(NCORE))).results
    B = 2
    y_p = np.zeros((B, SEQ, D), np.float32)
    kn_p = np.zeros((1, B, SEQ, 8, 64), np.float32)
    vn_p = np.zeros((1, B, SEQ, 8, 64), np.float32)
    y_s = np.zeros((32, 32, D), np.float32)
    kn_s = np.zeros((1, 32, 32, 8, 64), np.float32)
    vn_s = np.zeros((1, 32, 32, 8, 64), np.float32)
    S_p = np.zeros((1, B, 4, 128, 128), np.float32)
    S_s = np.zeros((1, 32, 4, 128, 128), np.float32)
    cv_p = np.zeros((1, B, 3, 1536), np.float32)
    cv_s = np.zeros((1, 32, 3, 1536), np.float32)
    for c in range(NCORE):
        b, r = c // 4, c % 4
        o = res[c]
        for m in range(NT_FULL // 4):
            t0 = (4 * m + r) * WP
            y_p[b, t0:t0 + WP] = o["y_p"][m * WP:(m + 1) * WP]
            kn_p[0, b, t0:t0 + WP] = o["kn_p"][m * WP:(m + 1) * WP].reshape(WP, 8, 64)
            vn_p[0, b, t0:t0 + WP] = o["vn_p"][m * WP:(m + 1) * WP].reshape(WP, 8, 64)
        if r == 3:
            S_p[0, b] = o["S_p"].transpose(1, 0, 2)
            cv_p[0, b] = o["cv_p"]
        y_s[4 * c:4 * c + 4] = o["y_s"].reshape(4, 32, D)
        kn_s[0, 4 * c:4 * c + 4] = o["kn_s"].reshape(4, 32, 8, 64)
        vn_s[0, 4 * c:4 * c + 4] = o["vn_s"].reshape(4, 32, 8, 64)
        S_s[0, 4 * c:4 * c + 4] = o["S_s"].transpose(0, 2, 1, 3)
        cv_s[0, 4 * c:4 * c + 4] = o["cv_s"]
    return (y_p, y_s, kn_p, vn_p, kn_s, vn_s, S_p, S_s, cv_p, cv_s)
```

```python
import numpy as np
import ml_dtypes
import concourse.bass as bass
import concourse.mybir as mybir
from concourse.bass_utils import run_bass_kernel_spmd

F32 = mybir.dt.float32
BF16 = mybir.dt.bfloat16
I32 = mybir.dt.int32
U32 = mybir.dt.uint32
AF = mybir.ActivationFunctionType
ALU = mybir.AluOpType

D = 1024
SEQ = 16384
NCORE = 8
WP = 256
NT_FULL = SEQ // WP
PAST = 2048
OFF_DN = 1536
OFF_Z = 3072
OFF_B = 3584
OFF_G = 3592
ALPHA = 2 ** 0.25
LN_EPS = 1e-5
RMS_EPS = 1e-6


class Buf:
    def __init__(self, S, t, name, dma=False):
        self.t = t
        self.name = name
        self.writers = {}
        self.readers = {}
        self.dsem = None
        self.dcnt = 0
        if dma:
            self.dsem = S.nc.alloc_semaphore("d_" + name)
            S.semh[("d", name)] = self.dsem
            S.dmabufs.append(self)

    def __getitem__(self, idx):
        return self.t[idx]


class Sched:
    def __init__(self, nc):
        self.nc = nc
        self.eng = {"pe": nc.tensor, "act": nc.scalar, "dve": nc.vector, "pool": nc.gpsimd, "sp": nc.sync}
        self.semh = {}
        self.cnt = {}
        self.seen = {}
        for k in self.eng:
            self.semh[k] = nc.alloc_semaphore("e_" + k)
            self.cnt[k] = 0
            self.seen[k] = {}
        self.nbuf = 0
        self.dmabufs = []
        self.gbufs = []
        self.n_ins = 0
        self.n_wait = 0

    def sb(self, shape, dtype, name=None, dma=False):
        self.nbuf += 1
        name = "s_" + (name or f"b{self.nbuf}")
        t = self.nc.alloc_sbuf_tensor(name, list(shape), dtype)
        return Buf(self, t, name, dma)

    def ps(self, shape, dtype=F32, name=None):
        self.nbuf += 1
        name = name or f"p{self.nbuf}"
        t = self.nc.alloc_psum_tensor(name, list(shape), dtype)
        return Buf(self, t, name, False)

    def dram(self, name, shape, dtype, kind="Internal"):
        t = self.nc.dram_tensor(name, list(shape), dtype, kind=kind)
        return Buf(self, t, name, False)

    def _wait(self, e, key, val):
        if self.seen[e].get(key, 0) >= val:
            return
        self.eng[e].wait_ge(self.semh[key], val)
        self.seen[e][key] = val
        self.n_wait += 1

    def _deps(self, e, r, w):
        for b in r:
            for (k, v) in b.writers.items():
                if not (k == "pe" and e == "pe"):
                    self._wait(e, k, v)
        for b in w:
            for (k, v) in b.writers.items():
                if not (k == "pe" and e == "pe"):
                    self._wait(e, k, v)
            for (k, v) in b.readers.items():
                if not (k == "pe" and e == "pe"):
                    self._wait(e, k, v)

    def _mark(self, key, val, r, w):
        for b in r:
            if b not in w:
                b.readers[key] = val
        for b in w:
            b.writers[key] = val

    def op(self, e, fn, r=(), w=()):
        r = list(r)
        w = list(w)
        self._deps(e, r, w)
        ins = fn(self.eng[e])
        self.cnt[e] += 1
        ins.then_inc(self.semh[e], 1)
        self._mark(e, self.cnt[e], r, w)
        self.n_ins += 1
        return ins

    def dma(self, e, out, in_, sbuf_buf, r=(), w=(), indirect=None, **kw):
        r = list(r)
        w = list(w)
        self._deps(e, r, w)
        b = sbuf_buf
        if e == "pool":
            if not hasattr(b, "gsem"):
                b.gsem = self.nc.alloc_semaphore("g_" + b.name)
                b.gcnt = 0
                self.semh[("g", b.name)] = b.gsem
                self.gbufs.append(b)
            key = ("g", b.name)
            if b.gcnt > 0:
                self._wait(e, key, 16 * b.gcnt)
            if indirect is None:
                ins = self.eng[e].dma_start(out=out, in_=in_, **kw)
            else:
                ins = self.eng[e].indirect_dma_start(out=out, in_=in_, **indirect)
            b.gcnt += 1
            ins.then_inc(b.gsem, 16)
            self._mark(key, 16 * b.gcnt, r, w)
            self.n_ins += 1
            return ins
        key = ("d", b.name)
        if b.dcnt > 0:
            self._wait(e, key, 16 * b.dcnt)
        ins = self.eng[e].dma_start(out=out, in_=in_, **kw)
        b.dcnt += 1
        ins.then_inc(b.dsem, 16)
        self._mark(key, 16 * b.dcnt, r, w)
        self.n_ins += 1
        return ins

    def finish(self, bufs):
        for b in bufs:
            for (k, v) in list(b.writers.items()) + list(b.readers.items()):
                self._wait("sp", k, v)
        for b in self.dmabufs:
            if b.dcnt > 0:
                self._wait("sp", ("d", b.name), 16 * b.dcnt)
        for b in self.gbufs:
            if b.gcnt > 0:
                self._wait("sp", ("g", b.name), 16 * b.gcnt)
        for k in ("pe", "act", "dve", "pool"):
            if self.cnt[k] > 0:
                self._wait("sp", k, self.cnt[k])


def make_consts():
    c = {}
    i = np.arange(128)
    c["ident"] = np.eye(128, dtype=np.float32)
    c["ones"] = np.ones((128, 128), np.float32)
    c["tri_b"] = (i[:, None] >= i[None, :]).astype(np.float32).astype(ml_dtypes.bfloat16)
    c["ones_b"] = np.ones((128, 128), np.float32).astype(ml_dtypes.bfloat16)
    t = np.arange(WP)
    mp = np.zeros((128, WP // 128, WP), np.float32)
    for o in range(WP // 128):
        mp[:, o, :] = ((128 * o + i[:, None]) < t[None, :])
    c["mask_p"] = mp
    j32 = np.arange(32)
    ms = np.zeros((128, 8, 32), np.float32)
    ms[:32] = (j32[:, None, None] < j32[None, None, :])
    c["mask_s"] = ms.reshape(128, 256)
    m = np.arange(64)
    dn = np.zeros((64, 6, 4, 64), np.float32)
    dn[:, 0] = (m[:, None] <= m[None, :])[:, None, :]
    dn[:, 1] = (m[:, None] > m[None, :])[:, None, :]
    dn[:, 2] = (m[None, :] > m[:, None])[:, None, :]
    dn[:, 3] = (m[None, :] >= m[:, None])[:, None, :]
    dn[:, 4] = np.eye(64)[:, None, :]
    dnp = np.zeros((128, 6 * 4 * 64), np.float32)
    dnp[:64] = dn.reshape(64, -1)
    c["dn"] = dnp
    return c


def build_program(n_ptiles=NT_FULL, do_sample=True, debug=False, stop=None):
    nc = bass.Bass("TRN2", target_bir_lowering=False)
    S = Sched(nc)
    EI = "ExternalInput"
    EO = "ExternalOutput"
    do_prompt = n_ptiles > 0
    n_own = (n_ptiles + 3) // 4 if do_prompt else 0

    din = {}

    def inp(name, shape, dt=F32):
        din[name] = S.dram(name, shape, dt, kind=EI)
        return din[name]

    dout = {}

    def outp(name, shape, dt=F32):
        dout[name] = S.dram(name, shape, dt, kind=EO)
        return dout[name]

    if do_prompt:
        xT_p = inp("xT_p", [D, n_ptiles * WP])
    if do_sample:
        xT_s = inp("xT_s", [D, 128])
        kTc = inp("kTc", [4, 64, 8, PAST])
        vc = inp("vc", [4, PAST, 512])
        S0in = inp("S0", [4, 128, 4, 128])
        convT = inp("convT", [128, 12, 4, 3])
    w_in = inp("w_in", [D, 5640])
    wconvT = inp("wconvT", [128, 12, 4])
    bgT = inp("bgT", [128, 16])
    alog = inp("alog", [128, 4])
    dtb = inp("dtb", [128, 4])
    normw = inp("normw", [128, 1])
    w_up_sb = inp("w_up_sb", [64, 8, D])
    w_up_dn = inp("w_up_dn", [128, 4, D])
    w_out = inp("w_out", [128, 8, D])
    ln1g = inp("ln1g", [128, 8])
    ln1b = inp("ln1b", [128, 8])
    peer_wq = inp("peer_wq", [128, 8, 2048])
    keysT = inp("keysT", [128, 16, 128])
    NEXP = 8 if stop == 'dn' else 16384
    peer_u = inp("peer_u", [NEXP, D])
    peer_v = inp("peer_v", [NEXP, D])
    ln2g = inp("ln2g", [128, D])
    ln2b = inp("ln2b", [128, D])
    c_ident = inp("ident", [128, 128])
    c_ones = inp("ones", [128, 128])
    c_tri_b = inp("tri_b", [128, 128], BF16)
    c_ones_b = inp("ones_b", [128, 128], BF16)
    c_mask_p = inp("mask_p", [128, WP // 128, WP])
    c_mask_s = inp("mask_s", [128, 256])
    c_dn = inp("dn", [128, 6 * 4 * 64])
    kval_in = inp("kval", [128, 4])

    if do_prompt:
        NOWN = n_own
        y_p = outp("y_p", [NOWN * WP, D])
        kn_p = outp("kn_p", [NOWN * WP, 512])
        vn_p = outp("vn_p", [NOWN * WP, 512])
        S_p = outp("S_p", [128, 4, 128])
        cv_p = outp("cv_p", [3, 1536])
    if do_sample:
        y_s = outp("y_s", [128, D])
        kn_s = outp("kn_s", [128, 512])
        vn_s = outp("vn_s", [128, 512])
        S_s = outp("S_s", [4, 128, 4, 128])
        cv_s = outp("cv_s", [4, 3, 1536])
    dbg = {}
    if debug:
        for nm, shp in [("d_osb", [64, 8, 128]), ("d_odn", [128, 4, 128]), ("d_h", [128, 8, 128]), ("d_ffn", [128, D]),
                        ("d_qkv", [128, 12, 128]), ("d_bg", [32, 4, 8]), ("d_mrg", [128, 8, 128])]:
            dbg[nm] = outp(nm, shp, F32)

    wsc = S.dram("wsc", [128, 8, 5640], BF16)
    wsc_pq = S.dram("wsc_pq", [128, 8, 2048], BF16)
    wsc_out = S.dram("wsc_out", [128, 8, D], BF16)
    wsc_udn = S.dram("wsc_udn", [128, 4, D], BF16)
    wsc_usb = S.dram("wsc_usb", [64, 8, D], BF16)
    if do_prompt:
        KTs = [S.dram(f"KTs{i}", [64, 8, WP], BF16) for i in range(n_ptiles)]
        Vs = [S.dram(f"Vs{i}", [128, WP // 128, 512], BF16) for i in range(n_ptiles)]

    ident = S.sb([128, 128], F32, "ident", dma=True)
    ones_f = S.sb([128, 128], F32, "ones_f", dma=True)
    tri_b = S.sb([128, 128], BF16, "tri_b", dma=True)
    ones_b = S.sb([128, 128], BF16, "ones_b", dma=True)
    mask_p = S.sb([128, WP // 128, WP], F32, "mask_p", dma=True)
    mask_s = S.sb([128, 256], F32, "mask_s", dma=True)
    dnc = S.sb([128, 6, 4, 64], F32, "dnc", dma=True)
    wcv = S.sb([128, 12, 4], F32, "wcv", dma=True)
    bg_sb = S.sb([128, 16], F32, "bg_sb", dma=True)
    nA = S.sb([128, 4], F32, "nA", dma=True)
    dtb_sb = S.sb([128, 4], F32, "dtb_sb", dma=True)
    normw_sb = S.sb([128, 1], F32, "normw_sb", dma=True)
    l1g = S.sb([128, 8], F32, "l1g", dma=True)
    l1b = S.sb([128, 8], F32, "l1b", dma=True)
    keys_b = S.sb([128, 16, 128], BF16, "keys_b")
    l2g = S.sb([128, D], F32, "l2g", dma=True)
    l2b = S.sb([128, D], F32, "l2b", dma=True)
    kval = S.sb([128, 4], F32, "kval", dma=True)
    S.dma("sp", kval[:], kval_in[:], kval, r=[kval_in], w=[kval])
    for (sbuf, src, q) in [(ident, c_ident, "sp"), (ones_f, c_ones, "act"), (tri_b, c_tri_b, "sp"), (ones_b, c_ones_b, "act"),
                           (mask_p, c_mask_p, "sp"), (mask_s, c_mask_s, "act"), (wcv, wconvT, "sp"), (bg_sb, bgT, "act"),
                           (nA, alog, "sp"), (dtb_sb, dtb, "act"), (normw_sb, normw, "sp"), (l1g, ln1g, "act"),
                           (l1b, ln1b, "sp"), (l2g, ln2g, "sp"), (l2b, ln2b, "act")]:
        S.dma(q, sbuf[:], src[:], sbuf, r=[src], w=[sbuf])
    S.dma("sp", dnc[:].rearrange("p a h c -> p (a h c)"), c_dn[:], dnc, r=[c_dn], w=[dnc])
    S.op("act", lambda e: e.activation(nA[:], nA[:], AF.Exp), r=[nA], w=[nA])
    S.op("dve", lambda e: e.tensor_scalar(nA[:], nA[:], -1.0, None, ALU.mult), r=[nA], w=[nA])

    banks = [S.ps([128, 512], F32, f"bank{i}") for i in range(8)]
    gp_state = [0]

    def gp():
        b = banks[gp_state[0] % 6]
        gp_state[0] += 1
        return b

    bank_acc = banks[6]
    bank_acc2 = banks[7]

    ubuf = [S.sb([128, D], F32, f"ubuf{i}", dma=True) for i in range(3)]
    stg = ubuf[0:2]
    stgb = [S.sb([128, 1024], BF16, f"stgb{i}", dma=True) for i in range(2)]
    cv_i = [0]

    def convert(src_ap, dst_ap, srcbuf, dstbuf, npart, n):
        i = cv_i[0] % 2
        cv_i[0] += 1
        q = "sp" if i == 0 else "act"
        S.dma(q, stg[i][:npart, :n], src_ap, stg[i], r=[srcbuf], w=[stg[i]])
        eng = "dve" if i == 0 else "pool"
        S.op(eng, lambda e: e.tensor_copy(stgb[i][:npart, :n], stg[i][:npart, :n]), r=[stg[i]], w=[stgb[i]])
        S.dma(q, dst_ap, stgb[i][:npart, :n], stgb[i], r=[stgb[i]], w=[dstbuf])

    w_in_v = w_in[:].rearrange("(k p) c -> p k c", p=128)
    for kc in range(8):
        for c0 in range(0, 5640, 1024):
            n = min(1024, 5640 - c0)
            convert(w_in_v[:, kc, c0:c0 + n], wsc[:, kc, c0:c0 + n], w_in, wsc, 128, n)
    for kc in range(8):
        for c0 in range(0, 2048, 1024):
            convert(peer_wq[:, kc, c0:c0 + 1024], wsc_pq[:, kc, c0:c0 + 1024], peer_wq, wsc_pq, 128, 1024)
    for kc in range(8):
        convert(w_out[:, kc, :], wsc_out[:, kc, :], w_out, wsc_out, 128, 1024)
        convert(w_up_sb[:, kc, :], wsc_usb[:, kc, :], w_up_sb, wsc_usb, 64, 1024)
    for kc in range(4):
        convert(w_up_dn[:, kc, :], wsc_udn[:, kc, :], w_up_dn, wsc_udn, 128, 1024)
    for hp0 in range(0, 16, 8):
        S.dma("sp", stg[0][:, :], keysT[:, hp0:hp0 + 8, :].rearrange("p a k -> p (a k)"), stg[0], r=[keysT], w=[stg[0]])
        S.op("dve", lambda e: e.tensor_copy(keys_b[:, hp0:hp0 + 8, :].rearrange("p a k -> p (a k)"), stg[0][:, :]), r=[stg[0]], w=[keys_b])

    Wba = S.sb([128, 8, 8], BF16, "Wba", dma=True)
    S.dma("sp", Wba[:], wsc[:, :, OFF_B:OFF_B + 8], Wba, r=[wsc], w=[Wba])
    wslots = [S.sb([128, 8, 512], BF16, f"wslot{i}", dma=True) for i in range(2)]
    ws_i = [0]

    def wload(src_ap, srcbuf, npart=128, nk=8):
        i = ws_i[0] % 2
        ws_i[0] += 1
        q = ["sp", "act"][i]
        S.dma(q, wslots[i][:npart, :nk, :], src_ap, wslots[i], r=[srcbuf], w=[wslots[i]])
        return wslots[i]

    WM = WP if do_prompt else 128
    xTf = S.sb([128, 8, WM], F32, "xTf", dma=True)
    xTb = S.sb([128, 8, WM], BF16, "xTb")
    KTcur = S.sb([64, 8, WM], BF16, "KTcur", dma=True)
    Vcur = S.sb([128, 4, 512], BF16, "Vcur", dma=True)
    kvf = [S.sb([128, 512], F32, f"kvf{i}", dma=True) for i in range(2)]
    xin_flat = [S.sb([128, WM + 12], F32, f"xin{i}", dma=True) for i in range(2)]
    hal = S.sb([128, 12, 3], F32, "hal", dma=True)
    qkv = S.sb([128, 12, WM], F32, "qkv", dma=True)
    cvo = S.sb([4, 512], F32, "cvo", dma=True)
    betag = S.sb([64, 4, 8], F32, "betag", dma=True)
    tmp48 = S.sb([64, 8], F32, "tmp48")
    Sst = S.sb([128, 4, 128], F32, "Sst", dma=True)
    qTb = S.sb([64, 8, WM], BF16, "qTb")
    zsT = S.sb([128, 4, WM], BF16, "zsT")
    o_dnT = S.sb([128, 4, WM], BF16, "o_dnT", dma=True)
    oT_sb = S.sb([64, 8, WM], BF16, "oT_sb", dma=True)
    k_tok = S.sb([64, 4, 128], F32, "k_tok")
    v_tok = S.sb([64, 4, 128], F32, "v_tok")
    keg = S.sb([64, 4, 128], F32, "keg")
    sm = S.sb([128, 32], F32, "sm")
    trig = S.sb([64, 4, 64], F32, "trig")
    decT = S.sb([64, 4, 64], F32, "decT")
    decTs = S.sb([64, 4, 64], F32, "decTs")
    qkt = S.sb([64, 4, 64], F32, "qkt")
    Qm = [S.sb([64, 4, 64], F32, f"Qm{i}") for i in range(2)]
    QmT = [S.sb([64, 4, 64], F32, f"QmT{i}") for i in range(2)]
    FmT = [S.sb([64, 4, 64], F32, f"FmT{i}") for i in range(2)]
    Xb = [S.sb([64, 4, 256], F32, f"Xb{i}") for i in range(2)]
    xvb = S.sb([64, 4, 128], F32, "xvb")
    kdec = xvb
    xwT = S.sb([128, 4, 64], F32, "xwT")
    v_new = S.sb([64, 4, 128], F32, "v_new")
    o1s = keg
    o_tok = S.sb([64, 4, 128], F32, "o_tok")
    junk64 = S.sb([64, 128], F32, "junk64")
    KTblk = [S.sb([64, 8, 256], BF16, f"KTblk{i}", dma=True) for i in range(2)]
    Vblk = [S.sb([128, 2, 512], BF16, f"Vblk{i}", dma=True) for i in range(2)]
    kvstg = [ubuf[0], ubuf[1]]
    ZM = 512 if do_prompt else 256
    Eb = [S.sb([128, ZM], F32, f"Eb{i}") for i in range(2)]
    Pb = [S.sb([128, ZM], BF16, f"Pb{i}") for i in range(2)]
    Gb = [S.sb([128, ZM], BF16, f"Gb{i}") for i in range(2)]
    wTb = [S.sb([128, ZM], BF16, f"wTb{i}") for i in range(2)]
    Pacc = S.sb([128, ZM], BF16, "Pacc")
    WL = 8 if stop == 'dn' else WM
    gt = [S.sb([128, WM], F32, f"gt{i}") for i in range(2)]
    sqb, nrm = gt[0], gt[1]
    mtmp = S.sb([128, WL], F32, "mtmp")
    mrgA = qkv
    mrgT = S.sb([128, 8, WL], BF16, "mrgT")
    pre = mrgA
    hTf = xTf
    hTb = xTb
    mean = nrm
    rstd = S.sb([128, WL], F32, "rstd")
    qpT = S.sb([128, 16, WL], BF16, "qpT")
    sc = S.sb([128, 4, 128], F32, "sc")
    scw = S.sb([128, 128], F32, "scw")
    tv = S.sb([128, 16, 16], F32, "tv")
    ti = S.sb([128, 16, 16], U32, "ti")
    tif = S.sb([128, 16, 16], F32, "tif")
    cand = S.sb([128, 16, 16], F32, "cand")
    candw = S.sb([128, 256], F32, "candw")
    cidx = S.sb([128, 16, 16], F32, "cidx")
    tsv = S.sb([128, 8, 16], F32, "tsv")
    eidf = S.sb([128, 128], F32, "eidf")
    eidi_l = [S.sb([128, 128], I32, f"eidi{i}") for i in range(2)]
    gate_l = [S.sb([128, 8, 16], F32, f"gate{i}") for i in range(2)]
    psm = S.sb([128, 32], F32, "psm")
    junkp = S.sb([128, 256], F32, "junkp")
    h_tok_l = [S.sb([128, D], F32, f"h_tok{i}") for i in range(2)]
    psm2 = S.sb([128, 8], F32, "psm2")
    junku = stgb[0]
    actv = S.sb([128, 128], F32, "actv")
    coef = S.sb([128, 128], F32, "coef")
    facc = S.sb([128, D], F32, "facc", dma=True)
    ybuf = facc

    triT = lambda C: dnc[:C, 0, 0, :C]
    ustr = lambda C: dnc[:C, 1, 0, :C]
    maskS = lambda C: dnc[:C, 2, :, :C]
    maskI = lambda C: dnc[:C, 3, :, :C]
    identR = lambda C: dnc[:C, 4, :, :C]

    def mm(out, lhsT, rhs, r, w, start=True, stop=True):
        S.op("pe", lambda e: e.matmul(out, lhsT, rhs, start=start, stop=stop), r=r, w=w)

    def tr(out, in_, idn, r, w):
        S.op("pe", lambda e: e.transpose(out, in_, idn), r=r, w=w)

    def act(out, in_, func, r, w, **kw):
        S.op("act", lambda e: e.activation(out, in_, func, **kw), r=r, w=w)

    def proj_fm(wbuf, wcol0, ncc, evac):
        for cc in range(ncc):
            ps = gp()
            for kc in range(8):
                mm(ps[:, :W_], wbuf[:, kc, wcol0 + cc * 128: wcol0 + (cc + 1) * 128], xTb[:, kc, :W_], [wbuf, xTb], [ps], start=(kc == 0), stop=(kc == 7))
            evac(cc, ps)


    cur_scope = [None]

    def scope(name):
        return

    pending = []

    def tile(*a, **k):
        for _ in tile_(*a, **k):
            for pg in list(pending):
                try:
                    next(pg)
                except StopIteration:
                    pending.remove(pg)

    def tile_(W, nseq, Wseq, C, xsrc_ap, xsrc_buf, owned, first, last, halo_src, kv_past_blocks, out_row0, sample, ti_idx):
        nonlocal W_
        W_ = W
        y_out, kn_out, vn_out = (y_s, kn_s, vn_s) if sample else (y_p, kn_p, vn_p)
        y_buf, kn_buf, vn_buf = y_out, kn_out, vn_out
        nch = W // C
        if stop == 'pro':
            return
        scope('kvproj')
        S.dma("sp", xTf[:, 0:4, :W], xsrc_ap(0), xTf, r=[xsrc_buf], w=[xTf])
        S.dma("act", xTf[:, 4:8, :W], xsrc_ap(1), xTf, r=[xsrc_buf], w=[xTf])
        S.op("pool", lambda e: e.tensor_copy(xTb[:, :, :W], xTf[:, :, :W]), r=[xTf], w=[xTb])
        if stop == 'x':
            return
        def proj_heads(wbuf, dst):
            for h in range(8):
                ps = gp()
                for kc in range(8):
                    mm(ps[:64, :W], wbuf[:, kc, h * 64:(h + 1) * 64], xTb[:, kc, :W], [wbuf, xTb], [ps], start=(kc == 0), stop=(kc == 7))
                act(dst[:, h, :W], ps[:64, :W], AF.Copy, [ps], [dst])
        Wk = wload(wsc[:, :, 512:1024], wsc)
        proj_heads(Wk, KTcur)
        if not sample:
            S.dma("sp", KTs[ti_idx][:, :, :], KTcur[:, :, :W], KTcur, r=[KTcur], w=[KTs[ti_idx]])
        if stop == 'kt':
            return

        def tokmajor_out(wt, ob_, kb_, q_):
            for g in range(W // 128):
                ps = gp()
                for kc in range(8):
                    mm(ps[:, :], xTb[:, kc, g * 128:(g + 1) * 128], wt[:, kc, :], [xTb, wt], [ps], start=(kc == 0), stop=(kc == 7))
                act(kb_[:, :], ps[:, :], AF.Copy, [ps], [kb_])
                S.dma(q_, ob_[out_row0 + g * 128: out_row0 + (g + 1) * 128, :], kb_[:, :], kb_, r=[kb_], w=[ob_])
        if owned:
            tokmajor_out(Wk, kn_out, kvf[1], "act")
        Wv = wload(wsc[:, :, 1024:1536], wsc)
        gs = 32 if sample else 128
        ng = W // gs
        for g in range(ng):
            ps = gp()
            for kc in range(8):
                mm(ps[:gs, :], xTb[:, kc, g * gs:(g + 1) * gs], Wv[:, kc, :], [xTb, Wv], [ps], start=(kc == 0), stop=(kc == 7))
            S.op("dve", lambda e: e.tensor_copy(Vcur[:gs, g, :], ps[:gs, :]), r=[ps], w=[Vcur])
        if owned:
            tokmajor_out(Wv, vn_out, kvf[0], "sp")
        if not sample:
            S.dma("act", Vs[ti_idx][:, :, :], Vcur[:, :ng, :], Vcur, r=[Vcur], w=[Vs[ti_idx]])
        if stop in ('kv', 'kv1', 'kv2'):
            return
        yield
        scope('dnproj')
        if first and not sample:
            S.op("pool", lambda e: e.memset(hal[:], 0.0), r=[], w=[hal])
        for piece in range(3):
            wb = wload(wsc[:, :, OFF_DN + piece * 512: OFF_DN + (piece + 1) * 512], wsc)
            for c4 in range(4):
                cc = piece * 4 + c4
                xbuf_ = xin_flat[cc % 2]
                xb_ = xbuf_[:, :nseq * (Wseq + 3)].rearrange("p (s w) -> p s w", s=nseq)
                ps = gp()
                for kc in range(8):
                    mm(ps[:, :W], wb[:, kc, c4 * 128:(c4 + 1) * 128], xTb[:, kc, :W], [wb, xTb], [ps], start=(kc == 0), stop=(kc == 7))
                act(xb_[:, :nseq, 3:3 + Wseq], ps[:, :W].rearrange("p (s w) -> p s w", s=nseq), AF.Copy, [ps], [xbuf_])
                if sample:
                    S.dma("sp", xb_[:, :nseq, 0:3], convT[:, cc, :, :], xbuf_, r=[convT], w=[xbuf_])
                else:
                    S.op("pool", lambda e: e.tensor_copy(xb_[:, 0, 0:3], hal[:, cc, :]), r=[hal], w=[xbuf_])
                    S.op("pool", lambda e: e.tensor_copy(hal[:, cc, :], xb_[:, 0, Wseq:Wseq + 3]), r=[xbuf_], w=[hal])
                qv = qkv[:, cc, :W].rearrange("p (s w) -> p s w", s=nseq)
                S.op("dve", lambda e: e.tensor_scalar(qv, xb_[:, :nseq, 0:Wseq], wcv[:, cc, 0:1], None, ALU.mult), r=[xbuf_, wcv], w=[qkv])
                for i in range(1, 4):
                    S.op("dve", lambda e: e.scalar_tensor_tensor(qv, xb_[:, :nseq, i:i + Wseq], wcv[:, cc, i:i + 1], qv, ALU.mult, ALU.add), r=[xbuf_, wcv, qkv], w=[qkv])
                act(qkv[:, cc, :W], qkv[:, cc, :W], AF.Silu, [qkv], [qkv])
            yield
            if sample or last:
                for s_ in range(nseq):
                    ps = gp()
                    t1 = (s_ + 1) * Wseq
                    for kc in range(8):
                        mm(ps[:3, :], xTb[:, kc, t1 - 3:t1], wb[:, kc, :], [xTb, wb], [ps], start=(kc == 0), stop=(kc == 7))
                    S.op("dve", lambda e: e.tensor_copy(cvo[:3, :], ps[:3, :]), r=[ps], w=[cvo])
                    if sample:
                        S.dma("sp", cv_s[s_, :, piece * 512:(piece + 1) * 512], cvo[:3, :], cvo, r=[cvo], w=[cv_s])
                    else:
                        S.dma("sp", cv_p[:, piece * 512:(piece + 1) * 512], cvo[:3, :], cvo, r=[cvo], w=[cv_p])
        if stop == 'conv':
            return
        yield
        for cc in range(8):
            S.op("pool", lambda e: e.tensor_tensor(sqb[:, :W], qkv[:, cc, :W], qkv[:, cc, :W], ALU.mult), r=[qkv], w=[sqb])
            ps = gp()
            mm(ps[:, :W], ones_f[:, :], sqb[:, :W], [ones_f, sqb], [ps])
            act(nrm[:, :W], ps[:, :W], AF.Sqrt, [ps], [nrm], bias=RMS_EPS)
            S.op("dve", lambda e: e.reciprocal(nrm[:, :W], nrm[:, :W]), r=[nrm], w=[nrm])
            sc_ = (128 ** -0.5) if cc < 4 else 1.0
            S.op("dve", lambda e: e.scalar_tensor_tensor(qkv[:, cc, :W], qkv[:, cc, :W], sc_, nrm[:, :W], ALU.mult, ALU.mult), r=[qkv, nrm], w=[qkv])
        if debug and sample:
            S.dma("sp", dbg["d_qkv"][:], qkv[:, :, :128], qkv, r=[qkv], w=[dbg["d_qkv"]])
        for j in range(nch):
            ps = gp()
            for kc in range(8):
                mm(ps[:C, 0:8], xTb[:, kc, j * C:(j + 1) * C], Wba[:, kc, :], [xTb, Wba], [ps], start=(kc == 0), stop=(kc == 7))
            act(betag[:C, j, 0:4], ps[:C, 0:4], AF.Sigmoid, [ps], [betag])
            S.op("dve", lambda e: e.tensor_tensor(tmp48[:C, 0:4], ps[:C, 4:8], dtb_sb[:C, :], ALU.add), r=[ps, dtb_sb], w=[tmp48])
            act(tmp48[:C, 0:4], tmp48[:C, 0:4], AF.Exp, [tmp48], [tmp48])
            act(tmp48[:C, 0:4], tmp48[:C, 0:4], AF.Ln, [tmp48], [tmp48], bias=1.0)
            S.op("dve", lambda e: e.tensor_tensor(betag[:C, j, 4:8], tmp48[:C, 0:4], nA[:C, :], ALU.mult), r=[tmp48, nA], w=[betag])
        if debug and sample:
            S.dma("sp", dbg["d_bg"][:], betag[:32, :, :], betag, r=[betag], w=[dbg["d_bg"]])
        if stop == 'bg':
            return
        yield
        scope('qz')
        if owned:
            wb = wload(wsc[:, :, 0:512], wsc)
            proj_heads(wb, qTb)
            wb = wload(wsc[:, :, OFF_Z:OFF_Z + 512], wsc)
            def ev_z(cc, ps):
                act(zsT[:, cc, :W], ps[:, :W], AF.Silu, [ps], [zsT])
            proj_fm(wb, 0, 4, ev_z)
        scope('dnchunks')
        nlev = 6 if C == 64 else 5
        for j in range(nch):
            tc_ = slice(j * C, (j + 1) * C)
            if sample:
                S.dma("sp", Sst[:], S0in[j], Sst, r=[S0in], w=[Sst])
            elif first and j == 0:
                S.op("pool", lambda e: e.memset(Sst[:], 0.0), r=[], w=[Sst])
            ps = gp()
            for h in range(4):
                tr(ps[:C, h * 128:(h + 1) * 128], qkv[:, 4 + h, tc_], ident[:, :], [qkv, ident], [ps])
            act(k_tok[:C].rearrange("p h d -> p (h d)"), ps[:C, :], AF.Copy, [ps], [k_tok])
            ps = gp()
            for h in range(4):
                tr(ps[:C, h * 128:(h + 1) * 128], qkv[:, 8 + h, tc_], ident[:, :], [qkv, ident], [ps])
            S.op("dve", lambda e: e.tensor_copy(v_tok[:C].rearrange("p h d -> p (h d)"), ps[:C, :]), r=[ps], w=[v_tok])
            bgj = betag[:C, j, :]
            ps = gp()
            mm(ps[:C, 0:4], triT(C), betag[:C, j, 4:8], [dnc, betag], [ps])
            mm(ps[:, 8:12], ones_f[:C, :], betag[:C, j, 4:8], [ones_f, betag], [ps])
            S.op("dve", lambda e: e.tensor_copy(sm[:C, 0:4], ps[:C, 0:4]), r=[ps], w=[sm])
            act(sm[:C, 4:8], ps[:C, 0:4], AF.Exp, [ps], [sm])
            act(sm[:, 16:20], ps[:, 8:12], AF.Exp, [ps], [sm])
            S.op("dve", lambda e: e.tensor_tensor(sm[:C, 8:12], ps[:C, 8:12], sm[:C, 0:4], ALU.subtract), r=[ps, sm], w=[sm])
            act(sm[:C, 8:12], sm[:C, 8:12], AF.Exp, [sm], [sm])
            S.op("dve", lambda e: e.tensor_scalar(sm[:C, 12:16], betag[:C, j, 0:4], -1.0, None, ALU.mult), r=[betag], w=[sm])
            for h in range(4):
                S.op("pool", lambda e: e.tensor_scalar(trig[:C, h, :C], triT(C), betag[:C, j, 4 + h:5 + h], None, ALU.mult), r=[dnc, betag], w=[trig])
            ps = gp()
            for h in range(4):
                mm(ps[:C, h * C:(h + 1) * C], ustr(C), trig[:C, h, :C], [dnc, trig], [ps])
            act(decT[:C, :, :C], ps[:C, :4 * C].rearrange("p (h c) -> p h c", h=4), AF.Exp, [ps], [decT])
            S.op("pool", lambda e: e.tensor_tensor(decTs[:C, :, :C], decT[:C, :, :C], maskS(C), ALU.mult), r=[decT, dnc], w=[decTs])
            S.op("pool", lambda e: e.tensor_tensor(decT[:C, :, :C], decT[:C, :, :C], maskI(C), ALU.mult), r=[decT, dnc], w=[decT])
            psK = gp()
            for h in range(4):
                mm(psK[:C, h * C:(h + 1) * C], qkv[:, 4 + h, tc_], qkv[:, 4 + h, tc_], [qkv], [psK])
            psQ = gp()
            for h in range(4):
                mm(psQ[:C, h * C:(h + 1) * C], qkv[:, 4 + h, tc_], qkv[:, h, tc_], [qkv], [psQ])
            S.op("dve", lambda e: e.tensor_tensor(qkt[:C, :, :C], psQ[:C, :4 * C].rearrange("p (h c) -> p h c", h=4), decT[:C, :, :C], ALU.mult), r=[psQ, decT], w=[qkt])
            for h in range(4):
                S.op("dve", lambda e: e.scalar_tensor_tensor(QmT[0][:C, h, :C], psK[:C, h * C:(h + 1) * C], sm[:C, 12 + h:13 + h], decTs[:C, h, :C], ALU.mult, ALU.mult), r=[psK, sm, decTs], w=[QmT[0]])
            ps = gp()
            for h in range(4):
                tr(ps[:C, h * C:(h + 1) * C], QmT[0][:C, h, :C], ident[:C, :C], [QmT[0], ident], [ps])
            act(Qm[0][:C, :, :C], ps[:C, :4 * C].rearrange("p (h c) -> p h c", h=4), AF.Copy, [ps], [Qm[0]])
            S.op("pool", lambda e: e.tensor_tensor(FmT[0][:C, :, :C], QmT[0][:C, :, :C], identR(C), ALU.add), r=[QmT[0], dnc], w=[FmT[0]])
            for h in range(4):
                S.op("pool", lambda e: e.tensor_scalar(keg[:C, h, :], k_tok[:C, h, :], sm[:C, 4 + h:5 + h], None, ALU.mult), r=[k_tok, sm], w=[keg])
            yield
            for lv in range(nlev):
                if lv % 2 == 1:
                    yield
                a, b_ = lv % 2, (lv + 1) % 2
                lastlv = (lv == nlev - 1)
                if not lastlv:
                    for hh in range(2):
                        ps = gp()
                        for h2 in range(2):
                            h = hh * 2 + h2
                            if lv == 0:
                                mm(ps[:C, h2 * 256: h2 * 256 + 128], FmT[a][:C, h, :C], v_tok[:C, h, :], [FmT[a], v_tok], [ps])
                                mm(ps[:C, h2 * 256 + 128: h2 * 256 + 256], FmT[a][:C, h, :C], keg[:C, h, :], [FmT[a], keg], [ps])
                            else:
                                mm(ps[:C, h2 * 256:(h2 + 1) * 256], FmT[a][:C, h, :C], Xb[a][:C, h, :], [FmT[a], Xb[a]], [ps])
                        eng = "act" if hh == 0 else "dve"
                        if eng == "act":
                            act(Xb[b_][:C, hh * 2:hh * 2 + 2, :].rearrange("p h d -> p (h d)"), ps[:C, :], AF.Copy, [ps], [Xb[b_]])
                        else:
                            S.op("dve", lambda e: e.tensor_copy(Xb[b_][:C, hh * 2:hh * 2 + 2, :].rearrange("p h d -> p (h d)"), ps[:C, :]), r=[ps], w=[Xb[b_]])
                    ps1 = gp()
                    for h in range(4):
                        mm(ps1[:C, h * C:(h + 1) * C], QmT[a][:C, h, :C], Qm[a][:C, h, :C], [QmT[a], Qm[a]], [ps1])
                    ps2 = gp()
                    for h in range(4):
                        mm(ps2[:C, h * C:(h + 1) * C], Qm[a][:C, h, :C], QmT[a][:C, h, :C], [QmT[a], Qm[a]], [ps2])
                    act(Qm[b_][:C, :, :C], ps1[:C, :4 * C].rearrange("p (h c) -> p h c", h=4), AF.Copy, [ps1], [Qm[b_]])
                    S.op("dve", lambda e: e.tensor_copy(QmT[b_][:C, :, :C], ps2[:C, :4 * C].rearrange("p (h c) -> p h c", h=4)), r=[ps2], w=[QmT[b_]])
                    S.op("pool", lambda e: e.tensor_tensor(FmT[b_][:C, :, :C], QmT[b_][:C, :, :C], identR(C), ALU.add), r=[QmT[b_], dnc], w=[FmT[b_]])
                else:
                    psv = gp()
                    for h in range(4):
                        mm(psv[:C, h * 128:(h + 1) * 128], FmT[a][:C, h, :C], Xb[a][:C, h, 0:128], [FmT[a], Xb[a]], [psv])
                    psw = gp()
                    for h in range(4):
                        mm(psw[:, h * C:(h + 1) * C], Xb[a][:C, h, 128:256], FmT[a][:C, h, :C], [FmT[a], Xb[a]], [psw])
                    for h in range(4):
                        S.op("dve", lambda e: e.tensor_scalar(xvb[:C, h, :], psv[:C, h * 128:(h + 1) * 128], betag[:C, j, h:h + 1], None, ALU.mult), r=[psv, betag], w=[xvb])
                    act(xwT[:, :, :C], psw[:, :4 * C].rearrange("p (h c) -> p h c", h=4), AF.Copy, [psw], [xwT])
            yield
            psW = gp()
            for h in range(4):
                mm(psW[:C, h * 128:(h + 1) * 128], xwT[:, h, :C], Sst[:, h, :], [xwT, Sst], [psW])
            for h in range(4):
                S.op("dve", lambda e: e.scalar_tensor_tensor(v_new[:C, h, :], psW[:C, h * 128:(h + 1) * 128], sm[:C, 12 + h:13 + h], xvb[:C, h, :], ALU.mult, ALU.add), r=[psW, sm, xvb], w=[v_new])
            if owned:
                psO1 = gp()
                for h in range(4):
                    mm(psO1[:C, h * 128:(h + 1) * 128], qkv[:, h, tc_], Sst[:, h, :], [qkv, Sst], [psO1])
                for h in range(4):
                    act(o1s[:C, h, :], psO1[:C, h * 128:(h + 1) * 128], AF.Copy, [psO1, sm], [o1s], scale=sm[:C, 4 + h:5 + h])
                psO2 = gp()
                for h in range(4):
                    mm(psO2[:C, h * 128:(h + 1) * 128], qkt[:C, h, :C], v_new[:C, h, :], [qkt, v_new], [psO2])
                S.op("dve", lambda e: e.tensor_tensor(o_tok[:C].rearrange("p h d -> p (h d)"), o1s[:C].rearrange("p h d -> p (h d)"), psO2[:C, :], ALU.add), r=[o1s, psO2], w=[o_tok])
            for h in range(4):
                S.op("pool", lambda e: e.tensor_scalar(kdec[:C, h, :], k_tok[:C, h, :], sm[:C, 8 + h:9 + h], None, ALU.mult), r=[k_tok, sm], w=[kdec])
            for hh in range(2):
                psS = gp()
                for h2 in range(2):
                    h = hh * 2 + h2
                    mm(psS[:, h2 * 128:(h2 + 1) * 128], kdec[:C, h, :], v_new[:C, h, :], [kdec, v_new], [psS])
                for h2 in range(2):
                    h = hh * 2 + h2
                    S.op("dve", lambda e: e.scalar_tensor_tensor(Sst[:, h, :], Sst[:, h, :], sm[:, 16 + h:17 + h], psS[:, h2 * 128:(h2 + 1) * 128], ALU.mult, ALU.add), r=[Sst, sm, psS], w=[Sst])
            if sample:
                S.dma("sp", S_s[j], Sst[:], Sst, r=[Sst], w=[S_s])
            elif last and j == nch - 1:
                S.dma("sp", S_p[:], Sst[:], Sst, r=[Sst], w=[S_p])
            if owned:
                for h in range(4):
                    act(junk64[:C, :], o_tok[:C, h, :], AF.Square, [o_tok], [junk64, sm], accum_out=sm[:C, 28 + h:29 + h])
                act(sm[:C, 24:28], sm[:C, 28:32], AF.Sqrt, [sm], [sm], scale=1.0 / 128, bias=RMS_EPS)
                S.op("dve", lambda e: e.reciprocal(sm[:C, 24:28], sm[:C, 24:28]), r=[sm], w=[sm])
                for h in range(4):
                    S.op("pool", lambda e: e.tensor_scalar(o_tok[:C, h, :], o_tok[:C, h, :], sm[:C, 24 + h:25 + h], None, ALU.mult), r=[o_tok, sm], w=[o_tok])
                ps = gp()
                for h in range(4):
                    tr(ps[:, h * C:(h + 1) * C], o_tok[:C, h, :], ident[:C, :C], [o_tok, ident], [ps])
                S.op("dve", lambda e: e.scalar_tensor_tensor(o_dnT[:, :, tc_], ps[:, :4 * C].rearrange("p (h c) -> p h c", h=4), normw_sb[:, 0:1], zsT[:, :, tc_], ALU.mult, ALU.mult), r=[ps, normw_sb, zsT], w=[o_dnT])
        if not owned or stop == 'dn':
            return
        if debug and sample:
            S.op("pool", lambda e: e.tensor_copy(mrgA[:, 0:4, :128], o_dnT[:, :, :128]), r=[o_dnT], w=[mrgA])
            S.dma("sp", dbg["d_odn"][:], mrgA[:, 0:4, :128], mrgA, r=[mrgA], w=[dbg["d_odn"]])
        if stop == 'dbgodn':
            return
        scope('attn')
        if sample:
            for s_ in range(4):
                grp = [(h, h * 32, s_ * 32, 32) for h in range(8)]
                yield from attention_stream(grp, 256, s_, sample=True)
        else:
            for h in range(0, 8, 2):
                grp = [(h, 0, 0, W), (h + 1, W, 0, W)]
                yield from attention_stream(grp, 2 * W, ti_idx, sample=False)
        if debug and sample:
            S.op("pool", lambda e: e.tensor_copy(mrgA[:64, 0:8, :128], oT_sb[:, :, :128]), r=[oT_sb], w=[mrgA])
            S.dma("sp", dbg["d_osb"][:], mrgA[:64, 0:8, :128], mrgA, r=[mrgA], w=[dbg["d_osb"]])
        if stop in ('attn', 'attn1', 'attn2') or (stop or '').startswith('al'):
            return
        yield
        scope('merge')
        for half in range(2):
            for pc in range(2):
                if half == 0:
                    wu = wload(wsc_usb[:, :, pc * 512:(pc + 1) * 512], wsc_usb, npart=64, nk=8)
                else:
                    wu = wload(wsc_udn[:, :, pc * 512:(pc + 1) * 512], wsc_udn, npart=128, nk=4)
                wg = wload(wsc[:, :, OFF_G + half * 1024 + pc * 512: OFF_G + half * 1024 + (pc + 1) * 512], wsc)
                yield
                for c4 in range(4):
                    cc = pc * 4 + c4
                    psm_ = gp()
                    if half == 0:
                        for h in range(8):
                            mm(psm_[:, :W], wu[:64, h, c4 * 128:(c4 + 1) * 128], oT_sb[:, h, :W], [wu, oT_sb], [psm_], start=(h == 0), stop=(h == 7))
                    else:
                        for f in range(4):
                            mm(psm_[:, :W], wu[:, f, c4 * 128:(c4 + 1) * 128], o_dnT[:, f, :W], [wu, o_dnT], [psm_], start=(f == 0), stop=(f == 3))
                    psg = gp()
                    for kc in range(8):
                        mm(psg[:, :W], wg[:, kc, c4 * 128:(c4 + 1) * 128], xTb[:, kc, :W], [wg, xTb], [psg], start=(kc == 0), stop=(kc == 7))
                    g_ = gt[cc % 2]
                    act(g_[:, :W], psg[:, :W], AF.Sigmoid, [psg, bg_sb], [g_], bias=bg_sb[:, half * 8 + cc: half * 8 + cc + 1])
                    if half == 0:
                        S.op("dve", lambda e: e.tensor_tensor(mrgA[:, cc, :W], g_[:, :W], psm_[:, :W], ALU.mult), r=[g_, psm_], w=[mrgA])
                    else:
                        S.op("dve", lambda e: e.tensor_tensor(mtmp[:, :W], g_[:, :W], psm_[:, :W], ALU.mult), r=[g_, psm_], w=[mtmp])
                        S.op("pool", lambda e: e.tensor_tensor(mrgT[:, cc, :W], mtmp[:, :W], mrgA[:, cc, :W], ALU.add), r=[mtmp, mrgA], w=[mrgT])
        if debug and sample:
            S.dma("sp", dbg["d_mrg"][:], mrgA[:, 0:8, :128], mrgA, r=[mrgA], w=[dbg["d_mrg"]])
        if stop == 'merge':
            return
        yield
        scope('ln1')
        for pc in range(2):
            wo = wload(wsc_out[:, :, pc * 512:(pc + 1) * 512], wsc_out)
            for c4 in range(4):
                dmc = pc * 4 + c4
                ps = gp()
                for cc in range(8):
                    mm(ps[:, :W], wo[:, cc, c4 * 128:(c4 + 1) * 128], mrgT[:, cc, :W], [wo, mrgT], [ps], start=(cc == 0), stop=(cc == 7))
                S.op("dve", lambda e: e.scalar_tensor_tensor(pre[:, dmc, :W], xTf[:, dmc, :W], ALPHA, ps[:, :W], ALU.mult, ALU.add), r=[xTf, ps], w=[pre])
        pss = bank_acc
        for dmc in range(8):
            mm(pss[:, :W], ones_f[:, :], pre[:, dmc, :W], [ones_f, pre], [pss], start=(dmc == 0), stop=(dmc == 7))
        S.op("dve", lambda e: e.tensor_scalar(mean[:, :W], pss[:, :W], 1.0 / D, None, ALU.mult), r=[pss], w=[mean])
        for dmc in range(8):
            S.op("pool", lambda e: e.tensor_tensor(pre[:, dmc, :W], pre[:, dmc, :W], mean[:, :W], ALU.subtract), r=[pre, mean], w=[pre])
        psq = bank_acc2
        for dmc in range(8):
            S.op("pool", lambda e: e.tensor_tensor(sqb[:, :W], pre[:, dmc, :W], pre[:, dmc, :W], ALU.mult), r=[pre], w=[sqb])
            mm(psq[:, :W], ones_f[:, :], sqb[:, :W], [ones_f, sqb], [psq], start=(dmc == 0), stop=(dmc == 7))
        act(rstd[:, :W], psq[:, :W], AF.Sqrt, [psq], [rstd], scale=1.0 / D, bias=LN_EPS)
        S.op("dve", lambda e: e.reciprocal(rstd[:, :W], rstd[:, :W]), r=[rstd], w=[rstd])
        for dmc in range(8):
            S.op("dve", lambda e: e.tensor_tensor(pre[:, dmc, :W], pre[:, dmc, :W], rstd[:, :W], ALU.mult), r=[pre, rstd], w=[pre])
            S.op("dve", lambda e: e.tensor_scalar(hTf[:, dmc, :W], pre[:, dmc, :W], l1g[:, dmc:dmc + 1], l1b[:, dmc:dmc + 1], ALU.mult, ALU.add), r=[pre, l1g, l1b], w=[hTf])
        S.op("pool", lambda e: e.tensor_copy(hTb[:, :, :W], hTf[:, :, :W]), r=[hTf], w=[hTb])
        if debug and sample:
            S.dma("sp", dbg["d_h"][:], hTf[:, :, :128], hTf, r=[hTf], w=[dbg["d_h"]])
        if stop == 'ln1':
            return
        yield
        scope('peerq')
        for pc in range(4):
            yield
            wq_ = wload(wsc_pq[:, :, pc * 512:(pc + 1) * 512], wsc_pq)
            for c4 in range(4):
                cq = pc * 4 + c4
                ps = gp()
                for kc in range(8):
                    mm(ps[:, :W], wq_[:, kc, c4 * 128:(c4 + 1) * 128], hTb[:, kc, :W], [wq_, hTb], [ps], start=(kc == 0), stop=(kc == 7))
                act(qpT[:, cq, :W], ps[:, :W], AF.Copy, [ps], [qpT])
        for a in range(W // 128):
            ta = slice(a * 128, (a + 1) * 128)
            eidi, gate, h_tok = eidi_l[a], gate_l[a], h_tok_l[a]
            yield
            for g4 in range(4):
                ps = gp()
                for q4 in range(4):
                    hp = g4 * 4 + q4
                    mm(ps[:, q4 * 128:(q4 + 1) * 128], qpT[:, hp, ta], keys_b[:, hp, :], [qpT, keys_b], [ps])
                act(sc[:, :, :].rearrange("p a k -> p (a k)"), ps[:, :], AF.Copy, [ps], [sc])
                for q4 in range(4):
                    hp = g4 * 4 + q4
                    S.op("dve", lambda e: e.max(tv[:, hp, 0:8], sc[:, q4, :]), r=[sc], w=[tv])
                    S.op("dve", lambda e: e.max_index(ti[:, hp, 0:8], tv[:, hp, 0:8], sc[:, q4, :]), r=[sc, tv], w=[ti])
                    S.op("dve", lambda e: e.match_replace(scw[:, :], tv[:, hp, 0:8], sc[:, q4, :], -1e30), r=[sc, tv], w=[scw])
                    S.op("dve", lambda e: e.max(tv[:, hp, 8:16], scw[:, :]), r=[scw], w=[tv])
                    S.op("dve", lambda e: e.max_index(ti[:, hp, 8:16], tv[:, hp, 8:16], scw[:, :]), r=[scw, tv], w=[ti])
            S.op("dve", lambda e: e.tensor_copy(tif[:], ti[:]), r=[ti], w=[tif])
            for h in range(8):
                S.op("dve", lambda e: e.tensor_tensor(cand[:], tv[:, 2 * h, :].unsqueeze(2).broadcast_to([128, 16, 16]),
                                                      tv[:, 2 * h + 1, :].unsqueeze(1).broadcast_to([128, 16, 16]), ALU.add), r=[tv], w=[cand])
                S.op("dve", lambda e: e.scalar_tensor_tensor(cidx[:], tif[:, 2 * h, :].unsqueeze(2).broadcast_to([128, 16, 16]), 128.0,
                                                             tif[:, 2 * h + 1, :].unsqueeze(1).broadcast_to([128, 16, 16]), ALU.mult, ALU.add), r=[tif], w=[cidx])
                cf = cand[:].rearrange("p a b -> p (a b)")
                xf = cidx[:].rearrange("p a b -> p (a b)")
                S.op("dve", lambda e: e.max(tsv[:, h, 0:8], cf), r=[cand], w=[tsv])
                S.op("dve", lambda e: e.match_replace(candw[:, :], tsv[:, h, 0:8], cf, -1e30), r=[cand, tsv], w=[candw])
                S.op("dve", lambda e: e.max(tsv[:, h, 8:16], candw[:, :]), r=[candw], w=[tsv])
                for k in range(16):
                    S.op("dve", lambda e: e.scalar_tensor_tensor(junkp[:, :], cf, tsv[:, h, k:k + 1], xf, ALU.is_equal, ALU.mult,
                                                                 accum_out=eidf[:, h * 16 + k: h * 16 + k + 1]), r=[cand, cidx, tsv], w=[junkp, eidf])
                S.op("dve", lambda e: e.tensor_scalar(psm[:, h:h + 1], tsv[:, h, 0:1], -1.0, None, ALU.mult), r=[tsv], w=[psm])
                act(gate[:, h, :], tsv[:, h, :], AF.Exp, [tsv, psm], [gate, psm], bias=psm[:, h:h + 1], accum_out=psm[:, 8 + h:9 + h])
            S.op("dve", lambda e: e.reciprocal(psm[:, 16:24], psm[:, 8:16]), r=[psm], w=[psm])
            for h in range(8):
                S.op("dve", lambda e: e.tensor_scalar(gate[:, h, :], gate[:, h, :], psm[:, 16 + h:17 + h], None, ALU.mult), r=[gate, psm], w=[gate])
            S.op("dve", lambda e: e.tensor_scalar(eidf[:], eidf[:], 16383.0, None, ALU.min), r=[eidf], w=[eidf])
            S.op("dve", lambda e: e.tensor_copy(eidi[:], eidf[:]), r=[eidf], w=[eidi])
            scope('peer_htok')
            for hh in range(2):
                ps = gp()
                for k4 in range(4):
                    kc = hh * 4 + k4
                    tr(ps[:, k4 * 128:(k4 + 1) * 128], hTf[:, kc, ta], ident[:, :], [hTf, ident], [ps])
                act(h_tok[:, hh * 512:(hh + 1) * 512], ps[:, :], AF.Copy, [ps], [h_tok])
        pending.append(peer_gather(W, out_row0, y_out, sample))

    def peer_gather(W, out_row0, y_out, sample):
        y_buf = y_out
        for a in range(W // 128):
            eidi, gate, h_tok = eidi_l[a], gate_l[a], h_tok_l[a]
            psm = psm2
            scope('peer_u')
            for s_ in range(128):
                ub = ubuf[s_ % 3]
                S.dma("pool", ub[:], peer_u[:, :], ub, r=[peer_u, eidi], w=[ub],
                      indirect=dict(out_offset=None, in_offset=bass.IndirectOffsetOnAxis(ap=eidi[:, s_:s_ + 1], axis=0)))
                S.op("dve", lambda e: e.scalar_tensor_tensor(junku[:], ub[:], 1.0, h_tok[:], ALU.mult, ALU.mult, accum_out=actv[:, s_:s_ + 1]), r=[ub, h_tok], w=[junku, actv])
                if s_ % 4 == 3:
                    yield
            scope('peer_v')
            act(coef[:], actv[:], AF.Gelu, [actv], [coef])
            S.op("dve", lambda e: e.tensor_tensor(coef[:], coef[:], gate[:].rearrange("p h k -> p (h k)"), ALU.mult), r=[coef, gate], w=[coef])
            for s_ in range(128):
                ub = ubuf[s_ % 3]
                S.dma("pool", ub[:], peer_v[:, :], ub, r=[peer_v, eidi], w=[ub],
                      indirect=dict(out_offset=None, in_offset=bass.IndirectOffsetOnAxis(ap=eidi[:, s_:s_ + 1], axis=0)))
                if s_ == 0:
                    S.op("dve", lambda e: e.tensor_scalar(facc[:], ub[:], coef[:, 0:1], None, ALU.mult), r=[ub, coef], w=[facc])
                else:
                    S.op("dve", lambda e: e.scalar_tensor_tensor(facc[:], ub[:], coef[:, s_:s_ + 1], facc[:], ALU.mult, ALU.add), r=[ub, coef, facc], w=[facc])
                if s_ % 4 == 3:
                    yield
            if debug and sample:
                S.dma("sp", dbg["d_ffn"][:], facc[:], facc, r=[facc], w=[dbg["d_ffn"]])
            scope('ln2')
            S.op("dve", lambda e: e.scalar_tensor_tensor(facc[:], h_tok[:], ALPHA, facc[:], ALU.mult, ALU.add), r=[h_tok, facc], w=[facc])
            act(junku[:], facc[:], AF.Copy, [facc], [junku, psm], accum_out=psm[:, 0:1])
            S.op("dve", lambda e: e.tensor_scalar(psm[:, 0:1], psm[:, 0:1], -1.0 / D, None, ALU.mult), r=[psm], w=[psm])
            S.op("dve", lambda e: e.tensor_scalar(facc[:], facc[:], psm[:, 0:1], None, ALU.add), r=[facc, psm], w=[facc])
            act(junku[:], facc[:], AF.Square, [facc], [junku, psm], accum_out=psm[:, 1:2])
            act(psm[:, 2:3], psm[:, 1:2], AF.Sqrt, [psm], [psm], scale=1.0 / D, bias=LN_EPS)
            S.op("dve", lambda e: e.reciprocal(psm[:, 2:3], psm[:, 2:3]), r=[psm], w=[psm])
            S.op("dve", lambda e: e.scalar_tensor_tensor(ybuf[:], facc[:], psm[:, 2:3], l2g[:], ALU.mult, ALU.mult), r=[facc, psm, l2g], w=[ybuf])
            S.op("pool", lambda e: e.tensor_tensor(ybuf[:], ybuf[:], l2b[:], ALU.add), r=[ybuf, l2b], w=[ybuf])
            S.dma("sp", y_out[out_row0 + a * 128: out_row0 + (a + 1) * 128, :], ybuf[:], ybuf, r=[ybuf], w=[y_buf])

    def attention_stream(grp, ZW, idx, sample):
        blocks = []
        loaders = []
        if sample:
            s_ = idx
            blocks.append(dict(ktbuf=KTcur, kt=(lambda cc: KTcur[:, cc, s_ * 32:(s_ + 1) * 32]), vbuf=Vcur,
                               v=(lambda h: Vcur[:32, s_, h * 64:(h + 1) * 64]), nk=32, mask=(mask_s[:32, :256], mask_s), load=None))
            for n_, g8 in enumerate(range(7, -1, -1)):
                i2 = n_ % 2

                def load(i2=i2, g8=g8):
                    for hh in range(2):
                        S.dma("sp", kvstg[0][:64, :].rearrange("p (c k) -> p c k", c=4), kTc[s_, :, hh * 4:(hh + 1) * 4, g8 * 256:(g8 + 1) * 256], kvstg[0], r=[kTc], w=[kvstg[0]])
                        S.op("pool", lambda e: e.tensor_copy(KTblk[i2][:, hh * 4:(hh + 1) * 4, :].rearrange("p c k -> p (c k)"), kvstg[0][:64, :]), r=[kvstg[0]], w=[KTblk[i2]])
                    S.dma("act", kvstg[1][:, :].rearrange("p (b c) -> p b c", b=2), vc[s_, g8 * 256:(g8 + 1) * 256, :].rearrange("(b p) c -> p b c", p=128), kvstg[1], r=[vc], w=[kvstg[1]])
                    S.op("dve", lambda e: e.tensor_copy(Vblk[i2][:].rearrange("p b c -> p (b c)"), kvstg[1][:, :]), r=[kvstg[1]], w=[Vblk[i2]])
                for b4 in range(1, -1, -1):
                    blocks.append(dict(ktbuf=KTblk[i2], kt=(lambda cc, i2=i2, b4=b4: KTblk[i2][:, cc, b4 * 128:(b4 + 1) * 128]), vbuf=Vblk[i2],
                                       v=(lambda h, i2=i2, b4=b4: Vblk[i2][:, b4, h * 64:(h + 1) * 64]), nk=128, mask=None,
                                       load=(load if b4 == 1 else None)))
        else:
            i = idx
            nsub = WP // 128
            for o in range(nsub - 1, -1, -1):
                blocks.append(dict(ktbuf=KTcur, kt=(lambda cc, o=o: KTcur[:, cc, o * 128:(o + 1) * 128]), vbuf=Vcur,
                                   v=(lambda h, o=o: Vcur[:, o, h * 64:(h + 1) * 64]), nk=128, mask=(mask_p[:, o, :], mask_p), load=None))
            n_ = 0
            pt = i - 1
            while pt >= 0:
                i2 = n_ % 2
                n_ += 1
                tiles_ = [pt]

                def load(i2=i2, tiles_=tiles_):
                    for u_, t_ in enumerate(tiles_):
                        S.dma("sp", KTblk[i2][:, :, u_ * WP:(u_ + 1) * WP], KTs[t_][:, :, :], KTblk[i2], r=[KTs[t_]], w=[KTblk[i2]])
                        S.dma("act", Vblk[i2][:, u_ * nsub:(u_ + 1) * nsub, :], Vs[t_][:, :, :], Vblk[i2], r=[Vs[t_]], w=[Vblk[i2]])
                firstb = True
                for u_, t_ in enumerate(tiles_):
                    for o in range(nsub - 1, -1, -1):
                        blocks.append(dict(ktbuf=KTblk[i2], kt=(lambda cc, i2=i2, u_=u_, o=o: KTblk[i2][:, cc, u_ * WP + o * 128: u_ * WP + (o + 1) * 128]),
                                           vbuf=Vblk[i2], v=(lambda h, i2=i2, u_=u_, o=o: Vblk[i2][:, u_ * nsub + o, h * 64:(h + 1) * 64]),
                                           nk=128, mask=None, load=(load if firstb else None), kvalid=(t_ if t_ < 3 else None)))
                        firstb = False
                pt -= 1
        if stop == 'attn1' or (stop or '').startswith('al'):
            blocks = blocks[:1]
        if stop == 'attn2':
            blocks = blocks[:3]
        yield from attention_run(grp, ZW, blocks)

    alvl = int(stop[2:]) if (stop or '').startswith('al') else 99

    def attention_run(groups, ZW, blocks):
        po = bank_acc
        nb_ = len(blocks)
        pk = 0
        for bi, blk in enumerate(blocks):
            if bi % 2 == 0:
                yield
            if blk.get("load") is not None:
                blk["load"]()
            nk = blk["nk"]
            i2 = bi % 2
            zp = gp()
            for (h, zc, qc, Wg) in groups:
                mm(zp[:nk, zc:zc + Wg], blk["kt"](h), qTb[:, h, qc:qc + Wg], [blk["ktbuf"], qTb], [zp])
            if alvl <= 0:
                continue
            act(Eb[i2][:nk, :ZW], zp[:nk, :ZW], AF.Exp, [zp], [Eb[i2]], scale=0.125)
            if alvl <= 1:
                continue
            if blk["mask"] is not None:
                mk, mkb = blk["mask"]
                mw = mk.shape[-1]
                for c0 in range(0, ZW, mw):
                    S.op("pool", lambda e: e.tensor_tensor(Eb[i2][:nk, c0:c0 + mw], Eb[i2][:nk, c0:c0 + mw], mk, ALU.mult), r=[Eb[i2], mkb], w=[Eb[i2]])
            if blk.get("kvalid") is not None:
                kvc = blk["kvalid"]
                S.op("pool", lambda e: e.tensor_scalar(Eb[i2][:nk, :ZW], Eb[i2][:nk, :ZW], kval[:nk, kvc:kvc + 1], None, ALU.mult), r=[Eb[i2], kval], w=[Eb[i2]])
            if alvl <= 2:
                continue
            act(Pb[i2][:nk, :ZW], Eb[i2][:nk, :ZW], AF.Ln, [Eb[i2]], [Pb[i2]], bias=1.0)
            if alvl <= 3:
                continue
            cp = gp()
            mm(cp[:nk, :ZW], tri_b[:nk, :nk], Pb[i2][:nk, :ZW], [tri_b, Pb[i2]], [cp], start=True, stop=(bi == 0))
            if bi > 0:
                mm(cp[:nk, :ZW], ones_b[:pk, :nk], Pacc[:pk, :ZW], [ones_b, Pacc], [cp], start=False, stop=True)
            if alvl <= 4:
                continue
            act(Gb[i2][:nk, :ZW], cp[:nk, :ZW], AF.Exp, [cp], [Gb[i2]], scale=-1.0)
            if alvl <= 5:
                continue
            S.op("dve", lambda e: e.tensor_tensor(wTb[i2][:nk, :ZW], Eb[i2][:nk, :ZW], Gb[i2][:nk, :ZW], ALU.mult), r=[Eb[i2], Gb[i2]], w=[wTb[i2]])
            if alvl <= 6:
                continue
            if bi == 0:
                pk = 128
                if nk < 128:
                    S.op("pool", lambda e: e.memset(Pacc[:, :ZW], 0.0), r=[], w=[Pacc])
                S.op("pool", lambda e: e.tensor_copy(Pacc[:nk, :ZW], Pb[i2][:nk, :ZW]), r=[Pb[i2]], w=[Pacc])
            elif bi < nb_ - 1:
                S.op("pool", lambda e: e.tensor_tensor(Pacc[:nk, :ZW], Pacc[:nk, :ZW], Pb[i2][:nk, :ZW], ALU.add), r=[Pb[i2], Pacc], w=[Pacc])
            if alvl <= 7:
                continue
            for gi_, (h, zc, qc, Wg) in enumerate(groups):
                S.op("pe", lambda e: e.matmul(po[:64, zc:zc + Wg], blk["v"](h), wTb[i2][:nk, zc:zc + Wg], start=(bi == 0 and gi_ == 0), stop=(bi == nb_ - 1),
                                              skip_group_check=True), r=[blk["vbuf"], wTb[i2]], w=[po])
        if alvl <= 8:
            return
        for (h, zc, qc, Wg) in groups:
            S.op("dve", lambda e: e.tensor_copy(oT_sb[:, h, qc:qc + Wg], po[:64, zc:zc + Wg]), r=[po], w=[oT_sb])

    W_ = 128
    if do_sample:
        xsv = xT_s[:].rearrange("(k p) t -> p k t", p=128)
        tile(128, 4, 32, 32, (lambda hf: xsv[:, hf * 4:(hf + 1) * 4, :]), xT_s, True, True, True, None, None, 0, True, 0)
    if do_prompt:
        xpv = xT_p[:].rearrange("(k p) t -> p k t", p=128)
        for p in range(n_ptiles):
            tile(WP, 1, WP, 64, (lambda hf, p=p: xpv[:, hf * 4:(hf + 1) * 4, p * WP:(p + 1) * WP]), xT_p, (p % 4 == 3), (p == 0), (p == n_ptiles - 1),
                 None, None, (p // 4) * WP, False, p)
    while pending:
        for pg in list(pending):
            try:
                next(pg)
            except StopIteration:
                pending.remove(pg)
    S.finish(list(dout.values()))
    return nc, S


def _shared_inputs(w_in, b_gate, w_conv, a_log, dt_bias, dn_norm_w, w_up_sb, w_up_dn, w_out, ln1_g, ln1_b,
                   peer_wq, peer_keys, peer_u, peer_v, ln2_g, ln2_b):
    c = make_consts()
    f = np.ascontiguousarray
    d = dict(c)
    d["w_in"] = f(w_in[0])
    d["wconvT"] = f(w_conv[0].reshape(4, 12, 128).transpose(2, 1, 0))
    d["bgT"] = f(b_gate[0].reshape(16, 128).T)
    d["alog"] = f(np.broadcast_to(a_log[0][None, :], (128, 4)))
    d["dtb"] = f(np.broadcast_to(dt_bias[0][None, :], (128, 4)))
    d["normw"] = f(dn_norm_w[0].reshape(128, 1))
    d["w_up_sb"] = f(w_up_sb[0].reshape(8, 64, D).transpose(1, 0, 2))
    d["w_up_dn"] = f(w_up_dn[0].reshape(4, 128, D).transpose(1, 0, 2))
    d["w_out"] = f(w_out[0].reshape(8, 128, D).transpose(1, 0, 2))
    d["ln1g"] = f(ln1_g[0].reshape(8, 128).T)
    d["ln1b"] = f(ln1_b[0].reshape(8, 128).T)
    d["peer_wq"] = f(peer_wq[0].reshape(8, 128, 2048).transpose(1, 0, 2))
    d["keysT"] = f(peer_keys[0].reshape(16, 128, 128).transpose(2, 0, 1))
    d["peer_u"] = f(peer_u[0])
    d["peer_v"] = f(peer_v[0])
    d["ln2g"] = f(np.broadcast_to(ln2_g[0][None, :], (128, D)))
    d["ln2b"] = f(np.broadcast_to(ln2_b[0][None, :], (128, D)))
    return d


def _sample_inputs(c, x_sample, cache_sb_k, cache_sb_v, state_dn_ssm, state_dn_conv):
    f = np.ascontiguousarray
    sl = slice(4 * c, 4 * c + 4)
    d = {}
    d["xT_s"] = f(x_sample[sl].reshape(128, D).T)
    d["kTc"] = f(cache_sb_k[0, sl].transpose(0, 3, 2, 1))
    d["vc"] = f(cache_sb_v[0, sl].reshape(4, PAST, 512))
    d["S0"] = f(state_dn_ssm[0, sl].transpose(0, 2, 1, 3))
    d["convT"] = f(state_dn_conv[0, sl].reshape(4, 3, 12, 128).transpose(3, 2, 0, 1))
    return d


_PROG = {}
STOP = None


def kernel(x_prompt, x_sample, cache_sb_k, cache_sb_v, state_dn_ssm, state_dn_conv,
           w_in, b_gate, w_conv, a_log, dt_bias, dn_norm_w, w_up_sb, w_up_dn, w_out,
           ln1_g, ln1_b, peer_wq, peer_keys, peer_u, peer_v, ln2_g, ln2_b):
    args = [np.asarray(a, dtype=np.float32) for a in (x_prompt, x_sample, cache_sb_k, cache_sb_v, state_dn_ssm, state_dn_conv,
            w_in, b_gate, w_conv, a_log, dt_bias, dn_norm_w, w_up_sb, w_up_dn, w_out,
            ln1_g, ln1_b, peer_wq, peer_keys, peer_u, peer_v, ln2_g, ln2_b)]
    (x_prompt, x_sample, cache_sb_k, cache_sb_v, state_dn_ssm, state_dn_conv,
     w_in, b_gate, w_conv, a_log, dt_bias, dn_norm_w, w_up_sb, w_up_dn, w_out,
     ln1_g, ln1_b, peer_wq, peer_keys, peer_u, peer_v, ln2_g, ln2_b) = args
    if "nc" not in _PROG:
        _PROG["nc"] = build_program(stop=STOP)[0]
    nc = _PROG["nc"]
    shared = _shared_inputs(w_in, b_gate, w_conv, a_log, dt_bias, dn_norm_w, w_up_sb, w_up_dn, w_out, ln1_g, ln1_b,
                            peer_wq, peer_keys, peer_u, peer_v, ln2_g, ln2_b)
    in_maps = []
    for c in range(NCORE):
        b, r = c // 4, c % 4
        d = dict(shared)
        d.update(_sample_inputs(c, x_sample, cache_sb_k, cache_sb_v, state_dn_ssm, state_dn_conv))
        sh = (3 - r) * WP
        xt = np.zeros((D, SEQ), np.float32)
        xt[:, sh:] = x_prompt[b, :SEQ - sh].T
        d["xT_p"] = xt
        kv = np.ones((128, 4), np.float32)
        kv[:, :3 - r] = 0.0
        d["kval"] = kv
        if STOP == 'dn':
            d["peer_u"] = d["peer_u"][:8]
            d["peer_v"] = d["peer_v"][:8]
        in_maps.append(d)
    res = run_bass_kernel_spmd(nc, in_maps, core_ids=list(range(NCORE))).results
    B = 2
    y_p = np.zeros((B, SEQ, D), np.float32)
    kn_p = np.zeros((1, B, SEQ, 8, 64), np.float32)
    vn_p = np.zeros((1, B, SEQ, 8, 64), np.float32)
    y_s = np.zeros((32, 32, D), np.float32)
    kn_s = np.zeros((1, 32, 32, 8, 64), np.float32)
    vn_s = np.zeros((1, 32, 32, 8, 64), np.float32)
    S_p = np.zeros((1, B, 4, 128, 128), np.float32)
    S_s = np.zeros((1, 32, 4, 128, 128), np.float32)
    cv_p = np.zeros((1, B, 3, 1536), np.float32)
    cv_s = np.zeros((1, 32, 3, 1536), np.float32)
    for c in range(NCORE):
        b, r = c // 4, c % 4
        o = res[c]
        for m in range(NT_FULL // 4):
            t0 = (4 * m + r) * WP
            y_p[b, t0:t0 + WP] = o["y_p"][m * WP:(m + 1) * WP]
            kn_p[0, b, t0:t0 + WP] = o["kn_p"][m * WP:(m + 1) * WP].reshape(WP, 8, 64)
            vn_p[0, b, t0:t0 + WP] = o["vn_p"][m * WP:(m + 1) * WP].reshape(WP, 8, 64)
        if r == 3:
            S_p[0, b] = o["S_p"].transpose(1, 0, 2)
            cv_p[0, b] = o["cv_p"]
        y_s[4 * c:4 * c + 4] = o["y_s"].reshape(4, 32, D)
        kn_s[0, 4 * c:4 * c + 4] = o["kn_s"].reshape(4, 32, 8, 64)
        vn_s[0, 4 * c:4 * c + 4] = o["vn_s"].reshape(4, 32, 8, 64)
        S_s[0, 4 * c:4 * c + 4] = o["S_s"].transpose(0, 2, 1, 3)
        cv_s[0, 4 * c:4 * c + 4] = o["cv_s"]
    return (y_p, y_s, kn_p, vn_p, kn_s, vn_s, S_p, S_s, cv_p, cv_s)
```

```python
import numpy as np
import ml_dtypes
import concourse.bass as bass
import concourse.mybir as mybir
from concourse.bass_utils import run_bass_kernel_spmd

F32 = mybir.dt.float32
BF16 = mybir.dt.bfloat16
I32 = mybir.dt.int32
U32 = mybir.dt.uint32
AF = mybir.ActivationFunctionType
ALU = mybir.AluOpType

D = 1024
SEQ = 16384
NCORE = 8
WP = 256
NT_FULL = SEQ // WP
PAST = 2048
OFF_DN = 1536
OFF_Z = 3072
OFF_B = 3584
OFF_G = 3592
ALPHA = 2 ** 0.25
LN_EPS = 1e-5
RMS_EPS = 1e-6


class Buf:
    def __init__(self, S, t, name, dma=False):
        self.t = t
        self.name = name
        self.writers = {}
        self.readers = {}
        self.dsem = None
        self.dcnt = 0
        if dma:
            self.dsem = S.nc.alloc_semaphore("d_" + name)
            S.semh[("d", name)] = self.dsem
            S.dmabufs.append(self)

    def __getitem__(self, idx):
        return self.t[idx]


class Sched:
    def __init__(self, nc):
        self.nc = nc
        self.eng = {"pe": nc.tensor, "act": nc.scalar, "dve": nc.vector, "pool": nc.gpsimd, "sp": nc.sync}
        self.semh = {}
        self.cnt = {}
        self.seen = {}
        for k in self.eng:
            self.semh[k] = nc.alloc_semaphore("e_" + k)
            self.cnt[k] = 0
            self.seen[k] = {}
        self.nbuf = 0
        self.dmabufs = []
        self.gbufs = []
        self.n_ins = 0
        self.n_wait = 0

    def sb(self, shape, dtype, name=None, dma=False):
        self.nbuf += 1
        name = "s_" + (name or f"b{self.nbuf}")
        t = self.nc.alloc_sbuf_tensor(name, list(shape), dtype)
        return Buf(self, t, name, dma)

    def ps(self, shape, dtype=F32, name=None):
        self.nbuf += 1
        name = name or f"p{self.nbuf}"
        t = self.nc.alloc_psum_tensor(name, list(shape), dtype)
        return Buf(self, t, name, False)

    def dram(self, name, shape, dtype, kind="Internal"):
        t = self.nc.dram_tensor(name, list(shape), dtype, kind=kind)
        return Buf(self, t, name, False)

    def _wait(self, e, key, val):
        if self.seen[e].get(key, 0) >= val:
            return
        self.eng[e].wait_ge(self.semh[key], val)
        self.seen[e][key] = val
        self.n_wait += 1

    def _deps(self, e, r, w):
        for b in r:
            for (k, v) in b.writers.items():
                if not (k == "pe" and e == "pe"):
                    self._wait(e, k, v)
        for b in w:
            for (k, v) in b.writers.items():
                if not (k == "pe" and e == "pe"):
                    self._wait(e, k, v)
            for (k, v) in b.readers.items():
                if not (k == "pe" and e == "pe"):
                    self._wait(e, k, v)

    def _mark(self, key, val, r, w):
        for b in r:
            if b not in w:
                b.readers[key] = val
        for b in w:
            b.writers[key] = val

    def op(self, e, fn, r=(), w=()):
        r = list(r)
        w = list(w)
        self._deps(e, r, w)
        ins = fn(self.eng[e])
        self.cnt[e] += 1
        ins.then_inc(self.semh[e], 1)
        self._mark(e, self.cnt[e], r, w)
        self.n_ins += 1
        return ins

    def dma(self, e, out, in_, sbuf_buf, r=(), w=(), indirect=None, **kw):
        r = list(r)
        w = list(w)
        self._deps(e, r, w)
        b = sbuf_buf
        if e == "pool":
            if not hasattr(b, "gsem"):
                b.gsem = self.nc.alloc_semaphore("g_" + b.name)
                b.gcnt = 0
                self.semh[("g", b.name)] = b.gsem
                self.gbufs.append(b)
            key = ("g", b.name)
            if b.gcnt > 0:
                self._wait(e, key, 16 * b.gcnt)
            if indirect is None:
                ins = self.eng[e].dma_start(out=out, in_=in_, **kw)
            else:
                ins = self.eng[e].indirect_dma_start(out=out, in_=in_, **indirect)
            b.gcnt += 1
            ins.then_inc(b.gsem, 16)
            self._mark(key, 16 * b.gcnt, r, w)
            self.n_ins += 1
            return ins
        key = ("d", b.name)
        if b.dcnt > 0:
            self._wait(e, key, 16 * b.dcnt)
        ins = self.eng[e].dma_start(out=out, in_=in_, **kw)
        b.dcnt += 1
        ins.then_inc(b.dsem, 16)
        self._mark(key, 16 * b.dcnt, r, w)
        self.n_ins += 1
        return ins

    def finish(self, bufs):
        for b in bufs:
            for (k, v) in list(b.writers.items()) + list(b.readers.items()):
                self._wait("sp", k, v)
        for b in self.dmabufs:
            if b.dcnt > 0:
                self._wait("sp", ("d", b.name), 16 * b.dcnt)
        for b in self.gbufs:
            if b.gcnt > 0:
                self._wait("sp", ("g", b.name), 16 * b.gcnt)
        for k in ("pe", "act", "dve", "pool"):
            if self.cnt[k] > 0:
                self._wait("sp", k, self.cnt[k])


def make_consts():
    c = {}
    i = np.arange(128)
    c["ident"] = np.eye(128, dtype=np.float32)
    c["ones"] = np.ones((128, 128), np.float32)
    c["tri_b"] = (i[:, None] >= i[None, :]).astype(np.float32).astype(ml_dtypes.bfloat16)
    c["ones_b"] = np.ones((128, 128), np.float32).astype(ml_dtypes.bfloat16)
    t = np.arange(WP)
    mp = np.zeros((128, WP // 128, WP), np.float32)
    for o in range(WP // 128):
        mp[:, o, :] = ((128 * o + i[:, None]) < t[None, :])
    c["mask_p"] = mp
    j32 = np.arange(32)
    ms = np.zeros((128, 8, 32), np.float32)
    ms[:32] = (j32[:, None, None] < j32[None, None, :])
    c["mask_s"] = ms.reshape(128, 256)
    m = np.arange(64)
    dn = np.zeros((64, 6, 4, 64), np.float32)
    dn[:, 0] = (m[:, None] <= m[None, :])[:, None, :]
    dn[:, 1] = (m[:, None] > m[None, :])[:, None, :]
    dn[:, 2] = (m[None, :] > m[:, None])[:, None, :]
    dn[:, 3] = (m[None, :] >= m[:, None])[:, None, :]
    dn[:, 4] = np.eye(64)[:, None, :]
    dnp = np.zeros((128, 6 * 4 * 64), np.float32)
    dnp[:64] = dn.reshape(64, -1)
    c["dn"] = dnp
    return c


def build_program(n_ptiles=NT_FULL, do_sample=True, debug=False, stop=None):
    nc = bass.Bass("TRN2", target_bir_lowering=False)
    S = Sched(nc)
    EI = "ExternalInput"
    EO = "ExternalOutput"
    do_prompt = n_ptiles > 0
    n_own = (n_ptiles + 3) // 4 if do_prompt else 0

    din = {}

    def inp(name, shape, dt=F32):
        din[name] = S.dram(name, shape, dt, kind=EI)
        return din[name]

    dout = {}

    def outp(name, shape, dt=F32):
        dout[name] = S.dram(name, shape, dt, kind=EO)
        return dout[name]

    if do_prompt:
        xT_p = inp("xT_p", [D, n_ptiles * WP])
    if do_sample:
        xT_s = inp("xT_s", [D, 128])
        kTc = inp("kTc", [4, 64, 8, PAST])
        vc = inp("vc", [4, PAST, 512])
        S0in = inp("S0", [4, 128, 4, 128])
        convT = inp("convT", [128, 12, 4, 3])
    w_in = inp("w_in", [D, 5640])
    wconvT = inp("wconvT", [128, 12, 4])
    bgT = inp("bgT", [128, 16])
    alog = inp("alog", [128, 4])
    dtb = inp("dtb", [128, 4])
    normw = inp("normw", [128, 1])
    w_up_sb = inp("w_up_sb", [64, 8, D])
    w_up_dn = inp("w_up_dn", [128, 4, D])
    w_out = inp("w_out", [128, 8, D])
    ln1g = inp("ln1g", [128, 8])
    ln1b = inp("ln1b", [128, 8])
    peer_wq = inp("peer_wq", [128, 8, 2048])
    keysT = inp("keysT", [128, 16, 128])
    NEXP = 8 if stop == 'dn' else 16384
    peer_u = inp("peer_u", [NEXP, D])
    peer_v = inp("peer_v", [NEXP, D])
    ln2g = inp("ln2g", [128, D])
    ln2b = inp("ln2b", [128, D])
    c_ident = inp("ident", [128, 128])
    c_ones = inp("ones", [128, 128])
    c_tri_b = inp("tri_b", [128, 128], BF16)
    c_ones_b = inp("ones_b", [128, 128], BF16)
    c_mask_p = inp("mask_p", [128, WP // 128, WP])
    c_mask_s = inp("mask_s", [128, 256])
    c_dn = inp("dn", [128, 6 * 4 * 64])
    kval_in = inp("kval", [128, 4])

    if do_prompt:
        NOWN = n_own
        y_p = outp("y_p", [NOWN * WP, D])
        kn_p = outp("kn_p", [NOWN * WP, 512])
        vn_p = outp("vn_p", [NOWN * WP, 512])
        S_p = outp("S_p", [128, 4, 128])
        cv_p = outp("cv_p", [3, 1536])
    if do_sample:
        y_s = outp("y_s", [128, D])
        kn_s = outp("kn_s", [128, 512])
        vn_s = outp("vn_s", [128, 512])
        S_s = outp("S_s", [4, 128, 4, 128])
        cv_s = outp("cv_s", [4, 3, 1536])
    dbg = {}
    if debug:
        for nm, shp in [("d_osb", [64, 8, 128]), ("d_odn", [128, 4, 128]), ("d_h", [128, 8, 128]), ("d_ffn", [128, D]),
                        ("d_qkv", [128, 12, 128]), ("d_bg", [32, 4, 8]), ("d_mrg", [128, 8, 128])]:
            dbg[nm] = outp(nm, shp, F32)

    wsc = S.dram("wsc", [128, 8, 5640], BF16)
    wsc_pq = S.dram("wsc_pq", [128, 8, 2048], BF16)
    wsc_out = S.dram("wsc_out", [128, 8, D], BF16)
    wsc_udn = S.dram("wsc_udn", [128, 4, D], BF16)
    wsc_usb = S.dram("wsc_usb", [64, 8, D], BF16)
    if do_prompt:
        KTs = [S.dram(f"KTs{i}", [64, 8, WP], BF16) for i in range(n_ptiles)]
        Vs = [S.dram(f"Vs{i}", [128, WP // 128, 512], BF16) for i in range(n_ptiles)]

    ident = S.sb([128, 128], F32, "ident", dma=True)
    ones_f = S.sb([128, 128], F32, "ones_f", dma=True)
    tri_b = S.sb([128, 128], BF16, "tri_b", dma=True)
    ones_b = S.sb([128, 128], BF16, "ones_b", dma=True)
    mask_p = S.sb([128, WP // 128, WP], F32, "mask_p", dma=True)
    mask_s = S.sb([128, 256], F32, "mask_s", dma=True)
    dnc = S.sb([128, 6, 4, 64], F32, "dnc", dma=True)
    wcv = S.sb([128, 12, 4], F32, "wcv", dma=True)
    bg_sb = S.sb([128, 16], F32, "bg_sb", dma=True)
    nA = S.sb([128, 4], F32, "nA", dma=True)
    dtb_sb = S.sb([128, 4], F32, "dtb_sb", dma=True)
    normw_sb = S.sb([128, 1], F32, "normw_sb", dma=True)
    l1g = S.sb([128, 8], F32, "l1g", dma=True)
    l1b = S.sb([128, 8], F32, "l1b", dma=True)
    keys_b = S.sb([128, 16, 128], BF16, "keys_b")
    l2g = S.sb([128, D], F32, "l2g", dma=True)
    l2b = S.sb([128, D], F32, "l2b", dma=True)
    kval = S.sb([128, 4], F32, "kval", dma=True)
    S.dma("sp", kval[:], kval_in[:], kval, r=[kval_in], w=[kval])
    for (sbuf, src, q) in [(ident, c_ident, "sp"), (ones_f, c_ones, "act"), (tri_b, c_tri_b, "sp"), (ones_b, c_ones_b, "act"),
                           (mask_p, c_mask_p, "sp"), (mask_s, c_mask_s, "act"), (wcv, wconvT, "sp"), (bg_sb, bgT, "act"),
                           (nA, alog, "sp"), (dtb_sb, dtb, "act"), (normw_sb, normw, "sp"), (l1g, ln1g, "act"),
                           (l1b, ln1b, "sp"), (l2g, ln2g, "sp"), (l2b, ln2b, "act")]:
        S.dma(q, sbuf[:], src[:], sbuf, r=[src], w=[sbuf])
    S.dma("sp", dnc[:].rearrange("p a h c -> p (a h c)"), c_dn[:], dnc, r=[c_dn], w=[dnc])
    S.op("act", lambda e: e.activation(nA[:], nA[:], AF.Exp), r=[nA], w=[nA])
    S.op("dve", lambda e: e.tensor_scalar(nA[:], nA[:], -1.0, None, ALU.mult), r=[nA], w=[nA])

    banks = [S.ps([128, 512], F32, f"bank{i}") for i in range(8)]
    gp_state = [0]

    def gp():
        b = banks[gp_state[0] % 6]
        gp_state[0] += 1
        return b

    bank_acc = banks[6]
    bank_acc2 = banks[7]

    ubuf = [S.sb([128, D], F32, f"ubuf{i}", dma=True) for i in range(3)]
    stg = ubuf[0:2]
    stgb = [S.sb([128, 1024], BF16, f"stgb{i}", dma=True) for i in range(2)]
    cv_i = [0]

    def convert(src_ap, dst_ap, srcbuf, dstbuf, npart, n):
        i = cv_i[0] % 2
        cv_i[0] += 1
        q = "sp" if i == 0 else "act"
        S.dma(q, stg[i][:npart, :n], src_ap, stg[i], r=[srcbuf], w=[stg[i]])
        eng = "dve" if i == 0 else "pool"
        S.op(eng, lambda e: e.tensor_copy(stgb[i][:npart, :n], stg[i][:npart, :n]), r=[stg[i]], w=[stgb[i]])
        S.dma(q, dst_ap, stgb[i][:npart, :n], stgb[i], r=[stgb[i]], w=[dstbuf])

    w_in_v = w_in[:].rearrange("(k p) c -> p k c", p=128)
    for kc in range(8):
        for c0 in range(0, 5640, 1024):
            n = min(1024, 5640 - c0)
            convert(w_in_v[:, kc, c0:c0 + n], wsc[:, kc, c0:c0 + n], w_in, wsc, 128, n)
    for kc in range(8):
        for c0 in range(0, 2048, 1024):
            convert(peer_wq[:, kc, c0:c0 + 1024], wsc_pq[:, kc, c0:c0 + 1024], peer_wq, wsc_pq, 128, 1024)
    for kc in range(8):
        convert(w_out[:, kc, :], wsc_out[:, kc, :], w_out, wsc_out, 128, 1024)
        convert(w_up_sb[:, kc, :], wsc_usb[:, kc, :], w_up_sb, wsc_usb, 64, 1024)
    for kc in range(4):
        convert(w_up_dn[:, kc, :], wsc_udn[:, kc, :], w_up_dn, wsc_udn, 128, 1024)
    for hp0 in range(0, 16, 8):
        S.dma("sp", stg[0][:, :], keysT[:, hp0:hp0 + 8, :].rearrange("p a k -> p (a k)"), stg[0], r=[keysT], w=[stg[0]])
        S.op("dve", lambda e: e.tensor_copy(keys_b[:, hp0:hp0 + 8, :].rearrange("p a k -> p (a k)"), stg[0][:, :]), r=[stg[0]], w=[keys_b])

    Wba = S.sb([128, 8, 8], BF16, "Wba", dma=True)
    S.dma("sp", Wba[:], wsc[:, :, OFF_B:OFF_B + 8], Wba, r=[wsc], w=[Wba])
    wslots = [S.sb([128, 8, 512], BF16, f"wslot{i}", dma=True) for i in range(2)]
    ws_i = [0]

    def wload(src_ap, srcbuf, npart=128, nk=8):
        i = ws_i[0] % 2
        ws_i[0] += 1
        q = ["sp", "act"][i]
        S.dma(q, wslots[i][:npart, :nk, :], src_ap, wslots[i], r=[srcbuf], w=[wslots[i]])
        return wslots[i]

    WM = WP if do_prompt else 128
    xTf = S.sb([128, 8, WM], F32, "xTf", dma=True)
    xTb = S.sb([128, 8, WM], BF16, "xTb")
    KTcur = S.sb([64, 8, WM], BF16, "KTcur", dma=True)
    Vcur = S.sb([128, 4, 512], BF16, "Vcur", dma=True)
    kvf = [S.sb([128, 512], F32, f"kvf{i}", dma=True) for i in range(2)]
    xin_flat = [S.sb([128, WM + 12], F32, f"xin{i}", dma=True) for i in range(2)]
    hal = S.sb([128, 12, 3], F32, "hal", dma=True)
    qkv = S.sb([128, 12, WM], F32, "qkv", dma=True)
    cvo = S.sb([4, 512], F32, "cvo", dma=True)
    betag = S.sb([64, 4, 8], F32, "betag", dma=True)
    tmp48 = S.sb([64, 8], F32, "tmp48")
    Sst = S.sb([128, 4, 128], F32, "Sst", dma=True)
    qTb = S.sb([64, 8, WM], BF16, "qTb")
    zsT = S.sb([128, 4, WM], BF16, "zsT")
    o_dnT = S.sb([128, 4, WM], BF16, "o_dnT", dma=True)
    oT_sb = S.sb([64, 8, WM], BF16, "oT_sb", dma=True)
    k_tok = S.sb([64, 4, 128], F32, "k_tok")
    v_tok = S.sb([64, 4, 128], F32, "v_tok")
    keg = S.sb([64, 4, 128], F32, "keg")
    sm = S.sb([128, 32], F32, "sm")
    trig = S.sb([64, 4, 64], F32, "trig")
    decT = S.sb([64, 4, 64], F32, "decT")
    decTs = S.sb([64, 4, 64], F32, "decTs")
    qkt = S.sb([64, 4, 64], F32, "qkt")
    Qm = [S.sb([64, 4, 64], F32, f"Qm{i}") for i in range(2)]
    QmT = [S.sb([64, 4, 64], F32, f"QmT{i}") for i in range(2)]
    FmT = [S.sb([64, 4, 64], F32, f"FmT{i}") for i in range(2)]
    Xb = [S.sb([64, 4, 256], F32, f"Xb{i}") for i in range(2)]
    xvb = S.sb([64, 4, 128], F32, "xvb")
    kdec = xvb
    xwT = S.sb([128, 4, 64], F32, "xwT")
    v_new = S.sb([64, 4, 128], F32, "v_new")
    o1s = keg
    o_tok = S.sb([64, 4, 128], F32, "o_tok")
    junk64 = S.sb([64, 128], F32, "junk64")
    KTblk = [S.sb([64, 8, 256], BF16, f"KTblk{i}", dma=True) for i in range(2)]
    Vblk = [S.sb([128, 2, 512], BF16, f"Vblk{i}", dma=True) for i in range(2)]
    kvstg = [ubuf[0], ubuf[1]]
    ZM = 512 if do_prompt else 256
    Eb = [S.sb([128, ZM], F32, f"Eb{i}") for i in range(2)]
    Pb = [S.sb([128, ZM], BF16, f"Pb{i}") for i in range(2)]
    Gb = [S.sb([128, ZM], BF16, f"Gb{i}") for i in range(2)]
    wTb = [S.sb([128, ZM], BF16, f"wTb{i}") for i in range(2)]
    Pacc_l = [S.sb([128, ZM], BF16, f"Pacc{i}") for i in range(2)]
    WL = 8 if stop == 'dn' else WM
    gt = [S.sb([128, WM], F32, f"gt{i}") for i in range(2)]
    sqb, nrm = gt[0], gt[1]
    mtmp = S.sb([128, WL], F32, "mtmp")
    mrgA = qkv
    mrgT = S.sb([128, 8, WL], BF16, "mrgT")
    pre = mrgA
    hTf = xTf
    hTb = xTb
    mean = nrm
    rstd = S.sb([128, WL], F32, "rstd")
    qpT = S.sb([128, 16, WL], BF16, "qpT")
    sc = S.sb([128, 4, 128], F32, "sc")
    scw = S.sb([128, 128], F32, "scw")
    tv = S.sb([128, 16, 16], F32, "tv")
    ti = S.sb([128, 16, 16], U32, "ti")
    tif = S.sb([128, 16, 16], F32, "tif")
    cand = S.sb([128, 16, 16], F32, "cand")
    candw = S.sb([128, 256], F32, "candw")
    cidx = S.sb([128, 16, 16], F32, "cidx")
    tsv = S.sb([128, 8, 16], F32, "tsv")
    eidf = S.sb([128, 128], F32, "eidf")
    eidi_l = [S.sb([128, 128], I32, f"eidi{i}") for i in range(2)]
    gate_l = [S.sb([128, 8, 16], F32, f"gate{i}") for i in range(2)]
    psm = S.sb([128, 32], F32, "psm")
    junkp = S.sb([128, 256], F32, "junkp")
    h_tok_l = [S.sb([128, D], F32, f"h_tok{i}") for i in range(2)]
    psm2 = S.sb([128, 8], F32, "psm2")
    junku = stgb[0]
    actv = S.sb([128, 128], F32, "actv")
    coef = S.sb([128, 128], F32, "coef")
    facc = S.sb([128, D], F32, "facc", dma=True)
    ybuf = facc

    triT = lambda C: dnc[:C, 0, 0, :C]
    ustr = lambda C: dnc[:C, 1, 0, :C]
    maskS = lambda C: dnc[:C, 2, :, :C]
    maskI = lambda C: dnc[:C, 3, :, :C]
    identR = lambda C: dnc[:C, 4, :, :C]

    def mm(out, lhsT, rhs, r, w, start=True, stop=True):
        S.op("pe", lambda e: e.matmul(out, lhsT, rhs, start=start, stop=stop), r=r, w=w)

    def tr(out, in_, idn, r, w):
        S.op("pe", lambda e: e.transpose(out, in_, idn), r=r, w=w)

    def act(out, in_, func, r, w, **kw):
        S.op("act", lambda e: e.activation(out, in_, func, **kw), r=r, w=w)

    def proj_fm(wbuf, wcol0, ncc, evac):
        for cc in range(ncc):
            ps = gp()
            for kc in range(8):
                mm(ps[:, :W_], wbuf[:, kc, wcol0 + cc * 128: wcol0 + (cc + 1) * 128], xTb[:, kc, :W_], [wbuf, xTb], [ps], start=(kc == 0), stop=(kc == 7))
            evac(cc, ps)


    cur_scope = [None]

    def scope(name):
        return

    pending = []

    def tile(*a, **k):
        for _ in tile_(*a, **k):
            for pg in list(pending):
                try:
                    next(pg)
                except StopIteration:
                    pending.remove(pg)

    def tile_(W, nseq, Wseq, C, xsrc_ap, xsrc_buf, owned, first, last, halo_src, kv_past_blocks, out_row0, sample, ti_idx):
        nonlocal W_
        W_ = W
        y_out, kn_out, vn_out = (y_s, kn_s, vn_s) if sample else (y_p, kn_p, vn_p)
        y_buf, kn_buf, vn_buf = y_out, kn_out, vn_out
        nch = W // C
        if stop == 'pro':
            return
        scope('kvproj')
        S.dma("sp", xTf[:, 0:4, :W], xsrc_ap(0), xTf, r=[xsrc_buf], w=[xTf])
        S.dma("act", xTf[:, 4:8, :W], xsrc_ap(1), xTf, r=[xsrc_buf], w=[xTf])
        S.op("pool", lambda e: e.tensor_copy(xTb[:, :, :W], xTf[:, :, :W]), r=[xTf], w=[xTb])
        if stop == 'x':
            return
        def proj_heads(wbuf, dst):
            for h in range(8):
                ps = gp()
                for kc in range(8):
                    mm(ps[:64, :W], wbuf[:, kc, h * 64:(h + 1) * 64], xTb[:, kc, :W], [wbuf, xTb], [ps], start=(kc == 0), stop=(kc == 7))
                act(dst[:, h, :W], ps[:64, :W], AF.Copy, [ps], [dst])
        Wk = wload(wsc[:, :, 512:1024], wsc)
        proj_heads(Wk, KTcur)
        if not sample:
            S.dma("sp", KTs[ti_idx][:, :, :], KTcur[:, :, :W], KTcur, r=[KTcur], w=[KTs[ti_idx]])
        if stop == 'kt':
            return

        def tokmajor_out(wt, ob_, kb_, q_):
            for g in range(W // 128):
                ps = gp()
                for kc in range(8):
                    mm(ps[:, :], xTb[:, kc, g * 128:(g + 1) * 128], wt[:, kc, :], [xTb, wt], [ps], start=(kc == 0), stop=(kc == 7))
                act(kb_[:, :], ps[:, :], AF.Copy, [ps], [kb_])
                S.dma(q_, ob_[out_row0 + g * 128: out_row0 + (g + 1) * 128, :], kb_[:, :], kb_, r=[kb_], w=[ob_])
        if owned:
            tokmajor_out(Wk, kn_out, kvf[1], "act")
        Wv = wload(wsc[:, :, 1024:1536], wsc)
        gs = 32 if sample else 128
        ng = W // gs
        for g in range(ng):
            ps = gp()
            for kc in range(8):
                mm(ps[:gs, :], xTb[:, kc, g * gs:(g + 1) * gs], Wv[:, kc, :], [xTb, Wv], [ps], start=(kc == 0), stop=(kc == 7))
            S.op("dve", lambda e: e.tensor_copy(Vcur[:gs, g, :], ps[:gs, :]), r=[ps], w=[Vcur])
        if owned:
            tokmajor_out(Wv, vn_out, kvf[0], "sp")
        if not sample:
            S.dma("act", Vs[ti_idx][:, :, :], Vcur[:, :ng, :], Vcur, r=[Vcur], w=[Vs[ti_idx]])
        if stop in ('kv', 'kv1', 'kv2'):
            return
        yield
        scope('dnproj')
        if first and not sample:
            S.op("pool", lambda e: e.memset(hal[:], 0.0), r=[], w=[hal])
        for piece in range(3):
            wb = wload(wsc[:, :, OFF_DN + piece * 512: OFF_DN + (piece + 1) * 512], wsc)
            for c4 in range(4):
                cc = piece * 4 + c4
                xbuf_ = xin_flat[cc % 2]
                xb_ = xbuf_[:, :nseq * (Wseq + 3)].rearrange("p (s w) -> p s w", s=nseq)
                ps = gp()
                for kc in range(8):
                    mm(ps[:, :W], wb[:, kc, c4 * 128:(c4 + 1) * 128], xTb[:, kc, :W], [wb, xTb], [ps], start=(kc == 0), stop=(kc == 7))
                act(xb_[:, :nseq, 3:3 + Wseq], ps[:, :W].rearrange("p (s w) -> p s w", s=nseq), AF.Copy, [ps], [xbuf_])
                if sample:
                    S.dma("sp", xb_[:, :nseq, 0:3], convT[:, cc, :, :], xbuf_, r=[convT], w=[xbuf_])
                else:
                    S.op("pool", lambda e: e.tensor_copy(xb_[:, 0, 0:3], hal[:, cc, :]), r=[hal], w=[xbuf_])
                    S.op("pool", lambda e: e.tensor_copy(hal[:, cc, :], xb_[:, 0, Wseq:Wseq + 3]), r=[xbuf_], w=[hal])
                qv = qkv[:, cc, :W].rearrange("p (s w) -> p s w", s=nseq)
                S.op("dve", lambda e: e.tensor_scalar(qv, xb_[:, :nseq, 0:Wseq], wcv[:, cc, 0:1], None, ALU.mult), r=[xbuf_, wcv], w=[qkv])
                for i in range(1, 4):
                    S.op("dve", lambda e: e.scalar_tensor_tensor(qv, xb_[:, :nseq, i:i + Wseq], wcv[:, cc, i:i + 1], qv, ALU.mult, ALU.add), r=[xbuf_, wcv, qkv], w=[qkv])
                act(qkv[:, cc, :W], qkv[:, cc, :W], AF.Silu, [qkv], [qkv])
            yield
            if sample or last:
                for s_ in range(nseq):
                    ps = gp()
                    t1 = (s_ + 1) * Wseq
                    for kc in range(8):
                        mm(ps[:3, :], xTb[:, kc, t1 - 3:t1], wb[:, kc, :], [xTb, wb], [ps], start=(kc == 0), stop=(kc == 7))
                    S.op("dve", lambda e: e.tensor_copy(cvo[:3, :], ps[:3, :]), r=[ps], w=[cvo])
                    if sample:
                        S.dma("sp", cv_s[s_, :, piece * 512:(piece + 1) * 512], cvo[:3, :], cvo, r=[cvo], w=[cv_s])
                    else:
                        S.dma("sp", cv_p[:, piece * 512:(piece + 1) * 512], cvo[:3, :], cvo, r=[cvo], w=[cv_p])
        if stop == 'conv':
            return
        yield
        for cc in range(8):
            S.op("pool", lambda e: e.tensor_tensor(sqb[:, :W], qkv[:, cc, :W], qkv[:, cc, :W], ALU.mult), r=[qkv], w=[sqb])
            ps = gp()
            mm(ps[:, :W], ones_f[:, :], sqb[:, :W], [ones_f, sqb], [ps])
            act(nrm[:, :W], ps[:, :W], AF.Sqrt, [ps], [nrm], bias=RMS_EPS)
            S.op("dve", lambda e: e.reciprocal(nrm[:, :W], nrm[:, :W]), r=[nrm], w=[nrm])
            sc_ = (128 ** -0.5) if cc < 4 else 1.0
            S.op("dve", lambda e: e.scalar_tensor_tensor(qkv[:, cc, :W], qkv[:, cc, :W], sc_, nrm[:, :W], ALU.mult, ALU.mult), r=[qkv, nrm], w=[qkv])
        if debug and sample:
            S.dma("sp", dbg["d_qkv"][:], qkv[:, :, :128], qkv, r=[qkv], w=[dbg["d_qkv"]])
        for j in range(nch):
            ps = gp()
            for kc in range(8):
                mm(ps[:C, 0:8], xTb[:, kc, j * C:(j + 1) * C], Wba[:, kc, :], [xTb, Wba], [ps], start=(kc == 0), stop=(kc == 7))
            act(betag[:C, j, 0:4], ps[:C, 0:4], AF.Sigmoid, [ps], [betag])
            S.op("dve", lambda e: e.tensor_tensor(tmp48[:C, 0:4], ps[:C, 4:8], dtb_sb[:C, :], ALU.add), r=[ps, dtb_sb], w=[tmp48])
            act(tmp48[:C, 0:4], tmp48[:C, 0:4], AF.Exp, [tmp48], [tmp48])
            act(tmp48[:C, 0:4], tmp48[:C, 0:4], AF.Ln, [tmp48], [tmp48], bias=1.0)
            S.op("dve", lambda e: e.tensor_tensor(betag[:C, j, 4:8], tmp48[:C, 0:4], nA[:C, :], ALU.mult), r=[tmp48, nA], w=[betag])
        if debug and sample:
            S.dma("sp", dbg["d_bg"][:], betag[:32, :, :], betag, r=[betag], w=[dbg["d_bg"]])
        if stop == 'bg':
            return
        yield
        scope('qz')
        if owned:
            wb = wload(wsc[:, :, 0:512], wsc)
            proj_heads(wb, qTb)
            wb = wload(wsc[:, :, OFF_Z:OFF_Z + 512], wsc)
            def ev_z(cc, ps):
                act(zsT[:, cc, :W], ps[:, :W], AF.Silu, [ps], [zsT])
            proj_fm(wb, 0, 4, ev_z)
        scope('dnchunks')
        nlev = 6 if C == 64 else 5
        for j in range(nch):
            tc_ = slice(j * C, (j + 1) * C)
            if sample:
                S.dma("sp", Sst[:], S0in[j], Sst, r=[S0in], w=[Sst])
            elif first and j == 0:
                S.op("pool", lambda e: e.memset(Sst[:], 0.0), r=[], w=[Sst])
            ps = gp()
            for h in range(4):
                tr(ps[:C, h * 128:(h + 1) * 128], qkv[:, 4 + h, tc_], ident[:, :], [qkv, ident], [ps])
            act(k_tok[:C].rearrange("p h d -> p (h d)"), ps[:C, :], AF.Copy, [ps], [k_tok])
            ps = gp()
            for h in range(4):
                tr(ps[:C, h * 128:(h + 1) * 128], qkv[:, 8 + h, tc_], ident[:, :], [qkv, ident], [ps])
            S.op("dve", lambda e: e.tensor_copy(v_tok[:C].rearrange("p h d -> p (h d)"), ps[:C, :]), r=[ps], w=[v_tok])
            bgj = betag[:C, j, :]
            ps = gp()
            mm(ps[:C, 0:4], triT(C), betag[:C, j, 4:8], [dnc, betag], [ps])
            mm(ps[:, 8:12], ones_f[:C, :], betag[:C, j, 4:8], [ones_f, betag], [ps])
            S.op("dve", lambda e: e.tensor_copy(sm[:C, 0:4], ps[:C, 0:4]), r=[ps], w=[sm])
            act(sm[:C, 4:8], ps[:C, 0:4], AF.Exp, [ps], [sm])
            act(sm[:, 16:20], ps[:, 8:12], AF.Exp, [ps], [sm])
            S.op("dve", lambda e: e.tensor_tensor(sm[:C, 8:12], ps[:C, 8:12], sm[:C, 0:4], ALU.subtract), r=[ps, sm], w=[sm])
            act(sm[:C, 8:12], sm[:C, 8:12], AF.Exp, [sm], [sm])
            S.op("dve", lambda e: e.tensor_scalar(sm[:C, 12:16], betag[:C, j, 0:4], -1.0, None, ALU.mult), r=[betag], w=[sm])
            for h in range(4):
                S.op("pool", lambda e: e.tensor_scalar(trig[:C, h, :C], triT(C), betag[:C, j, 4 + h:5 + h], None, ALU.mult), r=[dnc, betag], w=[trig])
            ps = gp()
            for h in range(4):
                mm(ps[:C, h * C:(h + 1) * C], ustr(C), trig[:C, h, :C], [dnc, trig], [ps])
            act(decT[:C, :, :C], ps[:C, :4 * C].rearrange("p (h c) -> p h c", h=4), AF.Exp, [ps], [decT])
            S.op("pool", lambda e: e.tensor_tensor(decTs[:C, :, :C], decT[:C, :, :C], maskS(C), ALU.mult), r=[decT, dnc], w=[decTs])
            S.op("pool", lambda e: e.tensor_tensor(decT[:C, :, :C], decT[:C, :, :C], maskI(C), ALU.mult), r=[decT, dnc], w=[decT])
            psK = gp()
            for h in range(4):
                mm(psK[:C, h * C:(h + 1) * C], qkv[:, 4 + h, tc_], qkv[:, 4 + h, tc_], [qkv], [psK])
            psQ = gp()
            for h in range(4):
                mm(psQ[:C, h * C:(h + 1) * C], qkv[:, 4 + h, tc_], qkv[:, h, tc_], [qkv], [psQ])
            S.op("dve", lambda e: e.tensor_tensor(qkt[:C, :, :C], psQ[:C, :4 * C].rearrange("p (h c) -> p h c", h=4), decT[:C, :, :C], ALU.mult), r=[psQ, decT], w=[qkt])
            for h in range(4):
                S.op("dve", lambda e: e.scalar_tensor_tensor(QmT[0][:C, h, :C], psK[:C, h * C:(h + 1) * C], sm[:C, 12 + h:13 + h], decTs[:C, h, :C], ALU.mult, ALU.mult), r=[psK, sm, decTs], w=[QmT[0]])
            ps = gp()
            for h in range(4):
                tr(ps[:C, h * C:(h + 1) * C], QmT[0][:C, h, :C], ident[:C, :C], [QmT[0], ident], [ps])
            act(Qm[0][:C, :, :C], ps[:C, :4 * C].rearrange("p (h c) -> p h c", h=4), AF.Copy, [ps], [Qm[0]])
            S.op("pool", lambda e: e.tensor_tensor(FmT[0][:C, :, :C], QmT[0][:C, :, :C], identR(C), ALU.add), r=[QmT[0], dnc], w=[FmT[0]])
            for h in range(4):
                S.op("pool", lambda e: e.tensor_scalar(keg[:C, h, :], k_tok[:C, h, :], sm[:C, 4 + h:5 + h], None, ALU.mult), r=[k_tok, sm], w=[keg])
            yield
            for lv in range(nlev):
                if lv % 2 == 1:
                    yield
                a, b_ = lv % 2, (lv + 1) % 2
                lastlv = (lv == nlev - 1)
                if not lastlv:
                    for hh in range(2):
                        ps = gp()
                        for h2 in range(2):
                            h = hh * 2 + h2
                            if lv == 0:
                                mm(ps[:C, h2 * 256: h2 * 256 + 128], FmT[a][:C, h, :C], v_tok[:C, h, :], [FmT[a], v_tok], [ps])
                                mm(ps[:C, h2 * 256 + 128: h2 * 256 + 256], FmT[a][:C, h, :C], keg[:C, h, :], [FmT[a], keg], [ps])
                            else:
                                mm(ps[:C, h2 * 256:(h2 + 1) * 256], FmT[a][:C, h, :C], Xb[a][:C, h, :], [FmT[a], Xb[a]], [ps])
                        eng = "act" if hh == 0 else "dve"
                        if eng == "act":
                            act(Xb[b_][:C, hh * 2:hh * 2 + 2, :].rearrange("p h d -> p (h d)"), ps[:C, :], AF.Copy, [ps], [Xb[b_]])
                        else:
                            S.op("dve", lambda e: e.tensor_copy(Xb[b_][:C, hh * 2:hh * 2 + 2, :].rearrange("p h d -> p (h d)"), ps[:C, :]), r=[ps], w=[Xb[b_]])
                    ps1 = gp()
                    for h in range(4):
                        mm(ps1[:C, h * C:(h + 1) * C], QmT[a][:C, h, :C], Qm[a][:C, h, :C], [QmT[a], Qm[a]], [ps1])
                    ps2 = gp()
                    for h in range(4):
                        mm(ps2[:C, h * C:(h + 1) * C], Qm[a][:C, h, :C], QmT[a][:C, h, :C], [QmT[a], Qm[a]], [ps2])
                    act(Qm[b_][:C, :, :C], ps1[:C, :4 * C].rearrange("p (h c) -> p h c", h=4), AF.Copy, [ps1], [Qm[b_]])
                    S.op("dve", lambda e: e.tensor_copy(QmT[b_][:C, :, :C], ps2[:C, :4 * C].rearrange("p (h c) -> p h c", h=4)), r=[ps2], w=[QmT[b_]])
                    S.op("pool", lambda e: e.tensor_tensor(FmT[b_][:C, :, :C], QmT[b_][:C, :, :C], identR(C), ALU.add), r=[QmT[b_], dnc], w=[FmT[b_]])
                else:
                    psv = gp()
                    for h in range(4):
                        mm(psv[:C, h * 128:(h + 1) * 128], FmT[a][:C, h, :C], Xb[a][:C, h, 0:128], [FmT[a], Xb[a]], [psv])
                    psw = gp()
                    for h in range(4):
                        mm(psw[:, h * C:(h + 1) * C], Xb[a][:C, h, 128:256], FmT[a][:C, h, :C], [FmT[a], Xb[a]], [psw])
                    for h in range(4):
                        S.op("dve", lambda e: e.tensor_scalar(xvb[:C, h, :], psv[:C, h * 128:(h + 1) * 128], betag[:C, j, h:h + 1], None, ALU.mult), r=[psv, betag], w=[xvb])
                    act(xwT[:, :, :C], psw[:, :4 * C].rearrange("p (h c) -> p h c", h=4), AF.Copy, [psw], [xwT])
            yield
            psW = gp()
            for h in range(4):
                mm(psW[:C, h * 128:(h + 1) * 128], xwT[:, h, :C], Sst[:, h, :], [xwT, Sst], [psW])
            for h in range(4):
                S.op("dve", lambda e: e.scalar_tensor_tensor(v_new[:C, h, :], psW[:C, h * 128:(h + 1) * 128], sm[:C, 12 + h:13 + h], xvb[:C, h, :], ALU.mult, ALU.add), r=[psW, sm, xvb], w=[v_new])
            if owned:
                psO1 = gp()
                for h in range(4):
                    mm(psO1[:C, h * 128:(h + 1) * 128], qkv[:, h, tc_], Sst[:, h, :], [qkv, Sst], [psO1])
                for h in range(4):
                    act(o1s[:C, h, :], psO1[:C, h * 128:(h + 1) * 128], AF.Copy, [psO1, sm], [o1s], scale=sm[:C, 4 + h:5 + h])
                psO2 = gp()
                for h in range(4):
                    mm(psO2[:C, h * 128:(h + 1) * 128], qkt[:C, h, :C], v_new[:C, h, :], [qkt, v_new], [psO2])
                S.op("dve", lambda e: e.tensor_tensor(o_tok[:C].rearrange("p h d -> p (h d)"), o1s[:C].rearrange("p h d -> p (h d)"), psO2[:C, :], ALU.add), r=[o1s, psO2], w=[o_tok])
            for h in range(4):
                S.op("pool", lambda e: e.tensor_scalar(kdec[:C, h, :], k_tok[:C, h, :], sm[:C, 8 + h:9 + h], None, ALU.mult), r=[k_tok, sm], w=[kdec])
            for hh in range(2):
                psS = gp()
                for h2 in range(2):
                    h = hh * 2 + h2
                    mm(psS[:, h2 * 128:(h2 + 1) * 128], kdec[:C, h, :], v_new[:C, h, :], [kdec, v_new], [psS])
                for h2 in range(2):
                    h = hh * 2 + h2
                    S.op("dve", lambda e: e.scalar_tensor_tensor(Sst[:, h, :], Sst[:, h, :], sm[:, 16 + h:17 + h], psS[:, h2 * 128:(h2 + 1) * 128], ALU.mult, ALU.add), r=[Sst, sm, psS], w=[Sst])
            if sample:
                S.dma("sp", S_s[j], Sst[:], Sst, r=[Sst], w=[S_s])
            elif last and j == nch - 1:
                S.dma("sp", S_p[:], Sst[:], Sst, r=[Sst], w=[S_p])
            if owned:
                for h in range(4):
                    act(junk64[:C, :], o_tok[:C, h, :], AF.Square, [o_tok], [junk64, sm], accum_out=sm[:C, 28 + h:29 + h])
                act(sm[:C, 24:28], sm[:C, 28:32], AF.Sqrt, [sm], [sm], scale=1.0 / 128, bias=RMS_EPS)
                S.op("dve", lambda e: e.reciprocal(sm[:C, 24:28], sm[:C, 24:28]), r=[sm], w=[sm])
                for h in range(4):
                    S.op("pool", lambda e: e.tensor_scalar(o_tok[:C, h, :], o_tok[:C, h, :], sm[:C, 24 + h:25 + h], None, ALU.mult), r=[o_tok, sm], w=[o_tok])
                ps = gp()
                for h in range(4):
                    tr(ps[:, h * C:(h + 1) * C], o_tok[:C, h, :], ident[:C, :C], [o_tok, ident], [ps])
                S.op("dve", lambda e: e.scalar_tensor_tensor(o_dnT[:, :, tc_], ps[:, :4 * C].rearrange("p (h c) -> p h c", h=4), normw_sb[:, 0:1], zsT[:, :, tc_], ALU.mult, ALU.mult), r=[ps, normw_sb, zsT], w=[o_dnT])
        if not owned or stop == 'dn':
            return
        if debug and sample:
            S.op("pool", lambda e: e.tensor_copy(mrgA[:, 0:4, :128], o_dnT[:, :, :128]), r=[o_dnT], w=[mrgA])
            S.dma("sp", dbg["d_odn"][:], mrgA[:, 0:4, :128], mrgA, r=[mrgA], w=[dbg["d_odn"]])
        if stop == 'dbgodn':
            return
        scope('attn')
        if sample:
            for s_ in range(4):
                grp = [(h, h * 32, s_ * 32, 32) for h in range(8)]
                yield from attention_stream([(grp, bank_acc)], 256, s_, sample=True)
        else:
            for h in range(0, 8, 4):
                st2 = [([(h, 0, 0, W), (h + 1, W, 0, W)], bank_acc), ([(h + 2, 0, 0, W), (h + 3, W, 0, W)], bank_acc2)]
                yield from attention_stream(st2, 2 * W, ti_idx, sample=False)
        if debug and sample:
            S.op("pool", lambda e: e.tensor_copy(mrgA[:64, 0:8, :128], oT_sb[:, :, :128]), r=[oT_sb], w=[mrgA])
            S.dma("sp", dbg["d_osb"][:], mrgA[:64, 0:8, :128], mrgA, r=[mrgA], w=[dbg["d_osb"]])
        if stop in ('attn', 'attn1', 'attn2') or (stop or '').startswith('al'):
            return
        yield
        scope('merge')
        for half in range(2):
            for pc in range(2):
                if half == 0:
                    wu = wload(wsc_usb[:, :, pc * 512:(pc + 1) * 512], wsc_usb, npart=64, nk=8)
                else:
                    wu = wload(wsc_udn[:, :, pc * 512:(pc + 1) * 512], wsc_udn, npart=128, nk=4)
                wg = wload(wsc[:, :, OFF_G + half * 1024 + pc * 512: OFF_G + half * 1024 + (pc + 1) * 512], wsc)
                yield
                for c4 in range(4):
                    cc = pc * 4 + c4
                    psm_ = gp()
                    if half == 0:
                        for h in range(8):
                            mm(psm_[:, :W], wu[:64, h, c4 * 128:(c4 + 1) * 128], oT_sb[:, h, :W], [wu, oT_sb], [psm_], start=(h == 0), stop=(h == 7))
                    else:
                        for f in range(4):
                            mm(psm_[:, :W], wu[:, f, c4 * 128:(c4 + 1) * 128], o_dnT[:, f, :W], [wu, o_dnT], [psm_], start=(f == 0), stop=(f == 3))
                    psg = gp()
                    for kc in range(8):
                        mm(psg[:, :W], wg[:, kc, c4 * 128:(c4 + 1) * 128], xTb[:, kc, :W], [wg, xTb], [psg], start=(kc == 0), stop=(kc == 7))
                    g_ = gt[cc % 2]
                    act(g_[:, :W], psg[:, :W], AF.Sigmoid, [psg, bg_sb], [g_], bias=bg_sb[:, half * 8 + cc: half * 8 + cc + 1])
                    if half == 0:
                        S.op("dve", lambda e: e.tensor_tensor(mrgA[:, cc, :W], g_[:, :W], psm_[:, :W], ALU.mult), r=[g_, psm_], w=[mrgA])
                    else:
                        S.op("dve", lambda e: e.tensor_tensor(mtmp[:, :W], g_[:, :W], psm_[:, :W], ALU.mult), r=[g_, psm_], w=[mtmp])
                        S.op("pool", lambda e: e.tensor_tensor(mrgT[:, cc, :W], mtmp[:, :W], mrgA[:, cc, :W], ALU.add), r=[mtmp, mrgA], w=[mrgT])
        if debug and sample:
            S.dma("sp", dbg["d_mrg"][:], mrgA[:, 0:8, :128], mrgA, r=[mrgA], w=[dbg["d_mrg"]])
        if stop == 'merge':
            return
        yield
        scope('ln1')
        for pc in range(2):
            wo = wload(wsc_out[:, :, pc * 512:(pc + 1) * 512], wsc_out)
            for c4 in range(4):
                dmc = pc * 4 + c4
                ps = gp()
                for cc in range(8):
                    mm(ps[:, :W], wo[:, cc, c4 * 128:(c4 + 1) * 128], mrgT[:, cc, :W], [wo, mrgT], [ps], start=(cc == 0), stop=(cc == 7))
                S.op("dve", lambda e: e.scalar_tensor_tensor(pre[:, dmc, :W], xTf[:, dmc, :W], ALPHA, ps[:, :W], ALU.mult, ALU.add), r=[xTf, ps], w=[pre])
        pss = bank_acc
        for dmc in range(8):
            mm(pss[:, :W], ones_f[:, :], pre[:, dmc, :W], [ones_f, pre], [pss], start=(dmc == 0), stop=(dmc == 7))
        S.op("dve", lambda e: e.tensor_scalar(mean[:, :W], pss[:, :W], 1.0 / D, None, ALU.mult), r=[pss], w=[mean])
        for dmc in range(8):
            S.op("pool", lambda e: e.tensor_tensor(pre[:, dmc, :W], pre[:, dmc, :W], mean[:, :W], ALU.subtract), r=[pre, mean], w=[pre])
        psq = bank_acc2
        for dmc in range(8):
            S.op("pool", lambda e: e.tensor_tensor(sqb[:, :W], pre[:, dmc, :W], pre[:, dmc, :W], ALU.mult), r=[pre], w=[sqb])
            mm(psq[:, :W], ones_f[:, :], sqb[:, :W], [ones_f, sqb], [psq], start=(dmc == 0), stop=(dmc == 7))
        act(rstd[:, :W], psq[:, :W], AF.Sqrt, [psq], [rstd], scale=1.0 / D, bias=LN_EPS)
        S.op("dve", lambda e: e.reciprocal(rstd[:, :W], rstd[:, :W]), r=[rstd], w=[rstd])
        for dmc in range(8):
            S.op("dve", lambda e: e.tensor_tensor(pre[:, dmc, :W], pre[:, dmc, :W], rstd[:, :W], ALU.mult), r=[pre, rstd], w=[pre])
            S.op("dve", lambda e: e.tensor_scalar(hTf[:, dmc, :W], pre[:, dmc, :W], l1g[:, dmc:dmc + 1], l1b[:, dmc:dmc + 1], ALU.mult, ALU.add), r=[pre, l1g, l1b], w=[hTf])
        S.op("pool", lambda e: e.tensor_copy(hTb[:, :, :W], hTf[:, :, :W]), r=[hTf], w=[hTb])
        if debug and sample:
            S.dma("sp", dbg["d_h"][:], hTf[:, :, :128], hTf, r=[hTf], w=[dbg["d_h"]])
        if stop == 'ln1':
            return
        yield
        scope('peerq')
        for pc in range(4):
            yield
            wq_ = wload(wsc_pq[:, :, pc * 512:(pc + 1) * 512], wsc_pq)
            for c4 in range(4):
                cq = pc * 4 + c4
                ps = gp()
                for kc in range(8):
                    mm(ps[:, :W], wq_[:, kc, c4 * 128:(c4 + 1) * 128], hTb[:, kc, :W], [wq_, hTb], [ps], start=(kc == 0), stop=(kc == 7))
                act(qpT[:, cq, :W], ps[:, :W], AF.Copy, [ps], [qpT])
        for a in range(W // 128):
            ta = slice(a * 128, (a + 1) * 128)
            eidi, gate, h_tok = eidi_l[a], gate_l[a], h_tok_l[a]
            yield
            for g4 in range(4):
                ps = gp()
                for q4 in range(4):
                    hp = g4 * 4 + q4
                    mm(ps[:, q4 * 128:(q4 + 1) * 128], qpT[:, hp, ta], keys_b[:, hp, :], [qpT, keys_b], [ps])
                act(sc[:, :, :].rearrange("p a k -> p (a k)"), ps[:, :], AF.Copy, [ps], [sc])
                for q4 in range(4):
                    hp = g4 * 4 + q4
                    S.op("dve", lambda e: e.max(tv[:, hp, 0:8], sc[:, q4, :]), r=[sc], w=[tv])
                    S.op("dve", lambda e: e.max_index(ti[:, hp, 0:8], tv[:, hp, 0:8], sc[:, q4, :]), r=[sc, tv], w=[ti])
                    S.op("dve", lambda e: e.match_replace(scw[:, :], tv[:, hp, 0:8], sc[:, q4, :], -1e30), r=[sc, tv], w=[scw])
                    S.op("dve", lambda e: e.max(tv[:, hp, 8:16], scw[:, :]), r=[scw], w=[tv])
                    S.op("dve", lambda e: e.max_index(ti[:, hp, 8:16], tv[:, hp, 8:16], scw[:, :]), r=[scw, tv], w=[ti])
            S.op("dve", lambda e: e.tensor_copy(tif[:], ti[:]), r=[ti], w=[tif])
            for h in range(8):
                S.op("dve", lambda e: e.tensor_tensor(cand[:], tv[:, 2 * h, :].unsqueeze(2).broadcast_to([128, 16, 16]),
                                                      tv[:, 2 * h + 1, :].unsqueeze(1).broadcast_to([128, 16, 16]), ALU.add), r=[tv], w=[cand])
                S.op("dve", lambda e: e.scalar_tensor_tensor(cidx[:], tif[:, 2 * h, :].unsqueeze(2).broadcast_to([128, 16, 16]), 128.0,
                                                             tif[:, 2 * h + 1, :].unsqueeze(1).broadcast_to([128, 16, 16]), ALU.mult, ALU.add), r=[tif], w=[cidx])
                cf = cand[:].rearrange("p a b -> p (a b)")
                xf = cidx[:].rearrange("p a b -> p (a b)")
                S.op("dve", lambda e: e.max(tsv[:, h, 0:8], cf), r=[cand], w=[tsv])
                S.op("dve", lambda e: e.match_replace(candw[:, :], tsv[:, h, 0:8], cf, -1e30), r=[cand, tsv], w=[candw])
                S.op("dve", lambda e: e.max(tsv[:, h, 8:16], candw[:, :]), r=[candw], w=[tsv])
                for k in range(16):
                    S.op("dve", lambda e: e.scalar_tensor_tensor(junkp[:, :], cf, tsv[:, h, k:k + 1], xf, ALU.is_equal, ALU.mult,
                                                                 accum_out=eidf[:, h * 16 + k: h * 16 + k + 1]), r=[cand, cidx, tsv], w=[junkp, eidf])
                S.op("dve", lambda e: e.tensor_scalar(psm[:, h:h + 1], tsv[:, h, 0:1], -1.0, None, ALU.mult), r=[tsv], w=[psm])
                act(gate[:, h, :], tsv[:, h, :], AF.Exp, [tsv, psm], [gate, psm], bias=psm[:, h:h + 1], accum_out=psm[:, 8 + h:9 + h])
            S.op("dve", lambda e: e.reciprocal(psm[:, 16:24], psm[:, 8:16]), r=[psm], w=[psm])
            for h in range(8):
                S.op("dve", lambda e: e.tensor_scalar(gate[:, h, :], gate[:, h, :], psm[:, 16 + h:17 + h], None, ALU.mult), r=[gate, psm], w=[gate])
            S.op("dve", lambda e: e.tensor_scalar(eidf[:], eidf[:], 16383.0, None, ALU.min), r=[eidf], w=[eidf])
            S.op("dve", lambda e: e.tensor_copy(eidi[:], eidf[:]), r=[eidf], w=[eidi])
            scope('peer_htok')
            for hh in range(2):
                ps = gp()
                for k4 in range(4):
                    kc = hh * 4 + k4
                    tr(ps[:, k4 * 128:(k4 + 1) * 128], hTf[:, kc, ta], ident[:, :], [hTf, ident], [ps])
                act(h_tok[:, hh * 512:(hh + 1) * 512], ps[:, :], AF.Copy, [ps], [h_tok])
        pending.append(peer_gather(W, out_row0, y_out, sample))

    def peer_gather(W, out_row0, y_out, sample):
        y_buf = y_out
        for a in range(W // 128):
            eidi, gate, h_tok = eidi_l[a], gate_l[a], h_tok_l[a]
            psm = psm2
            scope('peer_u')
            for s_ in range(128):
                ub = ubuf[s_ % 3]
                S.dma("pool", ub[:], peer_u[:, :], ub, r=[peer_u, eidi], w=[ub],
                      indirect=dict(out_offset=None, in_offset=bass.IndirectOffsetOnAxis(ap=eidi[:, s_:s_ + 1], axis=0)))
                S.op("dve", lambda e: e.scalar_tensor_tensor(junku[:], ub[:], 1.0, h_tok[:], ALU.mult, ALU.mult, accum_out=actv[:, s_:s_ + 1]), r=[ub, h_tok], w=[junku, actv])
                if s_ % 4 == 3:
                    yield
            scope('peer_v')
            act(coef[:], actv[:], AF.Gelu, [actv], [coef])
            S.op("dve", lambda e: e.tensor_tensor(coef[:], coef[:], gate[:].rearrange("p h k -> p (h k)"), ALU.mult), r=[coef, gate], w=[coef])
            for s_ in range(128):
                ub = ubuf[s_ % 3]
                S.dma("pool", ub[:], peer_v[:, :], ub, r=[peer_v, eidi], w=[ub],
                      indirect=dict(out_offset=None, in_offset=bass.IndirectOffsetOnAxis(ap=eidi[:, s_:s_ + 1], axis=0)))
                if s_ == 0:
                    S.op("dve", lambda e: e.tensor_scalar(facc[:], ub[:], coef[:, 0:1], None, ALU.mult), r=[ub, coef], w=[facc])
                else:
                    S.op("dve", lambda e: e.scalar_tensor_tensor(facc[:], ub[:], coef[:, s_:s_ + 1], facc[:], ALU.mult, ALU.add), r=[ub, coef, facc], w=[facc])
                if s_ % 4 == 3:
                    yield
            if debug and sample:
                S.dma("sp", dbg["d_ffn"][:], facc[:], facc, r=[facc], w=[dbg["d_ffn"]])
            scope('ln2')
            S.op("dve", lambda e: e.scalar_tensor_tensor(facc[:], h_tok[:], ALPHA, facc[:], ALU.mult, ALU.add), r=[h_tok, facc], w=[facc])
            act(junku[:], facc[:], AF.Copy, [facc], [junku, psm], accum_out=psm[:, 0:1])
            S.op("dve", lambda e: e.tensor_scalar(psm[:, 0:1], psm[:, 0:1], -1.0 / D, None, ALU.mult), r=[psm], w=[psm])
            S.op("dve", lambda e: e.tensor_scalar(facc[:], facc[:], psm[:, 0:1], None, ALU.add), r=[facc, psm], w=[facc])
            act(junku[:], facc[:], AF.Square, [facc], [junku, psm], accum_out=psm[:, 1:2])
            act(psm[:, 2:3], psm[:, 1:2], AF.Sqrt, [psm], [psm], scale=1.0 / D, bias=LN_EPS)
            S.op("dve", lambda e: e.reciprocal(psm[:, 2:3], psm[:, 2:3]), r=[psm], w=[psm])
            S.op("dve", lambda e: e.scalar_tensor_tensor(ybuf[:], facc[:], psm[:, 2:3], l2g[:], ALU.mult, ALU.mult), r=[facc, psm, l2g], w=[ybuf])
            S.op("pool", lambda e: e.tensor_tensor(ybuf[:], ybuf[:], l2b[:], ALU.add), r=[ybuf, l2b], w=[ybuf])
            S.dma("sp", y_out[out_row0 + a * 128: out_row0 + (a + 1) * 128, :], ybuf[:], ybuf, r=[ybuf], w=[y_buf])

    def attention_stream(grp, ZW, idx, sample):
        blocks = []
        loaders = []
        if sample:
            s_ = idx
            blocks.append(dict(ktbuf=KTcur, kt=(lambda cc: KTcur[:, cc, s_ * 32:(s_ + 1) * 32]), vbuf=Vcur,
                               v=(lambda h: Vcur[:32, s_, h * 64:(h + 1) * 64]), nk=32, mask=(mask_s[:32, :256], mask_s), load=None))
            for n_, g8 in enumerate(range(7, -1, -1)):
                i2 = n_ % 2

                def load(i2=i2, g8=g8):
                    for hh in range(2):
                        S.dma("sp", kvstg[0][:64, :].rearrange("p (c k) -> p c k", c=4), kTc[s_, :, hh * 4:(hh + 1) * 4, g8 * 256:(g8 + 1) * 256], kvstg[0], r=[kTc], w=[kvstg[0]])
                        S.op("pool", lambda e: e.tensor_copy(KTblk[i2][:, hh * 4:(hh + 1) * 4, :].rearrange("p c k -> p (c k)"), kvstg[0][:64, :]), r=[kvstg[0]], w=[KTblk[i2]])
                    S.dma("act", kvstg[1][:, :].rearrange("p (b c) -> p b c", b=2), vc[s_, g8 * 256:(g8 + 1) * 256, :].rearrange("(b p) c -> p b c", p=128), kvstg[1], r=[vc], w=[kvstg[1]])
                    S.op("dve", lambda e: e.tensor_copy(Vblk[i2][:].rearrange("p b c -> p (b c)"), kvstg[1][:, :]), r=[kvstg[1]], w=[Vblk[i2]])
                for b4 in range(1, -1, -1):
                    blocks.append(dict(ktbuf=KTblk[i2], kt=(lambda cc, i2=i2, b4=b4: KTblk[i2][:, cc, b4 * 128:(b4 + 1) * 128]), vbuf=Vblk[i2],
                                       v=(lambda h, i2=i2, b4=b4: Vblk[i2][:, b4, h * 64:(h + 1) * 64]), nk=128, mask=None,
                                       load=(load if b4 == 1 else None)))
        else:
            i = idx
            nsub = WP // 128
            for o in range(nsub - 1, -1, -1):
                blocks.append(dict(ktbuf=KTcur, kt=(lambda cc, o=o: KTcur[:, cc, o * 128:(o + 1) * 128]), vbuf=Vcur,
                                   v=(lambda h, o=o: Vcur[:, o, h * 64:(h + 1) * 64]), nk=128, mask=(mask_p[:, o, :], mask_p), load=None))
            n_ = 0
            pt = i - 1
            while pt >= 0:
                i2 = n_ % 2
                n_ += 1
                tiles_ = [pt]

                def load(i2=i2, tiles_=tiles_):
                    for u_, t_ in enumerate(tiles_):
                        S.dma("sp", KTblk[i2][:, :, u_ * WP:(u_ + 1) * WP], KTs[t_][:, :, :], KTblk[i2], r=[KTs[t_]], w=[KTblk[i2]])
                        S.dma("act", Vblk[i2][:, u_ * nsub:(u_ + 1) * nsub, :], Vs[t_][:, :, :], Vblk[i2], r=[Vs[t_]], w=[Vblk[i2]])
                firstb = True
                for u_, t_ in enumerate(tiles_):
                    for o in range(nsub - 1, -1, -1):
                        blocks.append(dict(ktbuf=KTblk[i2], kt=(lambda cc, i2=i2, u_=u_, o=o: KTblk[i2][:, cc, u_ * WP + o * 128: u_ * WP + (o + 1) * 128]),
                                           vbuf=Vblk[i2], v=(lambda h, i2=i2, u_=u_, o=o: Vblk[i2][:, u_ * nsub + o, h * 64:(h + 1) * 64]),
                                           nk=128, mask=None, load=(load if firstb else None), kvalid=(t_ if t_ < 3 else None)))
                        firstb = False
                pt -= 1
        if stop == 'attn1' or (stop or '').startswith('al'):
            blocks = blocks[:1]
        if stop == 'attn2':
            blocks = blocks[:3]
        yield from attention_run(grp, ZW, blocks)

    def attention_run(streams, ZW, blocks):
        nb_ = len(blocks)
        for bi, blk in enumerate(blocks):
            if bi % 2 == 0:
                yield
            if blk.get("load") is not None:
                blk["load"]()
            nk = blk["nk"]
            zps = []
            for si, (groups, po) in enumerate(streams):
                zp = gp()
                zps.append(zp)
                for (h, zc, qc, Wg) in groups:
                    mm(zp[:nk, zc:zc + Wg], blk["kt"](h), qTb[:, h, qc:qc + Wg], [blk["ktbuf"], qTb], [zp])
            for si, (groups, po) in enumerate(streams):
                act(Eb[si][:nk, :ZW], zps[si][:nk, :ZW], AF.Exp, [zps[si]], [Eb[si]], scale=0.125)
            for si, (groups, po) in enumerate(streams):
                if blk["mask"] is not None:
                    mk, mkb = blk["mask"]
                    mw = mk.shape[-1]
                    for c0 in range(0, ZW, mw):
                        S.op("pool", lambda e: e.tensor_tensor(Eb[si][:nk, c0:c0 + mw], Eb[si][:nk, c0:c0 + mw], mk, ALU.mult), r=[Eb[si], mkb], w=[Eb[si]])
                if blk.get("kvalid") is not None:
                    kvc = blk["kvalid"]
                    S.op("pool", lambda e: e.tensor_scalar(Eb[si][:nk, :ZW], Eb[si][:nk, :ZW], kval[:nk, kvc:kvc + 1], None, ALU.mult), r=[Eb[si], kval], w=[Eb[si]])
            for si, (groups, po) in enumerate(streams):
                act(Pb[si][:nk, :ZW], Eb[si][:nk, :ZW], AF.Ln, [Eb[si]], [Pb[si]], bias=1.0)
            cps = []
            for si, (groups, po) in enumerate(streams):
                cp = gp()
                cps.append(cp)
                mm(cp[:nk, :ZW], tri_b[:nk, :nk], Pb[si][:nk, :ZW], [tri_b, Pb[si]], [cp], start=True, stop=(bi == 0))
                if bi > 0:
                    mm(cp[:nk, :ZW], ones_b[:128, :nk], Pacc_l[si][:128, :ZW], [ones_b, Pacc_l[si]], [cp], start=False, stop=True)
            for si, (groups, po) in enumerate(streams):
                act(Gb[si][:nk, :ZW], cps[si][:nk, :ZW], AF.Exp, [cps[si]], [Gb[si]], scale=-1.0)
            for si, (groups, po) in enumerate(streams):
                S.op("dve", lambda e: e.tensor_tensor(wTb[si][:nk, :ZW], Eb[si][:nk, :ZW], Gb[si][:nk, :ZW], ALU.mult), r=[Eb[si], Gb[si]], w=[wTb[si]])
            for si, (groups, po) in enumerate(streams):
                Pa = Pacc_l[si]
                if bi == 0:
                    if nk < 128:
                        S.op("pool", lambda e: e.memset(Pa[:, :ZW], 0.0), r=[], w=[Pa])
                    S.op("pool", lambda e: e.tensor_copy(Pa[:nk, :ZW], Pb[si][:nk, :ZW]), r=[Pb[si]], w=[Pa])
                elif bi < nb_ - 1:
                    S.op("pool", lambda e: e.tensor_tensor(Pa[:nk, :ZW], Pa[:nk, :ZW], Pb[si][:nk, :ZW], ALU.add), r=[Pb[si], Pa], w=[Pa])
            for si, (groups, po) in enumerate(streams):
                for gi_, (h, zc, qc, Wg) in enumerate(groups):
                    S.op("pe", lambda e: e.matmul(po[:64, zc:zc + Wg], blk["v"](h), wTb[si][:nk, zc:zc + Wg], start=(bi == 0 and gi_ == 0), stop=(bi == nb_ - 1),
                                                  skip_group_check=True), r=[blk["vbuf"], wTb[si]], w=[po])
        for si, (groups, po) in enumerate(streams):
            for (h, zc, qc, Wg) in groups:
                S.op("dve", lambda e: e.tensor_copy(oT_sb[:, h, qc:qc + Wg], po[:64, zc:zc + Wg]), r=[po], w=[oT_sb])

    W_ = 128
    if do_sample:
        xsv = xT_s[:].rearrange("(k p) t -> p k t", p=128)
        tile(128, 4, 32, 32, (lambda hf: xsv[:, hf * 4:(hf + 1) * 4, :]), xT_s, True, True, True, None, None, 0, True, 0)
    if do_prompt:
        xpv = xT_p[:].rearrange("(k p) t -> p k t", p=128)
        for p in range(n_ptiles):
            tile(WP, 1, WP, 64, (lambda hf, p=p: xpv[:, hf * 4:(hf + 1) * 4, p * WP:(p + 1) * WP]), xT_p, (p % 4 == 3), (p == 0), (p == n_ptiles - 1),
                 None, None, (p // 4) * WP, False, p)
    while pending:
        for pg in list(pending):
            try:
                next(pg)
            except StopIteration:
                pending.remove(pg)
    S.finish(list(dout.values()))
    return nc, S


def _shared_inputs(w_in, b_gate, w_conv, a_log, dt_bias, dn_norm_w, w_up_sb, w_up_dn, w_out, ln1_g, ln1_b,
                   peer_wq, peer_keys, peer_u, peer_v, ln2_g, ln2_b):
    c = make_consts()
    f = np.ascontiguousarray
    d = dict(c)
    d["w_in"] = f(w_in[0])
    d["wconvT"] = f(w_conv[0].reshape(4, 12, 128).transpose(2, 1, 0))
    d["bgT"] = f(b_gate[0].reshape(16, 128).T)
    d["alog"] = f(np.broadcast_to(a_log[0][None, :], (128, 4)))
    d["dtb"] = f(np.broadcast_to(dt_bias[0][None, :], (128, 4)))
    d["normw"] = f(dn_norm_w[0].reshape(128, 1))
    d["w_up_sb"] = f(w_up_sb[0].reshape(8, 64, D).transpose(1, 0, 2))
    d["w_up_dn"] = f(w_up_dn[0].reshape(4, 128, D).transpose(1, 0, 2))
    d["w_out"] = f(w_out[0].reshape(8, 128, D).transpose(1, 0, 2))
    d["ln1g"] = f(ln1_g[0].reshape(8, 128).T)
    d["ln1b"] = f(ln1_b[0].reshape(8, 128).T)
    d["peer_wq"] = f(peer_wq[0].reshape(8, 128, 2048).transpose(1, 0, 2))
    d["keysT"] = f(peer_keys[0].reshape(16, 128, 128).transpose(2, 0, 1))
    d["peer_u"] = f(peer_u[0])
    d["peer_v"] = f(peer_v[0])
    d["ln2g"] = f(np.broadcast_to(ln2_g[0][None, :], (128, D)))
    d["ln2b"] = f(np.broadcast_to(ln2_b[0][None, :], (128, D)))
    return d


def _sample_inputs(c, x_sample, cache_sb_k, cache_sb_v, state_dn_ssm, state_dn_conv):
    f = np.ascontiguousarray
    sl = slice(4 * c, 4 * c + 4)
    d = {}
    d["xT_s"] = f(x_sample[sl].reshape(128, D).T)
    d["kTc"] = f(cache_sb_k[0, sl].transpose(0, 3, 2, 1))
    d["vc"] = f(cache_sb_v[0, sl].reshape(4, PAST, 512))
    d["S0"] = f(state_dn_ssm[0, sl].transpose(0, 2, 1, 3))
    d["convT"] = f(state_dn_conv[0, sl].reshape(4, 3, 12, 128).transpose(3, 2, 0, 1))
    return d


_PROG = {}
STOP = None


def kernel(x_prompt, x_sample, cache_sb_k, cache_sb_v, state_dn_ssm, state_dn_conv,
           w_in, b_gate, w_conv, a_log, dt_bias, dn_norm_w, w_up_sb, w_up_dn, w_out,
           ln1_g, ln1_b, peer_wq, peer_keys, peer_u, peer_v, ln2_g, ln2_b):
    args = [np.asarray(a, dtype=np.float32) for a in (x_prompt, x_sample, cache_sb_k, cache_sb_v, state_dn_ssm, state_dn_conv,
            w_in, b_gate, w_conv, a_log, dt_bias, dn_norm_w, w_up_sb, w_up_dn, w_out,
            ln1_g, ln1_b, peer_wq, peer_keys, peer_u, peer_v, ln2_g, ln2_b)]
    (x_prompt, x_sample, cache_sb_k, cache_sb_v, state_dn_ssm, state_dn_conv,
     w_in, b_gate, w_conv, a_log, dt_bias, dn_norm_w, w_up_sb, w_up_dn, w_out,
     ln1_g, ln1_b, peer_wq, peer_keys, peer_u, peer_v, ln2_g, ln2_b) = args
    if "nc" not in _PROG:
        _PROG["nc"] = build_program(stop=STOP)[0]
    nc = _PROG["nc"]
    shared = _shared_inputs(w_in, b_gate, w_conv, a_log, dt_bias, dn_norm_w, w_up_sb, w_up_dn, w_out, ln1_g, ln1_b,
                            peer_wq, peer_keys, peer_u, peer_v, ln2_g, ln2_b)
    in_maps = []
    for c in range(NCORE):
        b, r = c // 4, c % 4
        d = dict(shared)
        d.update(_sample_inputs(c, x_sample, cache_sb_k, cache_sb_v, state_dn_ssm, state_dn_conv))
        sh = (3 - r) * WP
        xt = np.zeros((D, SEQ), np.float32)
        xt[:, sh:] = x_prompt[b, :SEQ - sh].T
        d["xT_p"] = xt
        kv = np.ones((128, 4), np.float32)
        kv[:, :3 - r] = 0.0
        d["kval"] = kv
        if STOP == 'dn':
            d["peer_u"] = d["peer_u"][:8]
            d["peer_v"] = d["peer_v"][:8]
        in_maps.append(d)
    res = run_bass_kernel_spmd(nc, in_maps, core_ids=list(range(NCORE))).results
    B = 2
    y_p = np.zeros((B, SEQ, D), np.float32)
    kn_p = np.zeros((1, B, SEQ, 8, 64), np.float32)
    vn_p = np.zeros((1, B, SEQ, 8, 64), np.float32)
    y_s = np.zeros((32, 32, D), np.float32)
    kn_s = np.zeros((1, 32, 32, 8, 64), np.float32)
    vn_s = np.zeros((1, 32, 32, 8, 64), np.float32)
    S_p = np.zeros((1, B, 4, 128, 128), np.float32)
    S_s = np.zeros((1, 32, 4, 128, 128), np.float32)
    cv_p = np.zeros((1, B, 3, 1536), np.float32)
    cv_s = np.zeros((1, 32, 3, 1536), np.float32)
    for c in range(NCORE):
        b, r = c // 4, c % 4
        o = res[c]
        for m in range(NT_FULL // 4):
            t0 = (4 * m + r) * WP
            y_p[b, t0:t0 + WP] = o["y_p"][m * WP:(m + 1) * WP]
            kn_p[0, b, t0:t0 + WP] = o["kn_p"][m * WP:(m + 1) * WP].reshape(WP, 8, 64)
            vn_p[0, b, t0:t0 + WP] = o["vn_p"][m * WP:(m + 1) * WP].reshape(WP, 8, 64)
        if r == 3:
            S_p[0, b] = o["S_p"].transpose(1, 0, 2)
            cv_p[0, b] = o["cv_p"]
        y_s[4 * c:4 * c + 4] = o["y_s"].reshape(4, 32, D)
        kn_s[0, 4 * c:4 * c + 4] = o["kn_s"].reshape(4, 32, 8, 64)
        vn_s[0, 4 * c:4 * c + 4] = o["vn_s"].reshape(4, 32, 8, 64)
        S_s[0, 4 * c:4 * c + 4] = o["S_s"].transpose(0, 2, 1, 3)
        cv_s[0, 4 * c:4 * c + 4] = o["cv_s"]
    return (y_p, y_s, kn_p, vn_p, kn_s, vn_s, S_p, S_s, cv_p, cv_s)
```

```python
import numpy as np
import ml_dtypes
import concourse.bass as bass
import concourse.mybir as mybir
from concourse.bass_utils import run_bass_kernel_spmd

F32 = mybir.dt.float32
BF16 = mybir.dt.bfloat16
I32 = mybir.dt.int32
U32 = mybir.dt.uint32
AF = mybir.ActivationFunctionType
ALU = mybir.AluOpType

D = 1024
SEQ = 16384
NCORE = 8
WP = 256
NT_FULL = SEQ // WP
PAST = 2048
OFF_DN = 1536
OFF_Z = 3072
OFF_B = 3584
OFF_G = 3592
ALPHA = 2 ** 0.25
LN_EPS = 1e-5
RMS_EPS = 1e-6


class Buf:
    def __init__(self, S, t, name, dma=False):
        self.t = t
        self.name = name
        self.writers = {}
        self.readers = {}
        self.dsem = None
        self.dcnt = 0
        if dma:
            self.dsem = S.nc.alloc_semaphore("d_" + name)
            S.semh[("d", name)] = self.dsem
            S.dmabufs.append(self)

    def __getitem__(self, idx):
        return self.t[idx]


class Sched:
    def __init__(self, nc):
        self.nc = nc
        self.eng = {"pe": nc.tensor, "act": nc.scalar, "dve": nc.vector, "pool": nc.gpsimd, "sp": nc.sync}
        self.semh = {}
        self.cnt = {}
        self.seen = {}
        for k in self.eng:
            self.semh[k] = nc.alloc_semaphore("e_" + k)
            self.cnt[k] = 0
            self.seen[k] = {}
        self.nbuf = 0
        self.dmabufs = []
        self.gbufs = []
        self.n_ins = 0
        self.n_wait = 0

    def sb(self, shape, dtype, name=None, dma=False):
        self.nbuf += 1
        name = "s_" + (name or f"b{self.nbuf}")
        t = self.nc.alloc_sbuf_tensor(name, list(shape), dtype)
        return Buf(self, t, name, dma)

    def ps(self, shape, dtype=F32, name=None):
        self.nbuf += 1
        name = name or f"p{self.nbuf}"
        t = self.nc.alloc_psum_tensor(name, list(shape), dtype)
        return Buf(self, t, name, False)

    def dram(self, name, shape, dtype, kind="Internal"):
        t = self.nc.dram_tensor(name, list(shape), dtype, kind=kind)
        return Buf(self, t, name, False)

    def _wait(self, e, key, val):
        if self.seen[e].get(key, 0) >= val:
            return
        self.eng[e].wait_ge(self.semh[key], val)
        self.seen[e][key] = val
        self.n_wait += 1

    def _deps(self, e, r, w):
        for b in r:
            for (k, v) in b.writers.items():
                if not (k == "pe" and e == "pe"):
                    self._wait(e, k, v)
        for b in w:
            for (k, v) in b.writers.items():
                if not (k == "pe" and e == "pe"):
                    self._wait(e, k, v)
            for (k, v) in b.readers.items():
                if not (k == "pe" and e == "pe"):
                    self._wait(e, k, v)

    def _mark(self, key, val, r, w):
        for b in r:
            if b not in w:
                b.readers[key] = val
        for b in w:
            b.writers[key] = val

    def op(self, e, fn, r=(), w=()):
        r = list(r)
        w = list(w)
        self._deps(e, r, w)
        ins = fn(self.eng[e])
        self.cnt[e] += 1
        ins.then_inc(self.semh[e], 1)
        self._mark(e, self.cnt[e], r, w)
        self.n_ins += 1
        return ins

    def dma(self, e, out, in_, sbuf_buf, r=(), w=(), indirect=None, **kw):
        r = list(r)
        w = list(w)
        self._deps(e, r, w)
        b = sbuf_buf
        if e == "pool":
            if not hasattr(b, "gsem"):
                b.gsem = self.nc.alloc_semaphore("g_" + b.name)
                b.gcnt = 0
                self.semh[("g", b.name)] = b.gsem
                self.gbufs.append(b)
            key = ("g", b.name)
            if b.gcnt > 0:
                self._wait(e, key, 16 * b.gcnt)
            if indirect is None:
                ins = self.eng[e].dma_start(out=out, in_=in_, **kw)
            else:
                ins = self.eng[e].indirect_dma_start(out=out, in_=in_, **indirect)
            b.gcnt += 1
            ins.then_inc(b.gsem, 16)
            self._mark(key, 16 * b.gcnt, r, w)
            self.n_ins += 1
            return ins
        key = ("d", b.name)
        if b.dcnt > 0:
            self._wait(e, key, 16 * b.dcnt)
        ins = self.eng[e].dma_start(out=out, in_=in_, **kw)
        b.dcnt += 1
        ins.then_inc(b.dsem, 16)
        self._mark(key, 16 * b.dcnt, r, w)
        self.n_ins += 1
        return ins

    def finish(self, bufs):
        for b in bufs:
            for (k, v) in list(b.writers.items()) + list(b.readers.items()):
                self._wait("sp", k, v)
        for b in self.dmabufs:
            if b.dcnt > 0:
                self._wait("sp", ("d", b.name), 16 * b.dcnt)
        for b in self.gbufs:
            if b.gcnt > 0:
                self._wait("sp", ("g", b.name), 16 * b.gcnt)
        for k in ("pe", "act", "dve", "pool"):
            if self.cnt[k] > 0:
                self._wait("sp", k, self.cnt[k])


def make_consts():
    c = {}
    i = np.arange(128)
    c["ident"] = np.eye(128, dtype=np.float32)
    c["ones"] = np.ones((128, 128), np.float32)
    c["tri_b"] = (i[:, None] >= i[None, :]).astype(np.float32).astype(ml_dtypes.bfloat16)
    c["ones_b"] = np.ones((128, 128), np.float32).astype(ml_dtypes.bfloat16)
    t = np.arange(WP)
    mp = np.zeros((128, WP // 128, WP), np.float32)
    for o in range(WP // 128):
        mp[:, o, :] = ((128 * o + i[:, None]) < t[None, :])
    c["mask_p"] = mp
    j32 = np.arange(32)
    ms = np.zeros((128, 8, 32), np.float32)
    ms[:32] = (j32[:, None, None] < j32[None, None, :])
    c["mask_s"] = ms.reshape(128, 256)
    m = np.arange(64)
    dn = np.zeros((64, 6, 4, 64), np.float32)
    dn[:, 0] = (m[:, None] <= m[None, :])[:, None, :]
    dn[:, 1] = (m[:, None] > m[None, :])[:, None, :]
    dn[:, 2] = (m[None, :] > m[:, None])[:, None, :]
    dn[:, 3] = (m[None, :] >= m[:, None])[:, None, :]
    dn[:, 4] = np.eye(64)[:, None, :]
    dnp = np.zeros((128, 6 * 4 * 64), np.float32)
    dnp[:64] = dn.reshape(64, -1)
    c["dn"] = dnp
    return c


def build_program(n_ptiles=NT_FULL, do_sample=True, debug=False, stop=None):
    nc = bass.Bass("TRN2", target_bir_lowering=False)
    S = Sched(nc)
    EI = "ExternalInput"
    EO = "ExternalOutput"
    do_prompt = n_ptiles > 0
    n_own = (n_ptiles + 3) // 4 if do_prompt else 0
    din = {}

    def inp(name, shape, dt=F32):
        din[name] = S.dram(name, shape, dt, kind=EI)
        return din[name]

    dout = {}

    def outp(name, shape, dt=F32):
        dout[name] = S.dram(name, shape, dt, kind=EO)
        return dout[name]

    if do_prompt:
        xT_p = inp("xT_p", [D, n_ptiles * WP])
    if do_sample:
        xT_s = inp("xT_s", [D, 128])
        kTc = inp("kTc", [4, 64, 8, PAST])
        vc = inp("vc", [4, PAST, 512])
        S0in = inp("S0", [4, 128, 4, 128])
        convT = inp("convT", [128, 12, 4, 3])
    w_in = inp("w_in", [D, 5640])
    wconvT = inp("wconvT", [128, 12, 4])
    bgT = inp("bgT", [128, 16])
    alog = inp("alog", [128, 4])
    dtb = inp("dtb", [128, 4])
    normw = inp("normw", [128, 1])
    w_up_sb = inp("w_up_sb", [64, 8, D])
    w_up_dn = inp("w_up_dn", [128, 4, D])
    w_out = inp("w_out", [128, 8, D])
    ln1g = inp("ln1g", [128, 8])
    ln1b = inp("ln1b", [128, 8])
    peer_wq = inp("peer_wq", [128, 8, 2048])
    keysT = inp("keysT", [128, 16, 128])
    NEXP = 8 if stop == 'dn' else 16384
    peer_u = inp("peer_u", [NEXP, D])
    peer_v = inp("peer_v", [NEXP, D])
    ln2g = inp("ln2g", [128, D])
    ln2b = inp("ln2b", [128, D])
    c_ident = inp("ident", [128, 128])
    c_ones = inp("ones", [128, 128])
    c_tri_b = inp("tri_b", [128, 128], BF16)
    c_ones_b = inp("ones_b", [128, 128], BF16)
    c_mask_p = inp("mask_p", [128, WP // 128, WP])
    c_mask_s = inp("mask_s", [128, 256])
    c_dn = inp("dn", [128, 6 * 4 * 64])
    kval_in = inp("kval", [128, 4])

    if do_prompt:
        NOWN = n_own
        y_p = outp("y_p", [NOWN * WP, D])
        kn_p = outp("kn_p", [NOWN * WP, 512])
        vn_p = outp("vn_p", [NOWN * WP, 512])
        S_p = outp("S_p", [128, 4, 128])
        cv_p = outp("cv_p", [3, 1536])
    if do_sample:
        y_s = outp("y_s", [128, D])
        kn_s = outp("kn_s", [128, 512])
        vn_s = outp("vn_s", [128, 512])
        S_s = outp("S_s", [4, 128, 4, 128])
        cv_s = outp("cv_s", [4, 3, 1536])
    dbg = {}
    if debug:
        for nm, shp in [("d_osb", [64, 8, 128]), ("d_odn", [128, 4, 128]), ("d_h", [128, 8, 128]), ("d_ffn", [128, D]),
                        ("d_qkv", [128, 12, 128]), ("d_bg", [32, 4, 8]), ("d_mrg", [128, 8, 128])]:
            dbg[nm] = outp(nm, shp, F32)

    wsc = S.dram("wsc", [128, 8, 5640], BF16)
    wsc_pq = S.dram("wsc_pq", [128, 8, 2048], BF16)
    wsc_out = S.dram("wsc_out", [128, 8, D], BF16)
    wsc_udn = S.dram("wsc_udn", [128, 4, D], BF16)
    wsc_usb = S.dram("wsc_usb", [64, 8, D], BF16)
    if do_prompt:
        KTs = [S.dram(f"KTs{i}", [64, 8, WP], BF16) for i in range(n_ptiles)]
        Vs = [S.dram(f"Vs{i}", [128, WP // 128, 512], BF16) for i in range(n_ptiles)]

    ident = S.sb([128, 128], F32, "ident", dma=True)
    ones_f = S.sb([128, 128], F32, "ones_f", dma=True)
    tri_b = S.sb([128, 128], BF16, "tri_b", dma=True)
    ones_b = S.sb([128, 128], BF16, "ones_b", dma=True)
    mask_p = S.sb([128, WP // 128, WP], F32, "mask_p", dma=True)
    mask_s = S.sb([128, 256], F32, "mask_s", dma=True)
    dnc = S.sb([128, 6, 4, 64], F32, "dnc", dma=True)
    wcv = S.sb([128, 12, 4], F32, "wcv", dma=True)
    bg_sb = S.sb([128, 16], F32, "bg_sb", dma=True)
    nA = S.sb([128, 4], F32, "nA", dma=True)
    dtb_sb = S.sb([128, 4], F32, "dtb_sb", dma=True)
    normw_sb = S.sb([128, 1], F32, "normw_sb", dma=True)
    l1g = S.sb([128, 8], F32, "l1g", dma=True)
    l1b = S.sb([128, 8], F32, "l1b", dma=True)
    keys_b = S.sb([128, 16, 128], BF16, "keys_b")
    l2g = S.sb([128, D], F32, "l2g", dma=True)
    l2b = S.sb([128, D], F32, "l2b", dma=True)
    kval = S.sb([128, 4], F32, "kval", dma=True)
    S.dma("sp", kval[:], kval_in[:], kval, r=[kval_in], w=[kval])
    for (sbuf, src, q) in [(ident, c_ident, "sp"), (ones_f, c_ones, "act"), (tri_b, c_tri_b, "sp"), (ones_b, c_ones_b, "act"),
                           (mask_p, c_mask_p, "sp"), (mask_s, c_mask_s, "act"), (wcv, wconvT, "sp"), (bg_sb, bgT, "act"),
                           (nA, alog, "sp"), (dtb_sb, dtb, "act"), (normw_sb, normw, "sp"), (l1g, ln1g, "act"),
                           (l1b, ln1b, "sp"), (l2g, ln2g, "sp"), (l2b, ln2b, "act")]:
        S.dma(q, sbuf[:], src[:], sbuf, r=[src], w=[sbuf])
    S.dma("sp", dnc[:].rearrange("p a h c -> p (a h c)"), c_dn[:], dnc, r=[c_dn], w=[dnc])
    S.op("act", lambda e: e.activation(nA[:], nA[:], AF.Exp), r=[nA], w=[nA])
    S.op("dve", lambda e: e.tensor_scalar(nA[:], nA[:], -1.0, None, ALU.mult), r=[nA], w=[nA])

    banks = [S.ps([128, 512], F32, f"bank{i}") for i in range(8)]
    gp_state = [0]

    def gp():
        b = banks[gp_state[0] % 6]
        gp_state[0] += 1
        return b

    bank_acc = banks[6]
    bank_acc2 = banks[7]

    ubuf = [S.sb([128, D], F32, f"ubuf{i}", dma=True) for i in range(3)]
    stg = ubuf[0:2]
    stgb = [S.sb([128, 1024], BF16, f"stgb{i}", dma=True) for i in range(2)]
    cv_i = [0]

    def convert(src_ap, dst_ap, srcbuf, dstbuf, npart, n):
        i = cv_i[0] % 2
        cv_i[0] += 1
        q = "sp" if i == 0 else "act"
        S.dma(q, stg[i][:npart, :n], src_ap, stg[i], r=[srcbuf], w=[stg[i]])
        eng = "dve" if i == 0 else "pool"
        S.op(eng, lambda e: e.tensor_copy(stgb[i][:npart, :n], stg[i][:npart, :n]), r=[stg[i]], w=[stgb[i]])
        S.dma(q, dst_ap, stgb[i][:npart, :n], stgb[i], r=[stgb[i]], w=[dstbuf])

    w_in_v = w_in[:].rearrange("(k p) c -> p k c", p=128)
    for kc in range(8):
        for c0 in range(0, 5640, 1024):
            n = min(1024, 5640 - c0)
            convert(w_in_v[:, kc, c0:c0 + n], wsc[:, kc, c0:c0 + n], w_in, wsc, 128, n)
    for kc in range(8):
        for c0 in range(0, 2048, 1024):
            convert(peer_wq[:, kc, c0:c0 + 1024], wsc_pq[:, kc, c0:c0 + 1024], peer_wq, wsc_pq, 128, 1024)
    for kc in range(8):
        convert(w_out[:, kc, :], wsc_out[:, kc, :], w_out, wsc_out, 128, 1024)
        convert(w_up_sb[:, kc, :], wsc_usb[:, kc, :], w_up_sb, wsc_usb, 64, 1024)
    for kc in range(4):
        convert(w_up_dn[:, kc, :], wsc_udn[:, kc, :], w_up_dn, wsc_udn, 128, 1024)
    for hp0 in range(0, 16, 8):
        S.dma("sp", stg[0][:, :], keysT[:, hp0:hp0 + 8, :].rearrange("p a k -> p (a k)"), stg[0], r=[keysT], w=[stg[0]])
        S.op("dve", lambda e: e.tensor_copy(keys_b[:, hp0:hp0 + 8, :].rearrange("p a k -> p (a k)"), stg[0][:, :]), r=[stg[0]], w=[keys_b])

    Wba = S.sb([128, 8, 8], BF16, "Wba", dma=True)
    S.dma("sp", Wba[:], wsc[:, :, OFF_B:OFF_B + 8], Wba, r=[wsc], w=[Wba])
    wslots = [S.sb([128, 8, 512], BF16, f"wslot{i}", dma=True) for i in range(2)]
    ws_i = [0]

    def wload(src_ap, srcbuf, npart=128, nk=8):
        i = ws_i[0] % 2
        ws_i[0] += 1
        q = ["sp", "act"][i]
        S.dma(q, wslots[i][:npart, :nk, :], src_ap, wslots[i], r=[srcbuf], w=[wslots[i]])
        return wslots[i]

    WM = WP if do_prompt else 128
    xTf = S.sb([128, 8, WM], F32, "xTf", dma=True)
    xTb = S.sb([128, 8, WM], BF16, "xTb")
    KTcur = S.sb([64, 8, WM], BF16, "KTcur", dma=True)
    Vcur = S.sb([128, 4, 512], BF16, "Vcur", dma=True)
    kvf = [S.sb([128, 512], F32, f"kvf{i}", dma=True) for i in range(2)]
    xin_flat = [S.sb([128, WM + 12], F32, f"xin{i}", dma=True) for i in range(2)]
    hal = S.sb([128, 12, 3], F32, "hal", dma=True)
    qkv = S.sb([128, 12, WM], F32, "qkv", dma=True)
    cvo = S.sb([4, 512], F32, "cvo", dma=True)
    betag = S.sb([64, 4, 8], F32, "betag", dma=True)
    tmp48 = S.sb([64, 8], F32, "tmp48")
    Sst = S.sb([128, 4, 128], F32, "Sst", dma=True)
    qTb = S.sb([64, 8, WM], BF16, "qTb")
    zsT = S.sb([128, 4, WM], BF16, "zsT")
    o_dnT = S.sb([128, 4, WM], BF16, "o_dnT", dma=True)
    oT_sb = S.sb([64, 8, WM], BF16, "oT_sb", dma=True)
    k_tok = S.sb([64, 4, 128], F32, "k_tok")
    v_tok = S.sb([64, 4, 128], F32, "v_tok")
    keg = S.sb([64, 4, 128], F32, "keg")
    sm = S.sb([128, 32], F32, "sm")
    trig = S.sb([64, 4, 64], F32, "trig")
    decT = S.sb([64, 4, 64], F32, "decT")
    decTs = S.sb([64, 4, 64], F32, "decTs")
    qkt = S.sb([64, 4, 64], F32, "qkt")
    Qm = [S.sb([64, 4, 64], F32, f"Qm{i}") for i in range(2)]
    QmT = [S.sb([64, 4, 64], F32, f"QmT{i}") for i in range(2)]
    FmT = [S.sb([64, 4, 64], F32, f"FmT{i}") for i in range(2)]
    Xb = [S.sb([64, 4, 256], F32, f"Xb{i}") for i in range(2)]
    xvb = S.sb([64, 4, 128], F32, "xvb")
    kdec = xvb
    xwT = S.sb([128, 4, 64], F32, "xwT")
    v_new = S.sb([64, 4, 128], F32, "v_new")
    o1s = keg
    o_tok = S.sb([64, 4, 128], F32, "o_tok")
    junk64 = S.sb([64, 128], F32, "junk64")
    KTblk = [S.sb([64, 8, 256], BF16, f"KTblk{i}", dma=True) for i in range(2)]
    Vblk = [S.sb([128, 2, 512], BF16, f"Vblk{i}", dma=True) for i in range(2)]
    kvstg = [ubuf[0], ubuf[1]]
    ZM = 512 if do_prompt else 256
    Eb = [S.sb([128, ZM], F32, f"Eb{i}") for i in range(2)]
    Pb = [S.sb([128, ZM], BF16, f"Pb{i}") for i in range(2)]
    Gb = [S.sb([128, ZM], BF16, f"Gb{i}") for i in range(2)]
    wTb = [S.sb([128, ZM], BF16, f"wTb{i}") for i in range(2)]
    Pacc_l = [S.sb([128, ZM], BF16, f"Pacc{i}") for i in range(2)]
    WL = 8 if stop == 'dn' else WM
    gt = [S.sb([128, WM], F32, f"gt{i}") for i in range(2)]
    sqb, nrm = gt[0], gt[1]
    mtmp = S.sb([128, WL], F32, "mtmp")
    mrgA = qkv
    mrgT = S.sb([128, 8, WL], BF16, "mrgT")
    pre = mrgA
    hTf = xTf
    hTb = xTb
    mean = nrm
    rstd = S.sb([128, WL], F32, "rstd")
    qpT = S.sb([128, 16, WL], BF16, "qpT")
    sc = S.sb([128, 4, 128], F32, "sc")
    scw = S.sb([128, 128], F32, "scw")
    tv = S.sb([128, 16, 16], F32, "tv")
    ti = S.sb([128, 16, 16], U32, "ti")
    tif = S.sb([128, 16, 16], F32, "tif")
    cand = S.sb([128, 16, 16], F32, "cand")
    candw = S.sb([128, 256], F32, "candw")
    cidx = S.sb([128, 16, 16], F32, "cidx")
    tsv = S.sb([128, 8, 16], F32, "tsv")
    eidf = S.sb([128, 128], F32, "eidf")
    eidi_l = [S.sb([128, 128], I32, f"eidi{i}") for i in range(2)]
    gate_l = [S.sb([128, 8, 16], F32, f"gate{i}") for i in range(2)]
    psm = S.sb([128, 32], F32, "psm")
    junkp = S.sb([128, 256], F32, "junkp")
    h_tok_l = [S.sb([128, D], F32, f"h_tok{i}") for i in range(2)]
    psm2 = S.sb([128, 8], F32, "psm2")
    junku = stgb[0]
    actv = S.sb([128, 128], F32, "actv")
    coef = S.sb([128, 128], F32, "coef")
    facc = S.sb([128, D], F32, "facc", dma=True)
    ybuf = facc

    triT = lambda C: dnc[:C, 0, 0, :C]
    ustr = lambda C: dnc[:C, 1, 0, :C]
    maskS = lambda C: dnc[:C, 2, :, :C]
    maskI = lambda C: dnc[:C, 3, :, :C]
    identR = lambda C: dnc[:C, 4, :, :C]

    def mm(out, lhsT, rhs, r, w, start=True, stop=True):
        S.op("pe", lambda e: e.matmul(out, lhsT, rhs, start=start, stop=stop), r=r, w=w)

    def tr(out, in_, idn, r, w):
        S.op("pe", lambda e: e.transpose(out, in_, idn), r=r, w=w)

    def act(out, in_, func, r, w, **kw):
        S.op("act", lambda e: e.activation(out, in_, func, **kw), r=r, w=w)

    def proj_fm(wbuf, wcol0, ncc, evac):
        for cc in range(ncc):
            ps = gp()
            for kc in range(8):
                mm(ps[:, :W_], wbuf[:, kc, wcol0 + cc * 128: wcol0 + (cc + 1) * 128], xTb[:, kc, :W_], [wbuf, xTb], [ps], start=(kc == 0), stop=(kc == 7))
            evac(cc, ps)


    cur_scope = [None]

    def scope(name):
        return

    pending = []

    def tile(*a, **k):
        for _ in tile_(*a, **k):
            for pg in list(pending):
                try:
                    next(pg)
                except StopIteration:
                    pending.remove(pg)

    def tile_(W, nseq, Wseq, C, xsrc_ap, xsrc_buf, owned, first, last, halo_src, kv_past_blocks, out_row0, sample, ti_idx):
        nonlocal W_
        W_ = W
        y_out, kn_out, vn_out = (y_s, kn_s, vn_s) if sample else (y_p, kn_p, vn_p)
        y_buf, kn_buf, vn_buf = y_out, kn_out, vn_out
        nch = W // C
        if stop == 'pro':
            return
        scope('kvproj')
        S.dma("sp", xTf[:, 0:4, :W], xsrc_ap(0), xTf, r=[xsrc_buf], w=[xTf])
        S.dma("act", xTf[:, 4:8, :W], xsrc_ap(1), xTf, r=[xsrc_buf], w=[xTf])
        S.op("pool", lambda e: e.tensor_copy(xTb[:, :, :W], xTf[:, :, :W]), r=[xTf], w=[xTb])
        if stop == 'x':
            return
        def proj_heads(wbuf, dst):
            for h in range(8):
                ps = gp()
                for kc in range(8):
                    mm(ps[:64, :W], wbuf[:, kc, h * 64:(h + 1) * 64], xTb[:, kc, :W], [wbuf, xTb], [ps], start=(kc == 0), stop=(kc == 7))
                act(dst[:, h, :W], ps[:64, :W], AF.Copy, [ps], [dst])
        Wk = wload(wsc[:, :, 512:1024], wsc)
        proj_heads(Wk, KTcur)
        if not sample:
            S.dma("sp", KTs[ti_idx][:, :, :], KTcur[:, :, :W], KTcur, r=[KTcur], w=[KTs[ti_idx]])
        if stop == 'kt':
            return

        def tokmajor_out(wt, ob_, kb_, q_):
            for g in range(W // 128):
                ps = gp()
                for kc in range(8):
                    mm(ps[:, :], xTb[:, kc, g * 128:(g + 1) * 128], wt[:, kc, :], [xTb, wt], [ps], start=(kc == 0), stop=(kc == 7))
                act(kb_[:, :], ps[:, :], AF.Copy, [ps], [kb_])
                S.dma(q_, ob_[out_row0 + g * 128: out_row0 + (g + 1) * 128, :], kb_[:, :], kb_, r=[kb_], w=[ob_])
        if owned:
            tokmajor_out(Wk, kn_out, kvf[1], "act")
        Wv = wload(wsc[:, :, 1024:1536], wsc)
        gs = 32 if sample else 128
        ng = W // gs
        for g in range(ng):
            ps = gp()
            for kc in range(8):
                mm(ps[:gs, :], xTb[:, kc, g * gs:(g + 1) * gs], Wv[:, kc, :], [xTb, Wv], [ps], start=(kc == 0), stop=(kc == 7))
            S.op("dve", lambda e: e.tensor_copy(Vcur[:gs, g, :], ps[:gs, :]), r=[ps], w=[Vcur])
        if owned:
            tokmajor_out(Wv, vn_out, kvf[0], "sp")
        if not sample:
            S.dma("act", Vs[ti_idx][:, :, :], Vcur[:, :ng, :], Vcur, r=[Vcur], w=[Vs[ti_idx]])
        if stop in ('kv', 'kv1', 'kv2'):
            return
        yield
        scope('dnproj')
        if first and not sample:
            S.op("pool", lambda e: e.memset(hal[:], 0.0), r=[], w=[hal])
        for piece in range(3):
            wb = wload(wsc[:, :, OFF_DN + piece * 512: OFF_DN + (piece + 1) * 512], wsc)
            for c4 in range(4):
                cc = piece * 4 + c4
                xbuf_ = xin_flat[cc % 2]
                xb_ = xbuf_[:, :nseq * (Wseq + 3)].rearrange("p (s w) -> p s w", s=nseq)
                ps = gp()
                for kc in range(8):
                    mm(ps[:, :W], wb[:, kc, c4 * 128:(c4 + 1) * 128], xTb[:, kc, :W], [wb, xTb], [ps], start=(kc == 0), stop=(kc == 7))
                act(xb_[:, :nseq, 3:3 + Wseq], ps[:, :W].rearrange("p (s w) -> p s w", s=nseq), AF.Copy, [ps], [xbuf_])
                if sample:
                    S.dma("sp", xb_[:, :nseq, 0:3], convT[:, cc, :, :], xbuf_, r=[convT], w=[xbuf_])
                else:
                    S.op("pool", lambda e: e.tensor_copy(xb_[:, 0, 0:3], hal[:, cc, :]), r=[hal], w=[xbuf_])
                    S.op("pool", lambda e: e.tensor_copy(hal[:, cc, :], xb_[:, 0, Wseq:Wseq + 3]), r=[xbuf_], w=[hal])
                qv = qkv[:, cc, :W].rearrange("p (s w) -> p s w", s=nseq)
                S.op("dve", lambda e: e.tensor_scalar(qv, xb_[:, :nseq, 0:Wseq], wcv[:, cc, 0:1], None, ALU.mult), r=[xbuf_, wcv], w=[qkv])
                for i in range(1, 4):
                    S.op("dve", lambda e: e.scalar_tensor_tensor(qv, xb_[:, :nseq, i:i + Wseq], wcv[:, cc, i:i + 1], qv, ALU.mult, ALU.add), r=[xbuf_, wcv, qkv], w=[qkv])
                act(qkv[:, cc, :W], qkv[:, cc, :W], AF.Silu, [qkv], [qkv])
            yield
            if sample or last:
                for s_ in range(nseq):
                    ps = gp()
                    t1 = (s_ + 1) * Wseq
                    for kc in range(8):
                        mm(ps[:3, :], xTb[:, kc, t1 - 3:t1], wb[:, kc, :], [xTb, wb], [ps], start=(kc == 0), stop=(kc == 7))
                    S.op("dve", lambda e: e.tensor_copy(cvo[:3, :], ps[:3, :]), r=[ps], w=[cvo])
                    if sample:
                        S.dma("sp", cv_s[s_, :, piece * 512:(piece + 1) * 512], cvo[:3, :], cvo, r=[cvo], w=[cv_s])
                    else:
                        S.dma("sp", cv_p[:, piece * 512:(piece + 1) * 512], cvo[:3, :], cvo, r=[cvo], w=[cv_p])
        if stop == 'conv':
            return
        yield
        for cc in range(8):
            S.op("pool", lambda e: e.tensor_tensor(sqb[:, :W], qkv[:, cc, :W], qkv[:, cc, :W], ALU.mult), r=[qkv], w=[sqb])
            ps = gp()
            mm(ps[:, :W], ones_f[:, :], sqb[:, :W], [ones_f, sqb], [ps])
            act(nrm[:, :W], ps[:, :W], AF.Sqrt, [ps], [nrm], bias=RMS_EPS)
            S.op("dve", lambda e: e.reciprocal(nrm[:, :W], nrm[:, :W]), r=[nrm], w=[nrm])
            sc_ = (128 ** -0.5) if cc < 4 else 1.0
            S.op("dve", lambda e: e.scalar_tensor_tensor(qkv[:, cc, :W], qkv[:, cc, :W], sc_, nrm[:, :W], ALU.mult, ALU.mult), r=[qkv, nrm], w=[qkv])
        if debug and sample:
            S.dma("sp", dbg["d_qkv"][:], qkv[:, :, :128], qkv, r=[qkv], w=[dbg["d_qkv"]])
        for j in range(nch):
            ps = gp()
            for kc in range(8):
                mm(ps[:C, 0:8], xTb[:, kc, j * C:(j + 1) * C], Wba[:, kc, :], [xTb, Wba], [ps], start=(kc == 0), stop=(kc == 7))
            act(betag[:C, j, 0:4], ps[:C, 0:4], AF.Sigmoid, [ps], [betag])
            S.op("dve", lambda e: e.tensor_tensor(tmp48[:C, 0:4], ps[:C, 4:8], dtb_sb[:C, :], ALU.add), r=[ps, dtb_sb], w=[tmp48])
            act(tmp48[:C, 0:4], tmp48[:C, 0:4], AF.Exp, [tmp48], [tmp48])
            act(tmp48[:C, 0:4], tmp48[:C, 0:4], AF.Ln, [tmp48], [tmp48], bias=1.0)
            S.op("dve", lambda e: e.tensor_tensor(betag[:C, j, 4:8], tmp48[:C, 0:4], nA[:C, :], ALU.mult), r=[tmp48, nA], w=[betag])
        if debug and sample:
            S.dma("sp", dbg["d_bg"][:], betag[:32, :, :], betag, r=[betag], w=[dbg["d_bg"]])
        if stop == 'bg':
            return
        yield
        scope('qz')
        if owned:
            wb = wload(wsc[:, :, 0:512], wsc)
            proj_heads(wb, qTb)
            wb = wload(wsc[:, :, OFF_Z:OFF_Z + 512], wsc)
            def ev_z(cc, ps):
                act(zsT[:, cc, :W], ps[:, :W], AF.Silu, [ps], [zsT])
            proj_fm(wb, 0, 4, ev_z)
        scope('dnchunks')
        nlev = 6 if C == 64 else 5
        for j in range(nch):
            tc_ = slice(j * C, (j + 1) * C)
            if sample:
                S.dma("sp", Sst[:], S0in[j], Sst, r=[S0in], w=[Sst])
            elif first and j == 0:
                S.op("pool", lambda e: e.memset(Sst[:], 0.0), r=[], w=[Sst])
            ps = gp()
            for h in range(4):
                tr(ps[:C, h * 128:(h + 1) * 128], qkv[:, 4 + h, tc_], ident[:, :], [qkv, ident], [ps])
            act(k_tok[:C].rearrange("p h d -> p (h d)"), ps[:C, :], AF.Copy, [ps], [k_tok])
            ps = gp()
            for h in range(4):
                tr(ps[:C, h * 128:(h + 1) * 128], qkv[:, 8 + h, tc_], ident[:, :], [qkv, ident], [ps])
            S.op("dve", lambda e: e.tensor_copy(v_tok[:C].rearrange("p h d -> p (h d)"), ps[:C, :]), r=[ps], w=[v_tok])
            bgj = betag[:C, j, :]
            ps = gp()
            mm(ps[:C, 0:4], triT(C), betag[:C, j, 4:8], [dnc, betag], [ps])
            mm(ps[:, 8:12], ones_f[:C, :], betag[:C, j, 4:8], [ones_f, betag], [ps])
            S.op("dve", lambda e: e.tensor_copy(sm[:C, 0:4], ps[:C, 0:4]), r=[ps], w=[sm])
            act(sm[:C, 4:8], ps[:C, 0:4], AF.Exp, [ps], [sm])
            act(sm[:, 16:20], ps[:, 8:12], AF.Exp, [ps], [sm])
            S.op("dve", lambda e: e.tensor_tensor(sm[:C, 8:12], ps[:C, 8:12], sm[:C, 0:4], ALU.subtract), r=[ps, sm], w=[sm])
            act(sm[:C, 8:12], sm[:C, 8:12], AF.Exp, [sm], [sm])
            S.op("dve", lambda e: e.tensor_scalar(sm[:C, 12:16], betag[:C, j, 0:4], -1.0, None, ALU.mult), r=[betag], w=[sm])
            for h in range(4):
                S.op("pool", lambda e: e.tensor_scalar(trig[:C, h, :C], triT(C), betag[:C, j, 4 + h:5 + h], None, ALU.mult), r=[dnc, betag], w=[trig])
            ps = gp()
            for h in range(4):
                mm(ps[:C, h * C:(h + 1) * C], ustr(C), trig[:C, h, :C], [dnc, trig], [ps])
            act(decT[:C, :, :C], ps[:C, :4 * C].rearrange("p (h c) -> p h c", h=4), AF.Exp, [ps], [decT])
            S.op("pool", lambda e: e.tensor_tensor(decTs[:C, :, :C], decT[:C, :, :C], maskS(C), ALU.mult), r=[decT, dnc], w=[decTs])
            S.op("pool", lambda e: e.tensor_tensor(decT[:C, :, :C], decT[:C, :, :C], maskI(C), ALU.mult), r=[decT, dnc], w=[decT])
            psK = gp()
            for h in range(4):
                mm(psK[:C, h * C:(h + 1) * C], qkv[:, 4 + h, tc_], qkv[:, 4 + h, tc_], [qkv], [psK])
            psQ = gp()
            for h in range(4):
                mm(psQ[:C, h * C:(h + 1) * C], qkv[:, 4 + h, tc_], qkv[:, h, tc_], [qkv], [psQ])
            S.op("dve", lambda e: e.tensor_tensor(qkt[:C, :, :C], psQ[:C, :4 * C].rearrange("p (h c) -> p h c", h=4), decT[:C, :, :C], ALU.mult), r=[psQ, decT], w=[qkt])
            for h in range(4):
                S.op("dve", lambda e: e.scalar_tensor_tensor(QmT[0][:C, h, :C], psK[:C, h * C:(h + 1) * C], sm[:C, 12 + h:13 + h], decTs[:C, h, :C], ALU.mult, ALU.mult), r=[psK, sm, decTs], w=[QmT[0]])
            ps = gp()
            for h in range(4):
                tr(ps[:C, h * C:(h + 1) * C], QmT[0][:C, h, :C], ident[:C, :C], [QmT[0], ident], [ps])
            act(Qm[0][:C, :, :C], ps[:C, :4 * C].rearrange("p (h c) -> p h c", h=4), AF.Copy, [ps], [Qm[0]])
            S.op("pool", lambda e: e.tensor_tensor(FmT[0][:C, :, :C], QmT[0][:C, :, :C], identR(C), ALU.add), r=[QmT[0], dnc], w=[FmT[0]])
            for h in range(4):
                S.op("pool", lambda e: e.tensor_scalar(keg[:C, h, :], k_tok[:C, h, :], sm[:C, 4 + h:5 + h], None, ALU.mult), r=[k_tok, sm], w=[keg])
            yield
            for lv in range(nlev):
                if lv % 2 == 1:
                    yield
                a, b_ = lv % 2, (lv + 1) % 2
                lastlv = (lv == nlev - 1)
                if not lastlv:
                    for hh in range(2):
                        ps = gp()
                        for h2 in range(2):
                            h = hh * 2 + h2
                            if lv == 0:
                                mm(ps[:C, h2 * 256: h2 * 256 + 128], FmT[a][:C, h, :C], v_tok[:C, h, :], [FmT[a], v_tok], [ps])
                                mm(ps[:C, h2 * 256 + 128: h2 * 256 + 256], FmT[a][:C, h, :C], keg[:C, h, :], [FmT[a], keg], [ps])
                            else:
                                mm(ps[:C, h2 * 256:(h2 + 1) * 256], FmT[a][:C, h, :C], Xb[a][:C, h, :], [FmT[a], Xb[a]], [ps])
                        eng = "act" if hh == 0 else "dve"
                        if eng == "act":
                            act(Xb[b_][:C, hh * 2:hh * 2 + 2, :].rearrange("p h d -> p (h d)"), ps[:C, :], AF.Copy, [ps], [Xb[b_]])
                        else:
                            S.op("dve", lambda e: e.tensor_copy(Xb[b_][:C, hh * 2:hh * 2 + 2, :].rearrange("p h d -> p (h d)"), ps[:C, :]), r=[ps], w=[Xb[b_]])
                    ps1 = gp()
                    for h in range(4):
                        mm(ps1[:C, h * C:(h + 1) * C], QmT[a][:C, h, :C], Qm[a][:C, h, :C], [QmT[a], Qm[a]], [ps1])
                    ps2 = gp()
                    for h in range(4):
                        mm(ps2[:C, h * C:(h + 1) * C], Qm[a][:C, h, :C], QmT[a][:C, h, :C], [QmT[a], Qm[a]], [ps2])
                    act(Qm[b_][:C, :, :C], ps1[:C, :4 * C].rearrange("p (h c) -> p h c", h=4), AF.Copy, [ps1], [Qm[b_]])
                    S.op("dve", lambda e: e.tensor_copy(QmT[b_][:C, :, :C], ps2[:C, :4 * C].rearrange("p (h c) -> p h c", h=4)), r=[ps2], w=[QmT[b_]])
                    S.op("pool", lambda e: e.tensor_tensor(FmT[b_][:C, :, :C], QmT[b_][:C, :, :C], identR(C), ALU.add), r=[QmT[b_], dnc], w=[FmT[b_]])
                else:
                    psv = gp()
                    for h in range(4):
                        mm(psv[:C, h * 128:(h + 1) * 128], FmT[a][:C, h, :C], Xb[a][:C, h, 0:128], [FmT[a], Xb[a]], [psv])
                    psw = gp()
                    for h in range(4):
                        mm(psw[:, h * C:(h + 1) * C], Xb[a][:C, h, 128:256], FmT[a][:C, h, :C], [FmT[a], Xb[a]], [psw])
                    for h in range(4):
                        S.op("dve", lambda e: e.tensor_scalar(xvb[:C, h, :], psv[:C, h * 128:(h + 1) * 128], betag[:C, j, h:h + 1], None, ALU.mult), r=[psv, betag], w=[xvb])
                    act(xwT[:, :, :C], psw[:, :4 * C].rearrange("p (h c) -> p h c", h=4), AF.Copy, [psw], [xwT])
            yield
            psW = gp()
            for h in range(4):
                mm(psW[:C, h * 128:(h + 1) * 128], xwT[:, h, :C], Sst[:, h, :], [xwT, Sst], [psW])
            for h in range(4):
                S.op("dve", lambda e: e.scalar_tensor_tensor(v_new[:C, h, :], psW[:C, h * 128:(h + 1) * 128], sm[:C, 12 + h:13 + h], xvb[:C, h, :], ALU.mult, ALU.add), r=[psW, sm, xvb], w=[v_new])
            if owned:
                psO1 = gp()
                for h in range(4):
                    mm(psO1[:C, h * 128:(h + 1) * 128], qkv[:, h, tc_], Sst[:, h, :], [qkv, Sst], [psO1])
                for h in range(4):
                    act(o1s[:C, h, :], psO1[:C, h * 128:(h + 1) * 128], AF.Copy, [psO1, sm], [o1s], scale=sm[:C, 4 + h:5 + h])
                psO2 = gp()
                for h in range(4):
                    mm(psO2[:C, h * 128:(h + 1) * 128], qkt[:C, h, :C], v_new[:C, h, :], [qkt, v_new], [psO2])
                S.op("dve", lambda e: e.tensor_tensor(o_tok[:C].rearrange("p h d -> p (h d)"), o1s[:C].rearrange("p h d -> p (h d)"), psO2[:C, :], ALU.add), r=[o1s, psO2], w=[o_tok])
            for h in range(4):
                S.op("pool", lambda e: e.tensor_scalar(kdec[:C, h, :], k_tok[:C, h, :], sm[:C, 8 + h:9 + h], None, ALU.mult), r=[k_tok, sm], w=[kdec])
            for hh in range(2):
                psS = gp()
                for h2 in range(2):
                    h = hh * 2 + h2
                    mm(psS[:, h2 * 128:(h2 + 1) * 128], kdec[:C, h, :], v_new[:C, h, :], [kdec, v_new], [psS])
                for h2 in range(2):
                    h = hh * 2 + h2
                    S.op("dve", lambda e: e.scalar_tensor_tensor(Sst[:, h, :], Sst[:, h, :], sm[:, 16 + h:17 + h], psS[:, h2 * 128:(h2 + 1) * 128], ALU.mult, ALU.add), r=[Sst, sm, psS], w=[Sst])
            if sample:
                S.dma("sp", S_s[j], Sst[:], Sst, r=[Sst], w=[S_s])
            elif last and j == nch - 1:
                S.dma("sp", S_p[:], Sst[:], Sst, r=[Sst], w=[S_p])
            if owned:
                for h in range(4):
                    act(junk64[:C, :], o_tok[:C, h, :], AF.Square, [o_tok], [junk64, sm], accum_out=sm[:C, 28 + h:29 + h])
                act(sm[:C, 24:28], sm[:C, 28:32], AF.Sqrt, [sm], [sm], scale=1.0 / 128, bias=RMS_EPS)
                S.op("dve", lambda e: e.reciprocal(sm[:C, 24:28], sm[:C, 24:28]), r=[sm], w=[sm])
                for h in range(4):
                    S.op("pool", lambda e: e.tensor_scalar(o_tok[:C, h, :], o_tok[:C, h, :], sm[:C, 24 + h:25 + h], None, ALU.mult), r=[o_tok, sm], w=[o_tok])
                ps = gp()
                for h in range(4):
                    tr(ps[:, h * C:(h + 1) * C], o_tok[:C, h, :], ident[:C, :C], [o_tok, ident], [ps])
                S.op("dve", lambda e: e.scalar_tensor_tensor(o_dnT[:, :, tc_], ps[:, :4 * C].rearrange("p (h c) -> p h c", h=4), normw_sb[:, 0:1], zsT[:, :, tc_], ALU.mult, ALU.mult), r=[ps, normw_sb, zsT], w=[o_dnT])
        if not owned or stop == 'dn':
            return
        if debug and sample:
            S.op("pool", lambda e: e.tensor_copy(mrgA[:, 0:4, :128], o_dnT[:, :, :128]), r=[o_dnT], w=[mrgA])
            S.dma("sp", dbg["d_odn"][:], mrgA[:, 0:4, :128], mrgA, r=[mrgA], w=[dbg["d_odn"]])
        if stop == 'dbgodn':
            return
        scope('attn')
        if sample:
            for s_ in range(4):
                grp = [(h, h * 32, s_ * 32, 32) for h in range(8)]
                yield from attention_stream([(grp, bank_acc)], 256, s_, sample=True)
        else:
            for h in range(0, 8, 4):
                st2 = [([(h, 0, 0, W), (h + 1, W, 0, W)], bank_acc), ([(h + 2, 0, 0, W), (h + 3, W, 0, W)], bank_acc2)]
                yield from attention_stream(st2, 2 * W, ti_idx, sample=False)
        if debug and sample:
            S.op("pool", lambda e: e.tensor_copy(mrgA[:64, 0:8, :128], oT_sb[:, :, :128]), r=[oT_sb], w=[mrgA])
            S.dma("sp", dbg["d_osb"][:], mrgA[:64, 0:8, :128], mrgA, r=[mrgA], w=[dbg["d_osb"]])
        if stop in ('attn', 'attn1', 'attn2') or (stop or '').startswith('al'):
            return
        yield
        scope('merge')
        for half in range(2):
            for pc in range(2):
                if half == 0:
                    wu = wload(wsc_usb[:, :, pc * 512:(pc + 1) * 512], wsc_usb, npart=64, nk=8)
                else:
                    wu = wload(wsc_udn[:, :, pc * 512:(pc + 1) * 512], wsc_udn, npart=128, nk=4)
                wg = wload(wsc[:, :, OFF_G + half * 1024 + pc * 512: OFF_G + half * 1024 + (pc + 1) * 512], wsc)
                yield
                for c4 in range(4):
                    cc = pc * 4 + c4
                    psm_ = gp()
                    if half == 0:
                        for h in range(8):
                            mm(psm_[:, :W], wu[:64, h, c4 * 128:(c4 + 1) * 128], oT_sb[:, h, :W], [wu, oT_sb], [psm_], start=(h == 0), stop=(h == 7))
                    else:
                        for f in range(4):
                            mm(psm_[:, :W], wu[:, f, c4 * 128:(c4 + 1) * 128], o_dnT[:, f, :W], [wu, o_dnT], [psm_], start=(f == 0), stop=(f == 3))
                    psg = gp()
                    for kc in range(8):
                        mm(psg[:, :W], wg[:, kc, c4 * 128:(c4 + 1) * 128], xTb[:, kc, :W], [wg, xTb], [psg], start=(kc == 0), stop=(kc == 7))
                    g_ = gt[cc % 2]
                    act(g_[:, :W], psg[:, :W], AF.Sigmoid, [psg, bg_sb], [g_], bias=bg_sb[:, half * 8 + cc: half * 8 + cc + 1])
                    if half == 0:
                        S.op("dve", lambda e: e.tensor_tensor(mrgA[:, cc, :W], g_[:, :W], psm_[:, :W], ALU.mult), r=[g_, psm_], w=[mrgA])
                    else:
                        S.op("dve", lambda e: e.tensor_tensor(mtmp[:, :W], g_[:, :W], psm_[:, :W], ALU.mult), r=[g_, psm_], w=[mtmp])
                        S.op("pool", lambda e: e.tensor_tensor(mrgT[:, cc, :W], mtmp[:, :W], mrgA[:, cc, :W], ALU.add), r=[mtmp, mrgA], w=[mrgT])
        if debug and sample:
            S.dma("sp", dbg["d_mrg"][:], mrgA[:, 0:8, :128], mrgA, r=[mrgA], w=[dbg["d_mrg"]])
        if stop == 'merge':
            return
        yield
        scope('ln1')
        for pc in range(2):
            wo = wload(wsc_out[:, :, pc * 512:(pc + 1) * 512], wsc_out)
            for c4 in range(4):
                dmc = pc * 4 + c4
                ps = gp()
                for cc in range(8):
                    mm(ps[:, :W], wo[:, cc, c4 * 128:(c4 + 1) * 128], mrgT[:, cc, :W], [wo, mrgT], [ps], start=(cc == 0), stop=(cc == 7))
                S.op("dve", lambda e: e.scalar_tensor_tensor(pre[:, dmc, :W], xTf[:, dmc, :W], ALPHA, ps[:, :W], ALU.mult, ALU.add), r=[xTf, ps], w=[pre])
        pss = bank_acc
        for dmc in range(8):
            mm(pss[:, :W], ones_f[:, :], pre[:, dmc, :W], [ones_f, pre], [pss], start=(dmc == 0), stop=(dmc == 7))
        S.op("dve", lambda e: e.tensor_scalar(mean[:, :W], pss[:, :W], 1.0 / D, None, ALU.mult), r=[pss], w=[mean])
        for dmc in range(8):
            S.op("pool", lambda e: e.tensor_tensor(pre[:, dmc, :W], pre[:, dmc, :W], mean[:, :W], ALU.subtract), r=[pre, mean], w=[pre])
        psq = bank_acc2
        for dmc in range(8):
            S.op("pool", lambda e: e.tensor_tensor(sqb[:, :W], pre[:, dmc, :W], pre[:, dmc, :W], ALU.mult), r=[pre], w=[sqb])
            mm(psq[:, :W], ones_f[:, :], sqb[:, :W], [ones_f, sqb], [psq], start=(dmc == 0), stop=(dmc == 7))
        act(rstd[:, :W], psq[:, :W], AF.Sqrt, [psq], [rstd], scale=1.0 / D, bias=LN_EPS)
        S.op("dve", lambda e: e.reciprocal(rstd[:, :W], rstd[:, :W]), r=[rstd], w=[rstd])
        for dmc in range(8):
            S.op("dve", lambda e: e.tensor_tensor(pre[:, dmc, :W], pre[:, dmc, :W], rstd[:, :W], ALU.mult), r=[pre, rstd], w=[pre])
            S.op("dve", lambda e: e.tensor_scalar(hTf[:, dmc, :W], pre[:, dmc, :W], l1g[:, dmc:dmc + 1], l1b[:, dmc:dmc + 1], ALU.mult, ALU.add), r=[pre, l1g, l1b], w=[hTf])
        S.op("pool", lambda e: e.tensor_copy(hTb[:, :, :W], hTf[:, :, :W]), r=[hTf], w=[hTb])
        if debug and sample:
            S.dma("sp", dbg["d_h"][:], hTf[:, :, :128], hTf, r=[hTf], w=[dbg["d_h"]])
        if stop == 'ln1':
            return
        yield
        scope('peerq')
        for pc in range(4):
            yield
            wq_ = wload(wsc_pq[:, :, pc * 512:(pc + 1) * 512], wsc_pq)
            for c4 in range(4):
                cq = pc * 4 + c4
                ps = gp()
                for kc in range(8):
                    mm(ps[:, :W], wq_[:, kc, c4 * 128:(c4 + 1) * 128], hTb[:, kc, :W], [wq_, hTb], [ps], start=(kc == 0), stop=(kc == 7))
                act(qpT[:, cq, :W], ps[:, :W], AF.Copy, [ps], [qpT])
        for a in range(W // 128):
            ta = slice(a * 128, (a + 1) * 128)
            eidi, gate, h_tok = eidi_l[a], gate_l[a], h_tok_l[a]
            yield
            for g4 in range(4):
                ps = gp()
                for q4 in range(4):
                    hp = g4 * 4 + q4
                    mm(ps[:, q4 * 128:(q4 + 1) * 128], qpT[:, hp, ta], keys_b[:, hp, :], [qpT, keys_b], [ps])
                act(sc[:, :, :].rearrange("p a k -> p (a k)"), ps[:, :], AF.Copy, [ps], [sc])
                for q4 in range(4):
                    hp = g4 * 4 + q4
                    S.op("dve", lambda e: e.max(tv[:, hp, 0:8], sc[:, q4, :]), r=[sc], w=[tv])
                    S.op("dve", lambda e: e.max_index(ti[:, hp, 0:8], tv[:, hp, 0:8], sc[:, q4, :]), r=[sc, tv], w=[ti])
                    S.op("dve", lambda e: e.match_replace(scw[:, :], tv[:, hp, 0:8], sc[:, q4, :], -1e30), r=[sc, tv], w=[scw])
                    S.op("dve", lambda e: e.max(tv[:, hp, 8:16], scw[:, :]), r=[scw], w=[tv])
                    S.op("dve", lambda e: e.max_index(ti[:, hp, 8:16], tv[:, hp, 8:16], scw[:, :]), r=[scw, tv], w=[ti])
            S.op("dve", lambda e: e.tensor_copy(tif[:], ti[:]), r=[ti], w=[tif])
            for h in range(8):
                S.op("dve", lambda e: e.tensor_tensor(cand[:], tv[:, 2 * h, :].unsqueeze(2).broadcast_to([128, 16, 16]),
                                                      tv[:, 2 * h + 1, :].unsqueeze(1).broadcast_to([128, 16, 16]), ALU.add), r=[tv], w=[cand])
                S.op("dve", lambda e: e.scalar_tensor_tensor(cidx[:], tif[:, 2 * h, :].unsqueeze(2).broadcast_to([128, 16, 16]), 128.0,
                                                             tif[:, 2 * h + 1, :].unsqueeze(1).broadcast_to([128, 16, 16]), ALU.mult, ALU.add), r=[tif], w=[cidx])
                cf = cand[:].rearrange("p a b -> p (a b)")
                xf = cidx[:].rearrange("p a b -> p (a b)")
                S.op("dve", lambda e: e.max(tsv[:, h, 0:8], cf), r=[cand], w=[tsv])
                S.op("dve", lambda e: e.match_replace(candw[:, :], tsv[:, h, 0:8], cf, -1e30), r=[cand, tsv], w=[candw])
                S.op("dve", lambda e: e.max(tsv[:, h, 8:16], candw[:, :]), r=[candw], w=[tsv])
                for k in range(16):
                    S.op("dve", lambda e: e.scalar_tensor_tensor(junkp[:, :], cf, tsv[:, h, k:k + 1], xf, ALU.is_equal, ALU.mult,
                                                                 accum_out=eidf[:, h * 16 + k: h * 16 + k + 1]), r=[cand, cidx, tsv], w=[junkp, eidf])
                S.op("dve", lambda e: e.tensor_scalar(psm[:, h:h + 1], tsv[:, h, 0:1], -1.0, None, ALU.mult), r=[tsv], w=[psm])
                act(gate[:, h, :], tsv[:, h, :], AF.Exp, [tsv, psm], [gate, psm], bias=psm[:, h:h + 1], accum_out=psm[:, 8 + h:9 + h])
            S.op("dve", lambda e: e.reciprocal(psm[:, 16:24], psm[:, 8:16]), r=[psm], w=[psm])
            for h in range(8):
                S.op("dve", lambda e: e.tensor_scalar(gate[:, h, :], gate[:, h, :], psm[:, 16 + h:17 + h], None, ALU.mult), r=[gate, psm], w=[gate])
            S.op("dve", lambda e: e.tensor_scalar(eidf[:], eidf[:], 16383.0, None, ALU.min), r=[eidf], w=[eidf])
            S.op("dve", lambda e: e.tensor_copy(eidi[:], eidf[:]), r=[eidf], w=[eidi])
            scope('peer_htok')
            for hh in range(2):
                ps = gp()
                for k4 in range(4):
                    kc = hh * 4 + k4
                    tr(ps[:, k4 * 128:(k4 + 1) * 128], hTf[:, kc, ta], ident[:, :], [hTf, ident], [ps])
                act(h_tok[:, hh * 512:(hh + 1) * 512], ps[:, :], AF.Copy, [ps], [h_tok])
        pending.append(peer_gather(W, out_row0, y_out, sample))

    def peer_gather(W, out_row0, y_out, sample):
        y_buf = y_out
        for a in range(W // 128):
            eidi, gate, h_tok = eidi_l[a], gate_l[a], h_tok_l[a]
            psm = psm2
            scope('peer_u')
            for s_ in range(128):
                ub = ubuf[s_ % 3]
                S.dma("pool", ub[:], peer_u[:, :], ub, r=[peer_u, eidi], w=[ub],
                      indirect=dict(out_offset=None, in_offset=bass.IndirectOffsetOnAxis(ap=eidi[:, s_:s_ + 1], axis=0)))
                S.op("dve", lambda e: e.scalar_tensor_tensor(junku[:], ub[:], 1.0, h_tok[:], ALU.mult, ALU.mult, accum_out=actv[:, s_:s_ + 1]), r=[ub, h_tok], w=[junku, actv])
                if s_ % 4 == 3:
                    yield
            scope('peer_v')
            act(coef[:], actv[:], AF.Gelu, [actv], [coef])
            S.op("dve", lambda e: e.tensor_tensor(coef[:], coef[:], gate[:].rearrange("p h k -> p (h k)"), ALU.mult), r=[coef, gate], w=[coef])
            for s_ in range(128):
                ub = ubuf[s_ % 3]
                S.dma("pool", ub[:], peer_v[:, :], ub, r=[peer_v, eidi], w=[ub],
                      indirect=dict(out_offset=None, in_offset=bass.IndirectOffsetOnAxis(ap=eidi[:, s_:s_ + 1], axis=0)))
                if s_ == 0:
                    S.op("dve", lambda e: e.tensor_scalar(facc[:], ub[:], coef[:, 0:1], None, ALU.mult), r=[ub, coef], w=[facc])
                else:
                    S.op("dve", lambda e: e.scalar_tensor_tensor(facc[:], ub[:], coef[:, s_:s_ + 1], facc[:], ALU.mult, ALU.add), r=[ub, coef, facc], w=[facc])
                if s_ % 4 == 3:
                    yield
            if debug and sample:
                S.dma("sp", dbg["d_ffn"][:], facc[:], facc, r=[facc], w=[dbg["d_ffn"]])
            scope('ln2')
            S.op("dve", lambda e: e.scalar_tensor_tensor(facc[:], h_tok[:], ALPHA, facc[:], ALU.mult, ALU.add), r=[h_tok, facc], w=[facc])
            act(junku[:], facc[:], AF.Copy, [facc], [junku, psm], accum_out=psm[:, 0:1])
            S.op("dve", lambda e: e.tensor_scalar(psm[:, 0:1], psm[:, 0:1], -1.0 / D, None, ALU.mult), r=[psm], w=[psm])
            S.op("dve", lambda e: e.tensor_scalar(facc[:], facc[:], psm[:, 0:1], None, ALU.add), r=[facc, psm], w=[facc])
            act(junku[:], facc[:], AF.Square, [facc], [junku, psm], accum_out=psm[:, 1:2])
            act(psm[:, 2:3], psm[:, 1:2], AF.Sqrt, [psm], [psm], scale=1.0 / D, bias=LN_EPS)
            S.op("dve", lambda e: e.reciprocal(psm[:, 2:3], psm[:, 2:3]), r=[psm], w=[psm])
            S.op("dve", lambda e: e.scalar_tensor_tensor(ybuf[:], facc[:], psm[:, 2:3], l2g[:], ALU.mult, ALU.mult), r=[facc, psm, l2g], w=[ybuf])
            S.op("pool", lambda e: e.tensor_tensor(ybuf[:], ybuf[:], l2b[:], ALU.add), r=[ybuf, l2b], w=[ybuf])
            S.dma("sp", y_out[out_row0 + a * 128: out_row0 + (a + 1) * 128, :], ybuf[:], ybuf, r=[ybuf], w=[y_buf])

    def attention_stream(grp, ZW, idx, sample):
        blocks = []
        loaders = []
        if sample:
            s_ = idx
            blocks.append(dict(ktbuf=KTcur, kt=(lambda cc: KTcur[:, cc, s_ * 32:(s_ + 1) * 32]), vbuf=Vcur,
                               v=(lambda h: Vcur[:32, s_, h * 64:(h + 1) * 64]), nk=32, mask=(mask_s[:32, :256], mask_s), load=None))
            for n_, g8 in enumerate(range(7, -1, -1)):
                i2 = n_ % 2

                def load(i2=i2, g8=g8):
                    for hh in range(2):
                        S.dma("sp", kvstg[0][:64, :].rearrange("p (c k) -> p c k", c=4), kTc[s_, :, hh * 4:(hh + 1) * 4, g8 * 256:(g8 + 1) * 256], kvstg[0], r=[kTc], w=[kvstg[0]])
                        S.op("pool", lambda e: e.tensor_copy(KTblk[i2][:, hh * 4:(hh + 1) * 4, :].rearrange("p c k -> p (c k)"), kvstg[0][:64, :]), r=[kvstg[0]], w=[KTblk[i2]])
                    S.dma("act", kvstg[1][:, :].rearrange("p (b c) -> p b c", b=2), vc[s_, g8 * 256:(g8 + 1) * 256, :].rearrange("(b p) c -> p b c", p=128), kvstg[1], r=[vc], w=[kvstg[1]])
                    S.op("dve", lambda e: e.tensor_copy(Vblk[i2][:].rearrange("p b c -> p (b c)"), kvstg[1][:, :]), r=[kvstg[1]], w=[Vblk[i2]])
                for b4 in range(1, -1, -1):
                    blocks.append(dict(ktbuf=KTblk[i2], kt=(lambda cc, i2=i2, b4=b4: KTblk[i2][:, cc, b4 * 128:(b4 + 1) * 128]), vbuf=Vblk[i2],
                                       v=(lambda h, i2=i2, b4=b4: Vblk[i2][:, b4, h * 64:(h + 1) * 64]), nk=128, mask=None,
                                       load=(load if b4 == 1 else None)))
        else:
            i = idx
            nsub = WP // 128
            for o in range(nsub - 1, -1, -1):
                blocks.append(dict(ktbuf=KTcur, kt=(lambda cc, o=o: KTcur[:, cc, o * 128:(o + 1) * 128]), vbuf=Vcur,
                                   v=(lambda h, o=o: Vcur[:, o, h * 64:(h + 1) * 64]), nk=128, mask=(mask_p[:, o, :], mask_p), load=None))
            n_ = 0
            pt = i - 1
            while pt >= 0:
                i2 = n_ % 2
                n_ += 1
                tiles_ = [pt]

                def load(i2=i2, tiles_=tiles_):
                    for u_, t_ in enumerate(tiles_):
                        S.dma("sp", KTblk[i2][:, :, u_ * WP:(u_ + 1) * WP], KTs[t_][:, :, :], KTblk[i2], r=[KTs[t_]], w=[KTblk[i2]])
                        S.dma("act", Vblk[i2][:, u_ * nsub:(u_ + 1) * nsub, :], Vs[t_][:, :, :], Vblk[i2], r=[Vs[t_]], w=[Vblk[i2]])
                firstb = True
                for u_, t_ in enumerate(tiles_):
                    for o in range(nsub - 1, -1, -1):
                        blocks.append(dict(ktbuf=KTblk[i2], kt=(lambda cc, i2=i2, u_=u_, o=o: KTblk[i2][:, cc, u_ * WP + o * 128: u_ * WP + (o + 1) * 128]),
                                           vbuf=Vblk[i2], v=(lambda h, i2=i2, u_=u_, o=o: Vblk[i2][:, u_ * nsub + o, h * 64:(h + 1) * 64]),
                                           nk=128, mask=None, load=(load if firstb else None), kvalid=(t_ if t_ < 3 else None)))
                        firstb = False
                pt -= 1
        if stop == 'attn1' or (stop or '').startswith('al'):
            blocks = blocks[:1]
        if stop == 'attn2':
            blocks = blocks[:3]
        yield from attention_run(grp, ZW, blocks)

    def attention_run(streams, ZW, blocks):
        nb_ = len(blocks)
        loadable = [i for i, b_ in enumerate(blocks) if b_.get("load") is not None]
        if loadable:
            blocks[loadable[0]]["load"]()
        for bi, blk in enumerate(blocks):
            if bi % 2 == 0:
                yield
            if blk.get("load") is not None:
                nxt = [i for i in loadable if i > bi]
                if nxt:
                    blocks[nxt[0]]["load"]()
            nk = blk["nk"]
            zps = []
            for si, (groups, po) in enumerate(streams):
                zp = gp()
                zps.append(zp)
                for (h, zc, qc, Wg) in groups:
                    mm(zp[:nk, zc:zc + Wg], blk["kt"](h), qTb[:, h, qc:qc + Wg], [blk["ktbuf"], qTb], [zp])
            for si, (groups, po) in enumerate(streams):
                act(Eb[si][:nk, :ZW], zps[si][:nk, :ZW], AF.Exp, [zps[si]], [Eb[si]], scale=0.125)
            for si, (groups, po) in enumerate(streams):
                if blk["mask"] is not None:
                    mk, mkb = blk["mask"]
                    mw = mk.shape[-1]
                    for c0 in range(0, ZW, mw):
                        S.op("pool", lambda e: e.tensor_tensor(Eb[si][:nk, c0:c0 + mw], Eb[si][:nk, c0:c0 + mw], mk, ALU.mult), r=[Eb[si], mkb], w=[Eb[si]])
                if blk.get("kvalid") is not None:
                    kvc = blk["kvalid"]
                    S.op("pool", lambda e: e.tensor_scalar(Eb[si][:nk, :ZW], Eb[si][:nk, :ZW], kval[:nk, kvc:kvc + 1], None, ALU.mult), r=[Eb[si], kval], w=[Eb[si]])
            for si, (groups, po) in enumerate(streams):
                act(Pb[si][:nk, :ZW], Eb[si][:nk, :ZW], AF.Ln, [Eb[si]], [Pb[si]], bias=1.0)
            cps = []
            for si, (groups, po) in enumerate(streams):
                cp = gp()
                cps.append(cp)
                mm(cp[:nk, :ZW], tri_b[:nk, :nk], Pb[si][:nk, :ZW], [tri_b, Pb[si]], [cp], start=True, stop=(bi == 0))
                if bi > 0:
                    mm(cp[:nk, :ZW], ones_b[:128, :nk], Pacc_l[si][:128, :ZW], [ones_b, Pacc_l[si]], [cp], start=False, stop=True)
            for si, (groups, po) in enumerate(streams):
                act(Gb[si][:nk, :ZW], cps[si][:nk, :ZW], AF.Exp, [cps[si]], [Gb[si]], scale=-1.0)
            for si, (groups, po) in enumerate(streams):
                S.op("dve", lambda e: e.tensor_tensor(wTb[si][:nk, :ZW], Eb[si][:nk, :ZW], Gb[si][:nk, :ZW], ALU.mult), r=[Eb[si], Gb[si]], w=[wTb[si]])
            for si, (groups, po) in enumerate(streams):
                Pa = Pacc_l[si]
                if bi == 0:
                    if nk < 128:
                        S.op("pool", lambda e: e.memset(Pa[:, :ZW], 0.0), r=[], w=[Pa])
                    S.op("pool", lambda e: e.tensor_copy(Pa[:nk, :ZW], Pb[si][:nk, :ZW]), r=[Pb[si]], w=[Pa])
                elif bi < nb_ - 1:
                    S.op("pool", lambda e: e.tensor_tensor(Pa[:nk, :ZW], Pa[:nk, :ZW], Pb[si][:nk, :ZW], ALU.add), r=[Pb[si], Pa], w=[Pa])
            for si, (groups, po) in enumerate(streams):
                for gi_, (h, zc, qc, Wg) in enumerate(groups):
                    S.op("pe", lambda e: e.matmul(po[:64, zc:zc + Wg], blk["v"](h), wTb[si][:nk, zc:zc + Wg], start=(bi == 0 and gi_ == 0), stop=(bi == nb_ - 1),
                                                  skip_group_check=True), r=[blk["vbuf"], wTb[si]], w=[po])
        for si, (groups, po) in enumerate(streams):
            for (h, zc, qc, Wg) in groups:
                S.op("dve", lambda e: e.tensor_copy(oT_sb[:, h, qc:qc + Wg], po[:64, zc:zc + Wg]), r=[po], w=[oT_sb])

    W_ = 128
    if do_sample:
        xsv = xT_s[:].rearrange("(k p) t -> p k t", p=128)
        tile(128, 4, 32, 32, (lambda hf: xsv[:, hf * 4:(hf + 1) * 4, :]), xT_s, True, True, True, None, None, 0, True, 0)
    if do_prompt:
        xpv = xT_p[:].rearrange("(k p) t -> p k t", p=128)
        for p in range(n_ptiles):
            tile(WP, 1, WP, 64, (lambda hf, p=p: xpv[:, hf * 4:(hf + 1) * 4, p * WP:(p + 1) * WP]), xT_p, (p % 4 == 3), (p == 0), (p == n_ptiles - 1),
                 None, None, (p // 4) * WP, False, p)
    while pending:
        for pg in list(pending):
            try:
                next(pg)
            except StopIteration:
                pending.remove(pg)
    S.finish(list(dout.values()))
    return nc, S


def _shared_inputs(w_in, b_gate, w_conv, a_log, dt_bias, dn_norm_w, w_up_sb, w_up_dn, w_out, ln1_g, ln1_b,
                   peer_wq, peer_keys, peer_u, peer_v, ln2_g, ln2_b):
    c = make_consts()
    f = np.ascontiguousarray
    d = dict(c)
    d["w_in"] = f(w_in[0])
    d["wconvT"] = f(w_conv[0].reshape(4, 12, 128).transpose(2, 1, 0))
    d["bgT"] = f(b_gate[0].reshape(16, 128).T)
    d["alog"] = f(np.broadcast_to(a_log[0][None, :], (128, 4)))
    d["dtb"] = f(np.broadcast_to(dt_bias[0][None, :], (128, 4)))
    d["normw"] = f(dn_norm_w[0].reshape(128, 1))
    d["w_up_sb"] = f(w_up_sb[0].reshape(8, 64, D).transpose(1, 0, 2))
    d["w_up_dn"] = f(w_up_dn[0].reshape(4, 128, D).transpose(1, 0, 2))
    d["w_out"] = f(w_out[0].reshape(8, 128, D).transpose(1, 0, 2))
    d["ln1g"] = f(ln1_g[0].reshape(8, 128).T)
    d["ln1b"] = f(ln1_b[0].reshape(8, 128).T)
    d["peer_wq"] = f(peer_wq[0].reshape(8, 128, 2048).transpose(1, 0, 2))
    d["keysT"] = f(peer_keys[0].reshape(16, 128, 128).transpose(2, 0, 1))
    d["peer_u"] = f(peer_u[0])
    d["peer_v"] = f(peer_v[0])
    d["ln2g"] = f(np.broadcast_to(ln2_g[0][None, :], (128, D)))
    d["ln2b"] = f(np.broadcast_to(ln2_b[0][None, :], (128, D)))
    return d


def _sample_inputs(c, x_sample, cache_sb_k, cache_sb_v, state_dn_ssm, state_dn_conv):
    f = np.ascontiguousarray
    sl = slice(4 * c, 4 * c + 4)
    d = {}
    d["xT_s"] = f(x_sample[sl].reshape(128, D).T)
    d["kTc"] = f(cache_sb_k[0, sl].transpose(0, 3, 2, 1))
    d["vc"] = f(cache_sb_v[0, sl].reshape(4, PAST, 512))
    d["S0"] = f(state_dn_ssm[0, sl].transpose(0, 2, 1, 3))
    d["convT"] = f(state_dn_conv[0, sl].reshape(4, 3, 12, 128).transpose(3, 2, 0, 1))
    return d


_PROG = {}
STOP = None


def kernel(x_prompt, x_sample, cache_sb_k, cache_sb_v, state_dn_ssm, state_dn_conv,
           w_in, b_gate, w_conv, a_log, dt_bias, dn_norm_w, w_up_sb, w_up_dn, w_out,
           ln1_g, ln1_b, peer_wq, peer_keys, peer_u, peer_v, ln2_g, ln2_b):
    args = [np.asarray(a, dtype=np.float32) for a in (x_prompt, x_sample, cache_sb_k, cache_sb_v, state_dn_ssm, state_dn_conv,
            w_in, b_gate, w_conv, a_log, dt_bias, dn_norm_w, w_up_sb, w_up_dn, w_out,
            ln1_g, ln1_b, peer_wq, peer_keys, peer_u, peer_v, ln2_g, ln2_b)]
    (x_prompt, x_sample, cache_sb_k, cache_sb_v, state_dn_ssm, state_dn_conv,
     w_in, b_gate, w_conv, a_log, dt_bias, dn_norm_w, w_up_sb, w_up_dn, w_out,
     ln1_g, ln1_b, peer_wq, peer_keys, peer_u, peer_v, ln2_g, ln2_b) = args
    if "nc" not in _PROG:
        _PROG["nc"] = build_program(stop=STOP)[0]
    nc = _PROG["nc"]
    shared = _shared_inputs(w_in, b_gate, w_conv, a_log, dt_bias, dn_norm_w, w_up_sb, w_up_dn, w_out, ln1_g, ln1_b,
                            peer_wq, peer_keys, peer_u, peer_v, ln2_g, ln2_b)
    in_maps = []
    for c in range(NCORE):
        b, r = c // 4, c % 4
        d = dict(shared)
        d.update(_sample_inputs(c, x_sample, cache_sb_k, cache_sb_v, state_dn_ssm, state_dn_conv))
        sh = (3 - r) * WP
        xt = np.zeros((D, SEQ), np.float32)
        xt[:, sh:] = x_prompt[b, :SEQ - sh].T
        d["xT_p"] = xt
        kv = np.ones((128, 4), np.float32)
        kv[:, :3 - r] = 0.0
        d["kval"] = kv
        if STOP == 'dn':
            d["peer_u"] = d["peer_u"][:8]
            d["peer_v"] = d["peer_v"][:8]
        in_maps.append(d)
    res = run_bass_kernel_spmd(nc, in_maps, core_ids=list(range(NCORE))).results
    B = 2
    y_p = np.zeros((B, SEQ, D), np.float32)
    kn_p = np.zeros((1, B, SEQ, 8, 64), np.float32)
    vn_p = np.zeros((1, B, SEQ, 8, 64), np.float32)
    y_s = np.zeros((32, 32, D), np.float32)
    kn_s = np.zeros((1, 32, 32, 8, 64), np.float32)
    vn_s = np.zeros((1, 32, 32, 8, 64), np.float32)
    S_p = np.zeros((1, B, 4, 128, 128), np.float32)
    S_s = np.zeros((1, 32, 4, 128, 128), np.float32)
    cv_p = np.zeros((1, B, 3, 1536), np.float32)
    cv_s = np.zeros((1, 32, 3, 1536), np.float32)
    for c in range(NCORE):
        b, r = c // 4, c % 4
        o = res[c]
        for m in range(NT_FULL // 4):
            t0 = (4 * m + r) * WP
            y_p[b, t0:t0 + WP] = o["y_p"][m * WP:(m + 1) * WP]
            kn_p[0, b, t0:t0 + WP] = o["kn_p"][m * WP:(m + 1) * WP].reshape(WP, 8, 64)
            vn_p[0, b, t0:t0 + WP] = o["vn_p"][m * WP:(m + 1) * WP].reshape(WP, 8, 64)
        if r == 3:
            S_p[0, b] = o["S_p"].transpose(1, 0, 2)
            cv_p[0, b] = o["cv_p"]
        y_s[4 * c:4 * c + 4] = o["y_s"].reshape(4, 32, D)
        kn_s[0, 4 * c:4 * c + 4] = o["kn_s"].reshape(4, 32, 8, 64)
        vn_s[0, 4 * c:4 * c + 4] = o["vn_s"].reshape(4, 32, 8, 64)
        S_s[0, 4 * c:4 * c + 4] = o["S_s"].transpose(0, 2, 1, 3)
        cv_s[0, 4 * c:4 * c + 4] = o["cv_s"]
    return (y_p, y_s, kn_p, vn_p, kn_s, vn_s, S_p, S_s, cv_p, cv_s)
```

```python
import numpy as np
import ml_dtypes
import concourse.bass as bass
import concourse.mybir as mybir
from concourse.bass_utils import run_bass_kernel_spmd

F32 = mybir.dt.float32
BF16 = mybir.dt.bfloat16
I32 = mybir.dt.int32
U32 = mybir.dt.uint32
AF = mybir.ActivationFunctionType
ALU = mybir.AluOpType

D = 1024
SEQ = 16384
NCORE = 8
WP = 256
NT_FULL = SEQ // WP
PAST = 2048
OFF_DN = 1536
OFF_Z = 3072
OFF_B = 3584
OFF_G = 3592
ALPHA = 2 ** 0.25
LN_EPS = 1e-5
RMS_EPS = 1e-6


class Buf:
    def __init__(self, S, t, name, dma=False):
        self.t = t
        self.name = name
        self.writers = {}
        self.readers = {}
        self.dsem = None
        self.dcnt = 0
        if dma:
            self.dsem = S.nc.alloc_semaphore("d_" + name)
            S.semh[("d", name)] = self.dsem
            S.dmabufs.append(self)

    def __getitem__(self, idx):
        return self.t[idx]


class Sched:
    def __init__(self, nc):
        self.nc = nc
        self.eng = {"pe": nc.tensor, "act": nc.scalar, "dve": nc.vector, "pool": nc.gpsimd, "sp": nc.sync}
        self.semh = {}
        self.cnt = {}
        self.seen = {}
        for k in self.eng:
            self.semh[k] = nc.alloc_semaphore("e_" + k)
            self.cnt[k] = 0
            self.seen[k] = {}
        self.nbuf = 0
        self.dmabufs = []
        self.gbufs = []
        self.n_ins = 0
        self.n_wait = 0

    def sb(self, shape, dtype, name=None, dma=False):
        self.nbuf += 1
        name = "s_" + (name or f"b{self.nbuf}")
        t = self.nc.alloc_sbuf_tensor(name, list(shape), dtype)
        return Buf(self, t, name, dma)

    def ps(self, shape, dtype=F32, name=None):
        self.nbuf += 1
        name = name or f"p{self.nbuf}"
        t = self.nc.alloc_psum_tensor(name, list(shape), dtype)
        return Buf(self, t, name, False)

    def dram(self, name, shape, dtype, kind="Internal"):
        t = self.nc.dram_tensor(name, list(shape), dtype, kind=kind)
        return Buf(self, t, name, False)

    def _wait(self, e, key, val):
        if self.seen[e].get(key, 0) >= val:
            return
        self.eng[e].wait_ge(self.semh[key], val)
        self.seen[e][key] = val
        self.n_wait += 1

    def _deps(self, e, r, w):
        for b in r:
            for (k, v) in b.writers.items():
                if not (k == "pe" and e == "pe"):
                    self._wait(e, k, v)
        for b in w:
            for (k, v) in b.writers.items():
                if not (k == "pe" and e == "pe"):
                    self._wait(e, k, v)
            for (k, v) in b.readers.items():
                if not (k == "pe" and e == "pe"):
                    self._wait(e, k, v)

    def _mark(self, key, val, r, w):
        for b in r:
            if b not in w:
                b.readers[key] = val
        for b in w:
            b.writers[key] = val

    def op(self, e, fn, r=(), w=()):
        r = list(r)
        w = list(w)
        self._deps(e, r, w)
        ins = fn(self.eng[e])
        self.cnt[e] += 1
        ins.then_inc(self.semh[e], 1)
        self._mark(e, self.cnt[e], r, w)
        self.n_ins += 1
        return ins

    def dma(self, e, out, in_, sbuf_buf, r=(), w=(), indirect=None, **kw):
        r = list(r)
        w = list(w)
        self._deps(e, r, w)
        b = sbuf_buf
        if e == "pool":
            if not hasattr(b, "gsem"):
                b.gsem = self.nc.alloc_semaphore("g_" + b.name)
                b.gcnt = 0
                self.semh[("g", b.name)] = b.gsem
                self.gbufs.append(b)
            key = ("g", b.name)
            if b.gcnt > 0:
                self._wait(e, key, 16 * b.gcnt)
            if indirect is None:
                ins = self.eng[e].dma_start(out=out, in_=in_, **kw)
            else:
                ins = self.eng[e].indirect_dma_start(out=out, in_=in_, **indirect)
            b.gcnt += 1
            ins.then_inc(b.gsem, 16)
            self._mark(key, 16 * b.gcnt, r, w)
            self.n_ins += 1
            return ins
        key = ("d", b.name)
        if b.dcnt > 0:
            self._wait(e, key, 16 * b.dcnt)
        ins = self.eng[e].dma_start(out=out, in_=in_, **kw)
        b.dcnt += 1
        ins.then_inc(b.dsem, 16)
        self._mark(key, 16 * b.dcnt, r, w)
        self.n_ins += 1
        return ins

    def finish(self, bufs):
        for b in bufs:
            for (k, v) in list(b.writers.items()) + list(b.readers.items()):
                self._wait("sp", k, v)
        for b in self.dmabufs:
            if b.dcnt > 0:
                self._wait("sp", ("d", b.name), 16 * b.dcnt)
        for b in self.gbufs:
            if b.gcnt > 0:
                self._wait("sp", ("g", b.name), 16 * b.gcnt)
        for k in ("pe", "act", "dve", "pool"):
            if self.cnt[k] > 0:
                self._wait("sp", k, self.cnt[k])


def make_consts():
    c = {}
    i = np.arange(128)
    c["ident"] = np.eye(128, dtype=np.float32)
    c["ones"] = np.ones((128, 128), np.float32)
    c["tri_b"] = (i[:, None] >= i[None, :]).astype(np.float32).astype(ml_dtypes.bfloat16)
    c["ones_b"] = np.ones((128, 128), np.float32).astype(ml_dtypes.bfloat16)
    t = np.arange(WP)
    mp = np.zeros((128, WP // 128, WP), np.float32)
    for o in range(WP // 128):
        mp[:, o, :] = ((128 * o + i[:, None]) < t[None, :])
    c["mask_p"] = mp
    j32 = np.arange(32)
    ms = np.zeros((128, 8, 32), np.float32)
    ms[:32] = (j32[:, None, None] < j32[None, None, :])
    c["mask_s"] = ms.reshape(128, 256)
    m = np.arange(64)
    dn = np.zeros((64, 6, 4, 64), np.float32)
    dn[:, 0] = (m[:, None] <= m[None, :])[:, None, :]
    dn[:, 1] = (m[:, None] > m[None, :])[:, None, :]
    dn[:, 2] = (m[None, :] > m[:, None])[:, None, :]
    dn[:, 3] = (m[None, :] >= m[:, None])[:, None, :]
    dn[:, 4] = np.eye(64)[:, None, :]
    dnp = np.zeros((128, 6 * 4 * 64), np.float32)
    dnp[:64] = dn.reshape(64, -1)
    c["dn"] = dnp
    return c


def build_program(n_ptiles=NT_FULL, do_sample=True, debug=False, stop=None):
    nc = bass.Bass("TRN2", target_bir_lowering=False)
    S = Sched(nc)
    EI = "ExternalInput"
    EO = "ExternalOutput"
    do_prompt = n_ptiles > 0
    n_own = (n_ptiles + 3) // 4 if do_prompt else 0
    din = {}

    def inp(name, shape, dt=F32):
        din[name] = S.dram(name, shape, dt, kind=EI)
        return din[name]

    dout = {}

    def outp(name, shape, dt=F32):
        dout[name] = S.dram(name, shape, dt, kind=EO)
        return dout[name]

    if do_prompt:
        xT_p = inp("xT_p", [D, n_ptiles * WP])
    if do_sample:
        xT_s = inp("xT_s", [D, 128])
        kTc = inp("kTc", [4, 64, 8, PAST])
        vc = inp("vc", [4, PAST, 512])
        S0in = inp("S0", [4, 128, 4, 128])
        convT = inp("convT", [128, 12, 4, 3])
    w_in = inp("w_in", [D, 5640])
    wconvT = inp("wconvT", [128, 12, 4])
    bgT = inp("bgT", [128, 16])
    alog = inp("alog", [128, 4])
    dtb = inp("dtb", [128, 4])
    normw = inp("normw", [128, 1])
    w_up_sb = inp("w_up_sb", [64, 8, D])
    w_up_dn = inp("w_up_dn", [128, 4, D])
    w_out = inp("w_out", [128, 8, D])
    ln1g = inp("ln1g", [128, 8])
    ln1b = inp("ln1b", [128, 8])
    peer_wq = inp("peer_wq", [128, 8, 2048])
    keysT = inp("keysT", [128, 16, 128])
    NEXP = 8 if stop == 'dn' else 16384
    peer_u = inp("peer_u", [NEXP, D])
    peer_v = inp("peer_v", [NEXP, D])
    ln2g = inp("ln2g", [128, D])
    ln2b = inp("ln2b", [128, D])
    c_ident = inp("ident", [128, 128])
    c_ones = inp("ones", [128, 128])
    c_tri_b = inp("tri_b", [128, 128], BF16)
    c_ones_b = inp("ones_b", [128, 128], BF16)
    c_mask_p = inp("mask_p", [128, WP // 128, WP])
    c_mask_s = inp("mask_s", [128, 256])
    c_dn = inp("dn", [128, 6 * 4 * 64])
    kval_in = inp("kval", [128, 4])

    if do_prompt:
        NOWN = n_own
        y_p = outp("y_p", [NOWN * WP, D])
        kn_p = outp("kn_p", [NOWN * WP, 512])
        vn_p = outp("vn_p", [NOWN * WP, 512])
        S_p = outp("S_p", [128, 4, 128])
        cv_p = outp("cv_p", [3, 1536])
    if do_sample:
        y_s = outp("y_s", [128, D])
        kn_s = outp("kn_s", [128, 512])
        vn_s = outp("vn_s", [128, 512])
        S_s = outp("S_s", [4, 128, 4, 128])
        cv_s = outp("cv_s", [4, 3, 1536])
    dbg = {}
    if debug:
        for nm, shp in [("d_osb", [64, 8, 128]), ("d_odn", [128, 4, 128]), ("d_h", [128, 8, 128]), ("d_ffn", [128, D]),
                        ("d_qkv", [128, 12, 128]), ("d_bg", [32, 4, 8]), ("d_mrg", [128, 8, 128])]:
            dbg[nm] = outp(nm, shp, F32)

    wsc = S.dram("wsc", [128, 8, 5640], BF16)
    wsc_pq = S.dram("wsc_pq", [128, 8, 2048], BF16)
    wsc_out = S.dram("wsc_out", [128, 8, D], BF16)
    wsc_udn = S.dram("wsc_udn", [128, 4, D], BF16)
    wsc_usb = S.dram("wsc_usb", [64, 8, D], BF16)
    if do_prompt:
        KTs = [S.dram(f"KTs{i}", [64, 8, WP], BF16) for i in range(n_ptiles)]
        Vs = [S.dram(f"Vs{i}", [128, WP // 128, 512], BF16) for i in range(n_ptiles)]

    ident = S.sb([128, 128], F32, "ident", dma=True)
    ones_f = S.sb([128, 128], F32, "ones_f", dma=True)
    tri_b = S.sb([128, 128], BF16, "tri_b", dma=True)
    ones_b = S.sb([128, 128], BF16, "ones_b", dma=True)
    mask_p = S.sb([128, WP // 128, WP], F32, "mask_p", dma=True)
    mask_s = S.sb([128, 256], F32, "mask_s", dma=True)
    dnc = S.sb([128, 6, 4, 64], F32, "dnc", dma=True)
    wcv = S.sb([128, 12, 4], F32, "wcv", dma=True)
    bg_sb = S.sb([128, 16], F32, "bg_sb", dma=True)
    nA = S.sb([128, 4], F32, "nA", dma=True)
    dtb_sb = S.sb([128, 4], F32, "dtb_sb", dma=True)
    normw_sb = S.sb([128, 1], F32, "normw_sb", dma=True)
    l1g = S.sb([128, 8], F32, "l1g", dma=True)
    l1b = S.sb([128, 8], F32, "l1b", dma=True)
    keys_b = S.sb([128, 16, 128], BF16, "keys_b")
    l2g = S.sb([128, D], F32, "l2g", dma=True)
    l2b = S.sb([128, D], F32, "l2b", dma=True)
    kval = S.sb([128, 4], F32, "kval", dma=True)
    S.dma("sp", kval[:], kval_in[:], kval, r=[kval_in], w=[kval])
    for (sbuf, src, q) in [(ident, c_ident, "sp"), (ones_f, c_ones, "act"), (tri_b, c_tri_b, "sp"), (ones_b, c_ones_b, "act"),
                           (mask_p, c_mask_p, "sp"), (mask_s, c_mask_s, "act"), (wcv, wconvT, "sp"), (bg_sb, bgT, "act"),
                           (nA, alog, "sp"), (dtb_sb, dtb, "act"), (normw_sb, normw, "sp"), (l1g, ln1g, "act"),
                           (l1b, ln1b, "sp"), (l2g, ln2g, "sp"), (l2b, ln2b, "act")]:
        S.dma(q, sbuf[:], src[:], sbuf, r=[src], w=[sbuf])
    S.dma("sp", dnc[:].rearrange("p a h c -> p (a h c)"), c_dn[:], dnc, r=[c_dn], w=[dnc])
    S.op("act", lambda e: e.activation(nA[:], nA[:], AF.Exp), r=[nA], w=[nA])
    S.op("dve", lambda e: e.tensor_scalar(nA[:], nA[:], -1.0, None, ALU.mult), r=[nA], w=[nA])

    banks = [S.ps([128, 512], F32, f"bank{i}") for i in range(8)]
    gp_state = [0]

    def gp():
        b = banks[gp_state[0] % 6]
        gp_state[0] += 1
        return b

    bank_acc = banks[6]
    bank_acc2 = banks[7]

    ubuf = [S.sb([128, D], F32, f"ubuf{i}", dma=True) for i in range(3)]
    stg = ubuf[0:2]
    stgb = [S.sb([128, 1024], BF16, f"stgb{i}", dma=True) for i in range(2)]
    cv_i = [0]

    def convert(src_ap, dst_ap, srcbuf, dstbuf, npart, n):
        i = cv_i[0] % 2
        cv_i[0] += 1
        q = "sp" if i == 0 else "act"
        S.dma(q, stg[i][:npart, :n], src_ap, stg[i], r=[srcbuf], w=[stg[i]])
        eng = "dve" if i == 0 else "pool"
        S.op(eng, lambda e: e.tensor_copy(stgb[i][:npart, :n], stg[i][:npart, :n]), r=[stg[i]], w=[stgb[i]])
        S.dma(q, dst_ap, stgb[i][:npart, :n], stgb[i], r=[stgb[i]], w=[dstbuf])

    w_in_v = w_in[:].rearrange("(k p) c -> p k c", p=128)
    for kc in range(8):
        for c0 in range(0, 5640, 1024):
            n = min(1024, 5640 - c0)
            convert(w_in_v[:, kc, c0:c0 + n], wsc[:, kc, c0:c0 + n], w_in, wsc, 128, n)
    for kc in range(8):
        for c0 in range(0, 2048, 1024):
            convert(peer_wq[:, kc, c0:c0 + 1024], wsc_pq[:, kc, c0:c0 + 1024], peer_wq, wsc_pq, 128, 1024)
    for kc in range(8):
        convert(w_out[:, kc, :], wsc_out[:, kc, :], w_out, wsc_out, 128, 1024)
        convert(w_up_sb[:, kc, :], wsc_usb[:, kc, :], w_up_sb, wsc_usb, 64, 1024)
    for kc in range(4):
        convert(w_up_dn[:, kc, :], wsc_udn[:, kc, :], w_up_dn, wsc_udn, 128, 1024)
    for hp0 in range(0, 16, 8):
        S.dma("sp", stg[0][:, :], keysT[:, hp0:hp0 + 8, :].rearrange("p a k -> p (a k)"), stg[0], r=[keysT], w=[stg[0]])
        S.op("dve", lambda e: e.tensor_copy(keys_b[:, hp0:hp0 + 8, :].rearrange("p a k -> p (a k)"), stg[0][:, :]), r=[stg[0]], w=[keys_b])

    Wba = S.sb([128, 8, 8], BF16, "Wba", dma=True)
    S.dma("sp", Wba[:], wsc[:, :, OFF_B:OFF_B + 8], Wba, r=[wsc], w=[Wba])
    wslots = [S.sb([128, 8, 512], BF16, f"wslot{i}", dma=True) for i in range(2)]
    ws_i = [0]

    def wload(src_ap, srcbuf, npart=128, nk=8):
        i = ws_i[0] % 2
        ws_i[0] += 1
        q = ["sp", "act"][i]
        S.dma(q, wslots[i][:npart, :nk, :], src_ap, wslots[i], r=[srcbuf], w=[wslots[i]])
        return wslots[i]

    WM = WP if do_prompt else 128
    xTf = S.sb([128, 8, WM], F32, "xTf", dma=True)
    xTb = S.sb([128, 8, WM], BF16, "xTb")
    KTcur = S.sb([64, 8, WM], BF16, "KTcur", dma=True)
    Vcur = S.sb([128, 4, 512], BF16, "Vcur", dma=True)
    kvf = [S.sb([128, 512], F32, f"kvf{i}", dma=True) for i in range(2)]
    xin_flat = [S.sb([128, WM + 12], F32, f"xin{i}", dma=True) for i in range(2)]
    hal = S.sb([128, 12, 3], F32, "hal", dma=True)
    qkv = S.sb([128, 12, WM], F32, "qkv", dma=True)
    cvo = S.sb([4, 512], F32, "cvo", dma=True)
    betag = S.sb([64, 4, 8], F32, "betag", dma=True)
    tmp48 = S.sb([64, 8], F32, "tmp48")
    Sst = S.sb([128, 4, 128], F32, "Sst", dma=True)
    qTb = S.sb([64, 8, WM], BF16, "qTb")
    zsT = S.sb([128, 4, WM], BF16, "zsT")
    o_dnT = S.sb([128, 4, WM], BF16, "o_dnT", dma=True)
    oT_sb = S.sb([64, 8, WM], BF16, "oT_sb", dma=True)
    k_tok = S.sb([64, 4, 128], F32, "k_tok")
    v_tok = S.sb([64, 4, 128], F32, "v_tok")
    keg = S.sb([64, 4, 128], F32, "keg")
    sm = S.sb([128, 32], F32, "sm")
    trig = S.sb([64, 4, 64], F32, "trig")
    decT = S.sb([64, 4, 64], F32, "decT")
    decTs = S.sb([64, 4, 64], F32, "decTs")
    qkt = S.sb([64, 4, 64], F32, "qkt")
    Qm = [S.sb([64, 4, 64], F32, f"Qm{i}") for i in range(2)]
    QmT = [S.sb([64, 4, 64], F32, f"QmT{i}") for i in range(2)]
    FmT = [S.sb([64, 4, 64], F32, f"FmT{i}") for i in range(2)]
    Xb = [S.sb([64, 4, 256], F32, f"Xb{i}") for i in range(2)]
    xvb = S.sb([64, 4, 128], F32, "xvb")
    kdec = xvb
    xwT = S.sb([128, 4, 64], F32, "xwT")
    v_new = S.sb([64, 4, 128], F32, "v_new")
    o1s = keg
    o_tok = S.sb([64, 4, 128], F32, "o_tok")
    junk64 = S.sb([64, 128], F32, "junk64")
    KTblk = [S.sb([64, 8, 256], BF16, f"KTblk{i}", dma=True) for i in range(2)]
    Vblk = [S.sb([128, 2, 512], BF16, f"Vblk{i}", dma=True) for i in range(2)]
    kvstg = [ubuf[0], ubuf[1]]
    ZM = 512 if do_prompt else 256
    Eb = [S.sb([128, ZM], F32, f"Eb{i}") for i in range(2)]
    Pb = [S.sb([128, ZM], BF16, f"Pb{i}") for i in range(2)]
    Gb = [S.sb([128, ZM], BF16, f"Gb{i}") for i in range(2)]
    wTb = [S.sb([128, ZM], BF16, f"wTb{i}") for i in range(2)]
    Pacc_l = [S.sb([128, ZM], BF16, f"Pacc{i}") for i in range(2)]
    WL = 8 if stop == 'dn' else WM
    gt = [S.sb([128, WM], F32, f"gt{i}") for i in range(2)]
    sqb, nrm = gt[0], gt[1]
    mtmp = S.sb([128, WL], F32, "mtmp")
    mrgA = qkv
    mrgT = S.sb([128, 8, WL], BF16, "mrgT")
    pre = mrgA
    hTf = xTf
    hTb = xTb
    mean = nrm
    rstd = S.sb([128, WL], F32, "rstd")
    qpT = S.sb([128, 16, WL], BF16, "qpT")
    sc = S.sb([128, 4, 128], F32, "sc")
    scw = S.sb([128, 128], F32, "scw")
    tv = S.sb([128, 16, 16], F32, "tv")
    ti = S.sb([128, 16, 16], U32, "ti")
    tif = S.sb([128, 16, 16], F32, "tif")
    cand = S.sb([128, 16, 16], F32, "cand")
    candw = S.sb([128, 256], F32, "candw")
    cidx = S.sb([128, 16, 16], F32, "cidx")
    tsv = S.sb([128, 8, 16], F32, "tsv")
    eidf = S.sb([128, 128], F32, "eidf")
    eidi_l = [S.sb([128, 128], I32, f"eidi{i}") for i in range(2)]
    gate_l = [S.sb([128, 8, 16], F32, f"gate{i}") for i in range(2)]
    psm = S.sb([128, 32], F32, "psm")
    junkp = S.sb([128, 256], F32, "junkp")
    h_tok_l = [S.sb([128, D], F32, f"h_tok{i}") for i in range(2)]
    psm2 = S.sb([128, 8], F32, "psm2")
    junku = stgb[0]
    actv = S.sb([128, 128], F32, "actv")
    coef = S.sb([128, 128], F32, "coef")
    facc = S.sb([128, D], F32, "facc", dma=True)
    ybuf = facc

    triT = lambda C: dnc[:C, 0, 0, :C]
    ustr = lambda C: dnc[:C, 1, 0, :C]
    maskS = lambda C: dnc[:C, 2, :, :C]
    maskI = lambda C: dnc[:C, 3, :, :C]
    identR = lambda C: dnc[:C, 4, :, :C]

    def mm(out, lhsT, rhs, r, w, start=True, stop=True):
        S.op("pe", lambda e: e.matmul(out, lhsT, rhs, start=start, stop=stop), r=r, w=w)

    def tr(out, in_, idn, r, w):
        S.op("pe", lambda e: e.transpose(out, in_, idn), r=r, w=w)

    def act(out, in_, func, r, w, **kw):
        S.op("act", lambda e: e.activation(out, in_, func, **kw), r=r, w=w)

    def proj_fm(wbuf, wcol0, ncc, evac):
        for cc in range(ncc):
            ps = gp()
            for kc in range(8):
                mm(ps[:, :W_], wbuf[:, kc, wcol0 + cc * 128: wcol0 + (cc + 1) * 128], xTb[:, kc, :W_], [wbuf, xTb], [ps], start=(kc == 0), stop=(kc == 7))
            evac(cc, ps)


    cur_scope = [None]

    def scope(name):
        return

    pending = []

    def tile(*a, **k):
        for _ in tile_(*a, **k):
            for pg in list(pending):
                try:
                    next(pg)
                except StopIteration:
                    pending.remove(pg)

    def tile_(W, nseq, Wseq, C, xsrc_ap, xsrc_buf, owned, first, last, halo_src, kv_past_blocks, out_row0, sample, ti_idx):
        nonlocal W_
        W_ = W
        y_out, kn_out, vn_out = (y_s, kn_s, vn_s) if sample else (y_p, kn_p, vn_p)
        y_buf, kn_buf, vn_buf = y_out, kn_out, vn_out
        nch = W // C
        if stop == 'pro':
            return
        scope('kvproj')
        S.dma("sp", xTf[:, 0:4, :W], xsrc_ap(0), xTf, r=[xsrc_buf], w=[xTf])
        S.dma("act", xTf[:, 4:8, :W], xsrc_ap(1), xTf, r=[xsrc_buf], w=[xTf])
        S.op("pool", lambda e: e.tensor_copy(xTb[:, :, :W], xTf[:, :, :W]), r=[xTf], w=[xTb])
        if stop == 'x':
            return
        def proj_heads(wbuf, dst):
            for h in range(8):
                ps = gp()
                for kc in range(8):
                    mm(ps[:64, :W], wbuf[:, kc, h * 64:(h + 1) * 64], xTb[:, kc, :W], [wbuf, xTb], [ps], start=(kc == 0), stop=(kc == 7))
                act(dst[:, h, :W], ps[:64, :W], AF.Copy, [ps], [dst])
        Wk = wload(wsc[:, :, 512:1024], wsc)
        proj_heads(Wk, KTcur)
        if not sample:
            S.dma("sp", KTs[ti_idx][:, :, :], KTcur[:, :, :W], KTcur, r=[KTcur], w=[KTs[ti_idx]])
        if stop == 'kt':
            return

        def tokmajor_out(wt, ob_, kb_, q_):
            for g in range(W // 128):
                ps = gp()
                for kc in range(8):
                    mm(ps[:, :], xTb[:, kc, g * 128:(g + 1) * 128], wt[:, kc, :], [xTb, wt], [ps], start=(kc == 0), stop=(kc == 7))
                act(kb_[:, :], ps[:, :], AF.Copy, [ps], [kb_])
                S.dma(q_, ob_[out_row0 + g * 128: out_row0 + (g + 1) * 128, :], kb_[:, :], kb_, r=[kb_], w=[ob_])
        if owned:
            tokmajor_out(Wk, kn_out, kvf[1], "act")
        Wv = wload(wsc[:, :, 1024:1536], wsc)
        gs = 32 if sample else 128
        ng = W // gs
        for g in range(ng):
            ps = gp()
            for kc in range(8):
                mm(ps[:gs, :], xTb[:, kc, g * gs:(g + 1) * gs], Wv[:, kc, :], [xTb, Wv], [ps], start=(kc == 0), stop=(kc == 7))
            S.op("dve", lambda e: e.tensor_copy(Vcur[:gs, g, :], ps[:gs, :]), r=[ps], w=[Vcur])
        if owned:
            tokmajor_out(Wv, vn_out, kvf[0], "sp")
        if not sample:
            S.dma("act", Vs[ti_idx][:, :, :], Vcur[:, :ng, :], Vcur, r=[Vcur], w=[Vs[ti_idx]])
        if stop in ('kv', 'kv1', 'kv2'):
            return
        yield
        scope('dnproj')
        if first and not sample:
            S.op("pool", lambda e: e.memset(hal[:], 0.0), r=[], w=[hal])
        for piece in range(3):
            wb = wload(wsc[:, :, OFF_DN + piece * 512: OFF_DN + (piece + 1) * 512], wsc)
            for c4 in range(4):
                cc = piece * 4 + c4
                xbuf_ = xin_flat[cc % 2]
                xb_ = xbuf_[:, :nseq * (Wseq + 3)].rearrange("p (s w) -> p s w", s=nseq)
                ps = gp()
                for kc in range(8):
                    mm(ps[:, :W], wb[:, kc, c4 * 128:(c4 + 1) * 128], xTb[:, kc, :W], [wb, xTb], [ps], start=(kc == 0), stop=(kc == 7))
                act(xb_[:, :nseq, 3:3 + Wseq], ps[:, :W].rearrange("p (s w) -> p s w", s=nseq), AF.Copy, [ps], [xbuf_])
                if sample:
                    S.dma("sp", xb_[:, :nseq, 0:3], convT[:, cc, :, :], xbuf_, r=[convT], w=[xbuf_])
                else:
                    S.op("pool", lambda e: e.tensor_copy(xb_[:, 0, 0:3], hal[:, cc, :]), r=[hal], w=[xbuf_])
                    S.op("pool", lambda e: e.tensor_copy(hal[:, cc, :], xb_[:, 0, Wseq:Wseq + 3]), r=[xbuf_], w=[hal])
                qv = qkv[:, cc, :W].rearrange("p (s w) -> p s w", s=nseq)
                S.op("dve", lambda e: e.tensor_scalar(qv, xb_[:, :nseq, 0:Wseq], wcv[:, cc, 0:1], None, ALU.mult), r=[xbuf_, wcv], w=[qkv])
                for i in range(1, 4):
                    S.op("dve", lambda e: e.scalar_tensor_tensor(qv, xb_[:, :nseq, i:i + Wseq], wcv[:, cc, i:i + 1], qv, ALU.mult, ALU.add), r=[xbuf_, wcv, qkv], w=[qkv])
                act(qkv[:, cc, :W], qkv[:, cc, :W], AF.Silu, [qkv], [qkv])
            yield
            if sample or last:
                for s_ in range(nseq):
                    ps = gp()
                    t1 = (s_ + 1) * Wseq
                    for kc in range(8):
                        mm(ps[:3, :], xTb[:, kc, t1 - 3:t1], wb[:, kc, :], [xTb, wb], [ps], start=(kc == 0), stop=(kc == 7))
                    S.op("dve", lambda e: e.tensor_copy(cvo[:3, :], ps[:3, :]), r=[ps], w=[cvo])
                    if sample:
                        S.dma("sp", cv_s[s_, :, piece * 512:(piece + 1) * 512], cvo[:3, :], cvo, r=[cvo], w=[cv_s])
                    else:
                        S.dma("sp", cv_p[:, piece * 512:(piece + 1) * 512], cvo[:3, :], cvo, r=[cvo], w=[cv_p])
        if stop == 'conv':
            return
        yield
        for cc in range(8):
            S.op("pool", lambda e: e.tensor_tensor(sqb[:, :W], qkv[:, cc, :W], qkv[:, cc, :W], ALU.mult), r=[qkv], w=[sqb])
            ps = gp()
            mm(ps[:, :W], ones_f[:, :], sqb[:, :W], [ones_f, sqb], [ps])
            act(nrm[:, :W], ps[:, :W], AF.Sqrt, [ps], [nrm], bias=RMS_EPS)
            S.op("dve", lambda e: e.reciprocal(nrm[:, :W], nrm[:, :W]), r=[nrm], w=[nrm])
            sc_ = (128 ** -0.5) if cc < 4 else 1.0
            S.op("dve", lambda e: e.scalar_tensor_tensor(qkv[:, cc, :W], qkv[:, cc, :W], sc_, nrm[:, :W], ALU.mult, ALU.mult), r=[qkv, nrm], w=[qkv])
        if debug and sample:
            S.dma("sp", dbg["d_qkv"][:], qkv[:, :, :128], qkv, r=[qkv], w=[dbg["d_qkv"]])
        for j in range(nch):
            ps = gp()
            for kc in range(8):
                mm(ps[:C, 0:8], xTb[:, kc, j * C:(j + 1) * C], Wba[:, kc, :], [xTb, Wba], [ps], start=(kc == 0), stop=(kc == 7))
            act(betag[:C, j, 0:4], ps[:C, 0:4], AF.Sigmoid, [ps], [betag])
            S.op("dve", lambda e: e.tensor_tensor(tmp48[:C, 0:4], ps[:C, 4:8], dtb_sb[:C, :], ALU.add), r=[ps, dtb_sb], w=[tmp48])
            act(tmp48[:C, 0:4], tmp48[:C, 0:4], AF.Exp, [tmp48], [tmp48])
            act(tmp48[:C, 0:4], tmp48[:C, 0:4], AF.Ln, [tmp48], [tmp48], bias=1.0)
            S.op("dve", lambda e: e.tensor_tensor(betag[:C, j, 4:8], tmp48[:C, 0:4], nA[:C, :], ALU.mult), r=[tmp48, nA], w=[betag])
        if debug and sample:
            S.dma("sp", dbg["d_bg"][:], betag[:32, :, :], betag, r=[betag], w=[dbg["d_bg"]])
        if stop == 'bg':
            return
        yield
        scope('qz')
        if owned:
            wb = wload(wsc[:, :, 0:512], wsc)
            proj_heads(wb, qTb)
            wb = wload(wsc[:, :, OFF_Z:OFF_Z + 512], wsc)
            def ev_z(cc, ps):
                act(zsT[:, cc, :W], ps[:, :W], AF.Silu, [ps], [zsT])
            proj_fm(wb, 0, 4, ev_z)
        scope('dnchunks')
        nlev = 6 if C == 64 else 5
        for j in range(nch):
            tc_ = slice(j * C, (j + 1) * C)
            if sample:
                S.dma("sp", Sst[:], S0in[j], Sst, r=[S0in], w=[Sst])
            elif first and j == 0:
                S.op("pool", lambda e: e.memset(Sst[:], 0.0), r=[], w=[Sst])
            ps = gp()
            for h in range(4):
                tr(ps[:C, h * 128:(h + 1) * 128], qkv[:, 4 + h, tc_], ident[:, :], [qkv, ident], [ps])
            act(k_tok[:C].rearrange("p h d -> p (h d)"), ps[:C, :], AF.Copy, [ps], [k_tok])
            ps = gp()
            for h in range(4):
                tr(ps[:C, h * 128:(h + 1) * 128], qkv[:, 8 + h, tc_], ident[:, :], [qkv, ident], [ps])
            S.op("dve", lambda e: e.tensor_copy(v_tok[:C].rearrange("p h d -> p (h d)"), ps[:C, :]), r=[ps], w=[v_tok])
            bgj = betag[:C, j, :]
            ps = gp()
            mm(ps[:C, 0:4], triT(C), betag[:C, j, 4:8], [dnc, betag], [ps])
            mm(ps[:, 8:12], ones_f[:C, :], betag[:C, j, 4:8], [ones_f, betag], [ps])
            S.op("dve", lambda e: e.tensor_copy(sm[:C, 0:4], ps[:C, 0:4]), r=[ps], w=[sm])
            act(sm[:C, 4:8], ps[:C, 0:4], AF.Exp, [ps], [sm])
            act(sm[:, 16:20], ps[:, 8:12], AF.Exp, [ps], [sm])
            S.op("dve", lambda e: e.tensor_tensor(sm[:C, 8:12], ps[:C, 8:12], sm[:C, 0:4], ALU.subtract), r=[ps, sm], w=[sm])
            act(sm[:C, 8:12], sm[:C, 8:12], AF.Exp, [sm], [sm])
            S.op("dve", lambda e: e.tensor_scalar(sm[:C, 12:16], betag[:C, j, 0:4], -1.0, None, ALU.mult), r=[betag], w=[sm])
            S.op("dve", lambda e: e.tensor_tensor(trig[:C, :, :C], dnc[:C, 0, :, :C], betag[:C, j, 4:8].unsqueeze(2).broadcast_to([C, 4, C]), ALU.mult), r=[dnc, betag], w=[trig])
            ps = gp()
            for h in range(4):
                mm(ps[:C, h * C:(h + 1) * C], ustr(C), trig[:C, h, :C], [dnc, trig], [ps])
            act(decT[:C, :, :C], ps[:C, :4 * C].rearrange("p (h c) -> p h c", h=4), AF.Exp, [ps], [decT])
            S.op("pool", lambda e: e.tensor_tensor(decTs[:C, :, :C], decT[:C, :, :C], maskS(C), ALU.mult), r=[decT, dnc], w=[decTs])
            S.op("pool", lambda e: e.tensor_tensor(decT[:C, :, :C], decT[:C, :, :C], maskI(C), ALU.mult), r=[decT, dnc], w=[decT])
            psK = gp()
            for h in range(4):
                mm(psK[:C, h * C:(h + 1) * C], qkv[:, 4 + h, tc_], qkv[:, 4 + h, tc_], [qkv], [psK])
            psQ = gp()
            for h in range(4):
                mm(psQ[:C, h * C:(h + 1) * C], qkv[:, 4 + h, tc_], qkv[:, h, tc_], [qkv], [psQ])
            S.op("dve", lambda e: e.tensor_tensor(qkt[:C, :, :C], psQ[:C, :4 * C].rearrange("p (h c) -> p h c", h=4), decT[:C, :, :C], ALU.mult), r=[psQ, decT], w=[qkt])
            S.op("dve", lambda e: e.tensor_tensor(QmT[0][:C, :, :C], psK[:C, :4 * C].rearrange("p (h c) -> p h c", h=4), sm[:C, 12:16].unsqueeze(2).broadcast_to([C, 4, C]), ALU.mult), r=[psK, sm], w=[QmT[0]])
            S.op("dve", lambda e: e.tensor_tensor(QmT[0][:C, :, :C], QmT[0][:C, :, :C], decTs[:C, :, :C], ALU.mult), r=[QmT[0], decTs], w=[QmT[0]])
            ps = gp()
            for h in range(4):
                tr(ps[:C, h * C:(h + 1) * C], QmT[0][:C, h, :C], ident[:C, :C], [QmT[0], ident], [ps])
            act(Qm[0][:C, :, :C], ps[:C, :4 * C].rearrange("p (h c) -> p h c", h=4), AF.Copy, [ps], [Qm[0]])
            S.op("pool", lambda e: e.tensor_tensor(FmT[0][:C, :, :C], QmT[0][:C, :, :C], identR(C), ALU.add), r=[QmT[0], dnc], w=[FmT[0]])
            S.op("dve", lambda e: e.tensor_tensor(keg[:C, :, :], k_tok[:C, :, :], sm[:C, 4:8].unsqueeze(2).broadcast_to([C, 4, 128]), ALU.mult), r=[k_tok, sm], w=[keg])
            yield
            for lv in range(nlev):
                if lv % 2 == 1:
                    yield
                a, b_ = lv % 2, (lv + 1) % 2
                lastlv = (lv == nlev - 1)
                if not lastlv:
                    for hh in range(2):
                        ps = gp()
                        for h2 in range(2):
                            h = hh * 2 + h2
                            if lv == 0:
                                mm(ps[:C, h2 * 256: h2 * 256 + 128], FmT[a][:C, h, :C], v_tok[:C, h, :], [FmT[a], v_tok], [ps])
                                mm(ps[:C, h2 * 256 + 128: h2 * 256 + 256], FmT[a][:C, h, :C], keg[:C, h, :], [FmT[a], keg], [ps])
                            else:
                                mm(ps[:C, h2 * 256:(h2 + 1) * 256], FmT[a][:C, h, :C], Xb[a][:C, h, :], [FmT[a], Xb[a]], [ps])
                        eng = "act" if hh == 0 else "dve"
                        if eng == "act":
                            act(Xb[b_][:C, hh * 2:hh * 2 + 2, :].rearrange("p h d -> p (h d)"), ps[:C, :], AF.Copy, [ps], [Xb[b_]])
                        else:
                            S.op("dve", lambda e: e.tensor_copy(Xb[b_][:C, hh * 2:hh * 2 + 2, :].rearrange("p h d -> p (h d)"), ps[:C, :]), r=[ps], w=[Xb[b_]])
                    ps1 = gp()
                    for h in range(4):
                        mm(ps1[:C, h * C:(h + 1) * C], QmT[a][:C, h, :C], Qm[a][:C, h, :C], [QmT[a], Qm[a]], [ps1])
                    ps2 = gp()
                    for h in range(4):
                        mm(ps2[:C, h * C:(h + 1) * C], Qm[a][:C, h, :C], QmT[a][:C, h, :C], [QmT[a], Qm[a]], [ps2])
                    act(Qm[b_][:C, :, :C], ps1[:C, :4 * C].rearrange("p (h c) -> p h c", h=4), AF.Copy, [ps1], [Qm[b_]])
                    S.op("dve", lambda e: e.tensor_copy(QmT[b_][:C, :, :C], ps2[:C, :4 * C].rearrange("p (h c) -> p h c", h=4)), r=[ps2], w=[QmT[b_]])
                    S.op("pool", lambda e: e.tensor_tensor(FmT[b_][:C, :, :C], QmT[b_][:C, :, :C], identR(C), ALU.add), r=[QmT[b_], dnc], w=[FmT[b_]])
                else:
                    psv = gp()
                    for h in range(4):
                        mm(psv[:C, h * 128:(h + 1) * 128], FmT[a][:C, h, :C], Xb[a][:C, h, 0:128], [FmT[a], Xb[a]], [psv])
                    psw = gp()
                    for h in range(4):
                        mm(psw[:, h * C:(h + 1) * C], Xb[a][:C, h, 128:256], FmT[a][:C, h, :C], [FmT[a], Xb[a]], [psw])
                    S.op("dve", lambda e: e.tensor_tensor(xvb[:C, :, :], psv[:C, :].rearrange("p (h d) -> p h d", h=4), betag[:C, j, 0:4].unsqueeze(2).broadcast_to([C, 4, 128]), ALU.mult), r=[psv, betag], w=[xvb])
                    act(xwT[:, :, :C], psw[:, :4 * C].rearrange("p (h c) -> p h c", h=4), AF.Copy, [psw], [xwT])
            yield
            psW = gp()
            for h in range(4):
                mm(psW[:C, h * 128:(h + 1) * 128], xwT[:, h, :C], Sst[:, h, :], [xwT, Sst], [psW])
            S.op("dve", lambda e: e.tensor_tensor(v_new[:C, :, :], psW[:C, :].rearrange("p (h d) -> p h d", h=4), sm[:C, 12:16].unsqueeze(2).broadcast_to([C, 4, 128]), ALU.mult), r=[psW, sm], w=[v_new])
            S.op("dve", lambda e: e.tensor_tensor(v_new[:C, :, :], v_new[:C, :, :], xvb[:C, :, :], ALU.add), r=[v_new, xvb], w=[v_new])
            if owned:
                psO1 = gp()
                for h in range(4):
                    mm(psO1[:C, h * 128:(h + 1) * 128], qkv[:, h, tc_], Sst[:, h, :], [qkv, Sst], [psO1])
                S.op("dve", lambda e: e.tensor_tensor(o1s[:C, :, :], psO1[:C, :].rearrange("p (h d) -> p h d", h=4), sm[:C, 4:8].unsqueeze(2).broadcast_to([C, 4, 128]), ALU.mult), r=[psO1, sm], w=[o1s])
                psO2 = gp()
                for h in range(4):
                    mm(psO2[:C, h * 128:(h + 1) * 128], qkt[:C, h, :C], v_new[:C, h, :], [qkt, v_new], [psO2])
                S.op("dve", lambda e: e.tensor_tensor(o_tok[:C].rearrange("p h d -> p (h d)"), o1s[:C].rearrange("p h d -> p (h d)"), psO2[:C, :], ALU.add), r=[o1s, psO2], w=[o_tok])
            S.op("dve", lambda e: e.tensor_tensor(kdec[:C, :, :], k_tok[:C, :, :], sm[:C, 8:12].unsqueeze(2).broadcast_to([C, 4, 128]), ALU.mult), r=[k_tok, sm], w=[kdec])
            for hh in range(2):
                psS = gp()
                for h2 in range(2):
                    h = hh * 2 + h2
                    mm(psS[:, h2 * 128:(h2 + 1) * 128], kdec[:C, h, :], v_new[:C, h, :], [kdec, v_new], [psS])
                for h2 in range(2):
                    h = hh * 2 + h2
                    S.op("dve", lambda e: e.scalar_tensor_tensor(Sst[:, h, :], Sst[:, h, :], sm[:, 16 + h:17 + h], psS[:, h2 * 128:(h2 + 1) * 128], ALU.mult, ALU.add), r=[Sst, sm, psS], w=[Sst])
            if sample:
                S.dma("sp", S_s[j], Sst[:], Sst, r=[Sst], w=[S_s])
            elif last and j == nch - 1:
                S.dma("sp", S_p[:], Sst[:], Sst, r=[Sst], w=[S_p])
            if owned:
                for h in range(4):
                    act(junk64[:C, :], o_tok[:C, h, :], AF.Square, [o_tok], [junk64, sm], accum_out=sm[:C, 28 + h:29 + h])
                act(sm[:C, 24:28], sm[:C, 28:32], AF.Sqrt, [sm], [sm], scale=1.0 / 128, bias=RMS_EPS)
                S.op("dve", lambda e: e.reciprocal(sm[:C, 24:28], sm[:C, 24:28]), r=[sm], w=[sm])
                S.op("dve", lambda e: e.tensor_tensor(o_tok[:C, :, :], o_tok[:C, :, :], sm[:C, 24:28].unsqueeze(2).broadcast_to([C, 4, 128]), ALU.mult), r=[o_tok, sm], w=[o_tok])
                ps = gp()
                for h in range(4):
                    tr(ps[:, h * C:(h + 1) * C], o_tok[:C, h, :], ident[:C, :C], [o_tok, ident], [ps])
                S.op("dve", lambda e: e.scalar_tensor_tensor(o_dnT[:, :, tc_], ps[:, :4 * C].rearrange("p (h c) -> p h c", h=4), normw_sb[:, 0:1], zsT[:, :, tc_], ALU.mult, ALU.mult), r=[ps, normw_sb, zsT], w=[o_dnT])
        if not owned or stop == 'dn':
            return
        if debug and sample:
            S.op("pool", lambda e: e.tensor_copy(mrgA[:, 0:4, :128], o_dnT[:, :, :128]), r=[o_dnT], w=[mrgA])
            S.dma("sp", dbg["d_odn"][:], mrgA[:, 0:4, :128], mrgA, r=[mrgA], w=[dbg["d_odn"]])
        if stop == 'dbgodn':
            return
        scope('attn')
        if sample:
            for s_ in range(4):
                grp = [(h, h * 32, s_ * 32, 32) for h in range(8)]
                yield from attention_stream([(grp, bank_acc)], 256, s_, sample=True)
        else:
            for h in range(0, 8, 4):
                st2 = [([(h, 0, 0, W), (h + 1, W, 0, W)], bank_acc), ([(h + 2, 0, 0, W), (h + 3, W, 0, W)], bank_acc2)]
                yield from attention_stream(st2, 2 * W, ti_idx, sample=False)
        if debug and sample:
            S.op("pool", lambda e: e.tensor_copy(mrgA[:64, 0:8, :128], oT_sb[:, :, :128]), r=[oT_sb], w=[mrgA])
            S.dma("sp", dbg["d_osb"][:], mrgA[:64, 0:8, :128], mrgA, r=[mrgA], w=[dbg["d_osb"]])
        if stop in ('attn', 'attn1', 'attn2') or (stop or '').startswith('al'):
            return
        yield
        scope('merge')
        for half in range(2):
            for pc in range(2):
                if half == 0:
                    wu = wload(wsc_usb[:, :, pc * 512:(pc + 1) * 512], wsc_usb, npart=64, nk=8)
                else:
                    wu = wload(wsc_udn[:, :, pc * 512:(pc + 1) * 512], wsc_udn, npart=128, nk=4)
                wg = wload(wsc[:, :, OFF_G + half * 1024 + pc * 512: OFF_G + half * 1024 + (pc + 1) * 512], wsc)
                yield
                for c4 in range(4):
                    cc = pc * 4 + c4
                    psm_ = gp()
                    if half == 0:
                        for h in range(8):
                            mm(psm_[:, :W], wu[:64, h, c4 * 128:(c4 + 1) * 128], oT_sb[:, h, :W], [wu, oT_sb], [psm_], start=(h == 0), stop=(h == 7))
                    else:
                        for f in range(4):
                            mm(psm_[:, :W], wu[:, f, c4 * 128:(c4 + 1) * 128], o_dnT[:, f, :W], [wu, o_dnT], [psm_], start=(f == 0), stop=(f == 3))
                    psg = gp()
                    for kc in range(8):
                        mm(psg[:, :W], wg[:, kc, c4 * 128:(c4 + 1) * 128], xTb[:, kc, :W], [wg, xTb], [psg], start=(kc == 0), stop=(kc == 7))
                    g_ = gt[cc % 2]
                    act(g_[:, :W], psg[:, :W], AF.Sigmoid, [psg, bg_sb], [g_], bias=bg_sb[:, half * 8 + cc: half * 8 + cc + 1])
                    if half == 0:
                        S.op("dve", lambda e: e.tensor_tensor(mrgA[:, cc, :W], g_[:, :W], psm_[:, :W], ALU.mult), r=[g_, psm_], w=[mrgA])
                    else:
                        S.op("dve", lambda e: e.tensor_tensor(mtmp[:, :W], g_[:, :W], psm_[:, :W], ALU.mult), r=[g_, psm_], w=[mtmp])
                        S.op("pool", lambda e: e.tensor_tensor(mrgT[:, cc, :W], mtmp[:, :W], mrgA[:, cc, :W], ALU.add), r=[mtmp, mrgA], w=[mrgT])
        if debug and sample:
            S.dma("sp", dbg["d_mrg"][:], mrgA[:, 0:8, :128], mrgA, r=[mrgA], w=[dbg["d_mrg"]])
        if stop == 'merge':
            return
        yield
        scope('ln1')
        for pc in range(2):
            wo = wload(wsc_out[:, :, pc * 512:(pc + 1) * 512], wsc_out)
            for c4 in range(4):
                dmc = pc * 4 + c4
                ps = gp()
                for cc in range(8):
                    mm(ps[:, :W], wo[:, cc, c4 * 128:(c4 + 1) * 128], mrgT[:, cc, :W], [wo, mrgT], [ps], start=(cc == 0), stop=(cc == 7))
                S.op("dve", lambda e: e.scalar_tensor_tensor(pre[:, dmc, :W], xTf[:, dmc, :W], ALPHA, ps[:, :W], ALU.mult, ALU.add), r=[xTf, ps], w=[pre])
        pss = bank_acc
        for dmc in range(8):
            mm(pss[:, :W], ones_f[:, :], pre[:, dmc, :W], [ones_f, pre], [pss], start=(dmc == 0), stop=(dmc == 7))
        S.op("dve", lambda e: e.tensor_scalar(mean[:, :W], pss[:, :W], 1.0 / D, None, ALU.mult), r=[pss], w=[mean])
        for dmc in range(8):
            S.op("pool", lambda e: e.tensor_tensor(pre[:, dmc, :W], pre[:, dmc, :W], mean[:, :W], ALU.subtract), r=[pre, mean], w=[pre])
        psq = bank_acc2
        for dmc in range(8):
            S.op("pool", lambda e: e.tensor_tensor(sqb[:, :W], pre[:, dmc, :W], pre[:, dmc, :W], ALU.mult), r=[pre], w=[sqb])
            mm(psq[:, :W], ones_f[:, :], sqb[:, :W], [ones_f, sqb], [psq], start=(dmc == 0), stop=(dmc == 7))
        act(rstd[:, :W], psq[:, :W], AF.Sqrt, [psq], [rstd], scale=1.0 / D, bias=LN_EPS)
        S.op("dve", lambda e: e.reciprocal(rstd[:, :W], rstd[:, :W]), r=[rstd], w=[rstd])
        for dmc in range(8):
            S.op("dve", lambda e: e.tensor_tensor(pre[:, dmc, :W], pre[:, dmc, :W], rstd[:, :W], ALU.mult), r=[pre, rstd], w=[pre])
            S.op("dve", lambda e: e.tensor_scalar(hTf[:, dmc, :W], pre[:, dmc, :W], l1g[:, dmc:dmc + 1], l1b[:, dmc:dmc + 1], ALU.mult, ALU.add), r=[pre, l1g, l1b], w=[hTf])
        S.op("pool", lambda e: e.tensor_copy(hTb[:, :, :W], hTf[:, :, :W]), r=[hTf], w=[hTb])
        if debug and sample:
            S.dma("sp", dbg["d_h"][:], hTf[:, :, :128], hTf, r=[hTf], w=[dbg["d_h"]])
        if stop == 'ln1':
            return
        yield
        scope('peerq')
        for pc in range(4):
            yield
            wq_ = wload(wsc_pq[:, :, pc * 512:(pc + 1) * 512], wsc_pq)
            for c4 in range(4):
                cq = pc * 4 + c4
                ps = gp()
                for kc in range(8):
                    mm(ps[:, :W], wq_[:, kc, c4 * 128:(c4 + 1) * 128], hTb[:, kc, :W], [wq_, hTb], [ps], start=(kc == 0), stop=(kc == 7))
                act(qpT[:, cq, :W], ps[:, :W], AF.Copy, [ps], [qpT])
        for a in range(W // 128):
            ta = slice(a * 128, (a + 1) * 128)
            eidi, gate, h_tok = eidi_l[a], gate_l[a], h_tok_l[a]
            yield
            for g4 in range(4):
                ps = gp()
                for q4 in range(4):
                    hp = g4 * 4 + q4
                    mm(ps[:, q4 * 128:(q4 + 1) * 128], qpT[:, hp, ta], keys_b[:, hp, :], [qpT, keys_b], [ps])
                act(sc[:, :, :].rearrange("p a k -> p (a k)"), ps[:, :], AF.Copy, [ps], [sc])
                for q4 in range(4):
                    hp = g4 * 4 + q4
                    S.op("dve", lambda e: e.max(tv[:, hp, 0:8], sc[:, q4, :]), r=[sc], w=[tv])
                    S.op("dve", lambda e: e.max_index(ti[:, hp, 0:8], tv[:, hp, 0:8], sc[:, q4, :]), r=[sc, tv], w=[ti])
                    S.op("dve", lambda e: e.match_replace(scw[:, :], tv[:, hp, 0:8], sc[:, q4, :], -1e30), r=[sc, tv], w=[scw])
                    S.op("dve", lambda e: e.max(tv[:, hp, 8:16], scw[:, :]), r=[scw], w=[tv])
                    S.op("dve", lambda e: e.max_index(ti[:, hp, 8:16], tv[:, hp, 8:16], scw[:, :]), r=[scw, tv], w=[ti])
            S.op("dve", lambda e: e.tensor_copy(tif[:], ti[:]), r=[ti], w=[tif])
            for h in range(8):
                S.op("dve", lambda e: e.tensor_tensor(cand[:], tv[:, 2 * h, :].unsqueeze(2).broadcast_to([128, 16, 16]),
                                                      tv[:, 2 * h + 1, :].unsqueeze(1).broadcast_to([128, 16, 16]), ALU.add), r=[tv], w=[cand])
                S.op("dve", lambda e: e.scalar_tensor_tensor(cidx[:], tif[:, 2 * h, :].unsqueeze(2).broadcast_to([128, 16, 16]), 128.0,
                                                             tif[:, 2 * h + 1, :].unsqueeze(1).broadcast_to([128, 16, 16]), ALU.mult, ALU.add), r=[tif], w=[cidx])
                cf = cand[:].rearrange("p a b -> p (a b)")
                xf = cidx[:].rearrange("p a b -> p (a b)")
                S.op("dve", lambda e: e.max(tsv[:, h, 0:8], cf), r=[cand], w=[tsv])
                S.op("dve", lambda e: e.match_replace(candw[:, :], tsv[:, h, 0:8], cf, -1e30), r=[cand, tsv], w=[candw])
                S.op("dve", lambda e: e.max(tsv[:, h, 8:16], candw[:, :]), r=[candw], w=[tsv])
                for k in range(16):
                    S.op("dve", lambda e: e.scalar_tensor_tensor(junkp[:, :], cf, tsv[:, h, k:k + 1], xf, ALU.is_equal, ALU.mult,
                                                                 accum_out=eidf[:, h * 16 + k: h * 16 + k + 1]), r=[cand, cidx, tsv], w=[junkp, eidf])
                S.op("dve", lambda e: e.tensor_scalar(psm[:, h:h + 1], tsv[:, h, 0:1], -1.0, None, ALU.mult), r=[tsv], w=[psm])
                act(gate[:, h, :], tsv[:, h, :], AF.Exp, [tsv, psm], [gate, psm], bias=psm[:, h:h + 1], accum_out=psm[:, 8 + h:9 + h])
            S.op("dve", lambda e: e.reciprocal(psm[:, 16:24], psm[:, 8:16]), r=[psm], w=[psm])
            for h in range(8):
                S.op("dve", lambda e: e.tensor_scalar(gate[:, h, :], gate[:, h, :], psm[:, 16 + h:17 + h], None, ALU.mult), r=[gate, psm], w=[gate])
            S.op("dve", lambda e: e.tensor_scalar(eidf[:], eidf[:], 16383.0, None, ALU.min), r=[eidf], w=[eidf])
            S.op("dve", lambda e: e.tensor_copy(eidi[:], eidf[:]), r=[eidf], w=[eidi])
            scope('peer_htok')
            for hh in range(2):
                ps = gp()
                for k4 in range(4):
                    kc = hh * 4 + k4
                    tr(ps[:, k4 * 128:(k4 + 1) * 128], hTf[:, kc, ta], ident[:, :], [hTf, ident], [ps])
                act(h_tok[:, hh * 512:(hh + 1) * 512], ps[:, :], AF.Copy, [ps], [h_tok])
        pending.append(peer_gather(W, out_row0, y_out, sample))

    def peer_gather(W, out_row0, y_out, sample):
        y_buf = y_out
        for a in range(W // 128):
            eidi, gate, h_tok = eidi_l[a], gate_l[a], h_tok_l[a]
            psm = psm2
            scope('peer_u')
            for s_ in range(128):
                ub = ubuf[s_ % 3]
                S.dma("pool", ub[:], peer_u[:, :], ub, r=[peer_u, eidi], w=[ub],
                      indirect=dict(out_offset=None, in_offset=bass.IndirectOffsetOnAxis(ap=eidi[:, s_:s_ + 1], axis=0)))
                S.op("dve", lambda e: e.scalar_tensor_tensor(junku[:], ub[:], 1.0, h_tok[:], ALU.mult, ALU.mult, accum_out=actv[:, s_:s_ + 1]), r=[ub, h_tok], w=[junku, actv])
                if s_ % 4 == 3:
                    yield
            scope('peer_v')
            act(coef[:], actv[:], AF.Gelu, [actv], [coef])
            S.op("dve", lambda e: e.tensor_tensor(coef[:], coef[:], gate[:].rearrange("p h k -> p (h k)"), ALU.mult), r=[coef, gate], w=[coef])
            for s_ in range(128):
                ub = ubuf[s_ % 3]
                S.dma("pool", ub[:], peer_v[:, :], ub, r=[peer_v, eidi], w=[ub],
                      indirect=dict(out_offset=None, in_offset=bass.IndirectOffsetOnAxis(ap=eidi[:, s_:s_ + 1], axis=0)))
                if s_ == 0:
                    S.op("dve", lambda e: e.tensor_scalar(facc[:], ub[:], coef[:, 0:1], None, ALU.mult), r=[ub, coef], w=[facc])
                else:
                    S.op("dve", lambda e: e.scalar_tensor_tensor(facc[:], ub[:], coef[:, s_:s_ + 1], facc[:], ALU.mult, ALU.add), r=[ub, coef, facc], w=[facc])
                if s_ % 4 == 3:
                    yield
            if debug and sample:
                S.dma("sp", dbg["d_ffn"][:], facc[:], facc, r=[facc], w=[dbg["d_ffn"]])
            scope('ln2')
            S.op("dve", lambda e: e.scalar_tensor_tensor(facc[:], h_tok[:], ALPHA, facc[:], ALU.mult, ALU.add), r=[h_tok, facc], w=[facc])
            act(junku[:], facc[:], AF.Copy, [facc], [junku, psm], accum_out=psm[:, 0:1])
            S.op("dve", lambda e: e.tensor_scalar(psm[:, 0:1], psm[:, 0:1], -1.0 / D, None, ALU.mult), r=[psm], w=[psm])
            S.op("dve", lambda e: e.tensor_scalar(facc[:], facc[:], psm[:, 0:1], None, ALU.add), r=[facc, psm], w=[facc])
            act(junku[:], facc[:], AF.Square, [facc], [junku, psm], accum_out=psm[:, 1:2])
            act(psm[:, 2:3], psm[:, 1:2], AF.Sqrt, [psm], [psm], scale=1.0 / D, bias=LN_EPS)
            S.op("dve", lambda e: e.reciprocal(psm[:, 2:3], psm[:, 2:3]), r=[psm], w=[psm])
            S.op("dve", lambda e: e.scalar_tensor_tensor(ybuf[:], facc[:], psm[:, 2:3], l2g[:], ALU.mult, ALU.mult), r=[facc, psm, l2g], w=[ybuf])
            S.op("pool", lambda e: e.tensor_tensor(ybuf[:], ybuf[:], l2b[:], ALU.add), r=[ybuf, l2b], w=[ybuf])
            S.dma("sp", y_out[out_row0 + a * 128: out_row0 + (a + 1) * 128, :], ybuf[:], ybuf, r=[ybuf], w=[y_buf])

    def attention_stream(grp, ZW, idx, sample):
        blocks = []
        loaders = []
        if sample:
            s_ = idx
            blocks.append(dict(ktbuf=KTcur, kt=(lambda cc: KTcur[:, cc, s_ * 32:(s_ + 1) * 32]), vbuf=Vcur,
                               v=(lambda h: Vcur[:32, s_, h * 64:(h + 1) * 64]), nk=32, mask=(mask_s[:32, :256], mask_s), load=None))
            for n_, g8 in enumerate(range(7, -1, -1)):
                i2 = n_ % 2

                def load(i2=i2, g8=g8):
                    for hh in range(2):
                        S.dma("sp", kvstg[0][:64, :].rearrange("p (c k) -> p c k", c=4), kTc[s_, :, hh * 4:(hh + 1) * 4, g8 * 256:(g8 + 1) * 256], kvstg[0], r=[kTc], w=[kvstg[0]])
                        S.op("pool", lambda e: e.tensor_copy(KTblk[i2][:, hh * 4:(hh + 1) * 4, :].rearrange("p c k -> p (c k)"), kvstg[0][:64, :]), r=[kvstg[0]], w=[KTblk[i2]])
                    S.dma("act", kvstg[1][:, :].rearrange("p (b c) -> p b c", b=2), vc[s_, g8 * 256:(g8 + 1) * 256, :].rearrange("(b p) c -> p b c", p=128), kvstg[1], r=[vc], w=[kvstg[1]])
                    S.op("dve", lambda e: e.tensor_copy(Vblk[i2][:].rearrange("p b c -> p (b c)"), kvstg[1][:, :]), r=[kvstg[1]], w=[Vblk[i2]])
                for b4 in range(1, -1, -1):
                    blocks.append(dict(ktbuf=KTblk[i2], kt=(lambda cc, i2=i2, b4=b4: KTblk[i2][:, cc, b4 * 128:(b4 + 1) * 128]), vbuf=Vblk[i2],
                                       v=(lambda h, i2=i2, b4=b4: Vblk[i2][:, b4, h * 64:(h + 1) * 64]), nk=128, mask=None,
                                       load=(load if b4 == 1 else None)))
        else:
            i = idx
            nsub = WP // 128
            for o in range(nsub - 1, -1, -1):
                blocks.append(dict(ktbuf=KTcur, kt=(lambda cc, o=o: KTcur[:, cc, o * 128:(o + 1) * 128]), vbuf=Vcur,
                                   v=(lambda h, o=o: Vcur[:, o, h * 64:(h + 1) * 64]), nk=128, mask=(mask_p[:, o, :], mask_p), load=None))
            n_ = 0
            pt = i - 1
            while pt >= 0:
                i2 = n_ % 2
                n_ += 1
                tiles_ = [pt]

                def load(i2=i2, tiles_=tiles_):
                    for u_, t_ in enumerate(tiles_):
                        S.dma("sp", KTblk[i2][:, :, u_ * WP:(u_ + 1) * WP], KTs[t_][:, :, :], KTblk[i2], r=[KTs[t_]], w=[KTblk[i2]])
                        S.dma("act", Vblk[i2][:, u_ * nsub:(u_ + 1) * nsub, :], Vs[t_][:, :, :], Vblk[i2], r=[Vs[t_]], w=[Vblk[i2]])
                firstb = True
                for u_, t_ in enumerate(tiles_):
                    for o in range(nsub - 1, -1, -1):
                        blocks.append(dict(ktbuf=KTblk[i2], kt=(lambda cc, i2=i2, u_=u_, o=o: KTblk[i2][:, cc, u_ * WP + o * 128: u_ * WP + (o + 1) * 128]),
                                           vbuf=Vblk[i2], v=(lambda h, i2=i2, u_=u_, o=o: Vblk[i2][:, u_ * nsub + o, h * 64:(h + 1) * 64]),
                                           nk=128, mask=None, load=(load if firstb else None), kvalid=(t_ if t_ < 3 else None)))
                        firstb = False
                pt -= 1
        if stop == 'attn1' or (stop or '').startswith('al'):
            blocks = blocks[:1]
        if stop == 'attn2':
            blocks = blocks[:3]
        yield from attention_run(grp, ZW, blocks)

    def attention_run(streams, ZW, blocks):
        nb_ = len(blocks)
        loadable = [i for i, b_ in enumerate(blocks) if b_.get("load") is not None]
        if loadable:
            blocks[loadable[0]]["load"]()
        for bi, blk in enumerate(blocks):
            if bi % 2 == 0:
                yield
            if blk.get("load") is not None:
                nxt = [i for i in loadable if i > bi]
                if nxt:
                    blocks[nxt[0]]["load"]()
            nk = blk["nk"]
            zps = []
            for si, (groups, po) in enumerate(streams):
                zp = gp()
                zps.append(zp)
                for (h, zc, qc, Wg) in groups:
                    mm(zp[:nk, zc:zc + Wg], blk["kt"](h), qTb[:, h, qc:qc + Wg], [blk["ktbuf"], qTb], [zp])
            for si, (groups, po) in enumerate(streams):
                act(Eb[si][:nk, :ZW], zps[si][:nk, :ZW], AF.Exp, [zps[si]], [Eb[si]], scale=0.125)
            for si, (groups, po) in enumerate(streams):
                if blk["mask"] is not None:
                    mk, mkb = blk["mask"]
                    mw = mk.shape[-1]
                    for c0 in range(0, ZW, mw):
                        S.op("pool", lambda e: e.tensor_tensor(Eb[si][:nk, c0:c0 + mw], Eb[si][:nk, c0:c0 + mw], mk, ALU.mult), r=[Eb[si], mkb], w=[Eb[si]])
                if blk.get("kvalid") is not None:
                    kvc = blk["kvalid"]
                    S.op("pool", lambda e: e.tensor_scalar(Eb[si][:nk, :ZW], Eb[si][:nk, :ZW], kval[:nk, kvc:kvc + 1], None, ALU.mult), r=[Eb[si], kval], w=[Eb[si]])
            for si, (groups, po) in enumerate(streams):
                act(Pb[si][:nk, :ZW], Eb[si][:nk, :ZW], AF.Ln, [Eb[si]], [Pb[si]], bias=1.0)
            cps = []
            for si, (groups, po) in enumerate(streams):
                cp = gp()
                cps.append(cp)
                mm(cp[:nk, :ZW], tri_b[:nk, :nk], Pb[si][:nk, :ZW], [tri_b, Pb[si]], [cp], start=True, stop=(bi == 0))
                if bi > 0:
                    mm(cp[:nk, :ZW], ones_b[:128, :nk], Pacc_l[si][:128, :ZW], [ones_b, Pacc_l[si]], [cp], start=False, stop=True)
            for si, (groups, po) in enumerate(streams):
                act(Gb[si][:nk, :ZW], cps[si][:nk, :ZW], AF.Exp, [cps[si]], [Gb[si]], scale=-1.0)
            for si, (groups, po) in enumerate(streams):
                S.op("dve", lambda e: e.tensor_tensor(wTb[si][:nk, :ZW], Eb[si][:nk, :ZW], Gb[si][:nk, :ZW], ALU.mult), r=[Eb[si], Gb[si]], w=[wTb[si]])
            for si, (groups, po) in enumerate(streams):
                Pa = Pacc_l[si]
                if bi == 0:
                    if nk < 128:
                        S.op("pool", lambda e: e.memset(Pa[:, :ZW], 0.0), r=[], w=[Pa])
                    S.op("pool", lambda e: e.tensor_copy(Pa[:nk, :ZW], Pb[si][:nk, :ZW]), r=[Pb[si]], w=[Pa])
                elif bi < nb_ - 1:
                    S.op("pool", lambda e: e.tensor_tensor(Pa[:nk, :ZW], Pa[:nk, :ZW], Pb[si][:nk, :ZW], ALU.add), r=[Pb[si], Pa], w=[Pa])
            for si, (groups, po) in enumerate(streams):
                for gi_, (h, zc, qc, Wg) in enumerate(groups):
                    S.op("pe", lambda e: e.matmul(po[:64, zc:zc + Wg], blk["v"](h), wTb[si][:nk, zc:zc + Wg], start=(bi == 0 and gi_ == 0), stop=(bi == nb_ - 1),
                                                  skip_group_check=True), r=[blk["vbuf"], wTb[si]], w=[po])
        for si, (groups, po) in enumerate(streams):
            for (h, zc, qc, Wg) in groups:
                S.op("dve", lambda e: e.tensor_copy(oT_sb[:, h, qc:qc + Wg], po[:64, zc:zc + Wg]), r=[po], w=[oT_sb])

    W_ = 128
    if do_sample:
        xsv = xT_s[:].rearrange("(k p) t -> p k t", p=128)
        tile(128, 4, 32, 32, (lambda hf: xsv[:, hf * 4:(hf + 1) * 4, :]), xT_s, True, True, True, None, None, 0, True, 0)
    if do_prompt:
        xpv = xT_p[:].rearrange("(k p) t -> p k t", p=128)
        for p in range(n_ptiles):
            tile(WP, 1, WP, 64, (lambda hf, p=p: xpv[:, hf * 4:(hf + 1) * 4, p * WP:(p + 1) * WP]), xT_p, (p % 4 == 3), (p == 0), (p == n_ptiles - 1),
                 None, None, (p // 4) * WP, False, p)
    while pending:
        for pg in list(pending):
            try:
                next(pg)
            except StopIteration:
                pending.remove(pg)
    S.finish(list(dout.values()))
    return nc, S


def _shared_inputs(w_in, b_gate, w_conv, a_log, dt_bias, dn_norm_w, w_up_sb, w_up_dn, w_out, ln1_g, ln1_b,
                   peer_wq, peer_keys, peer_u, peer_v, ln2_g, ln2_b):
    c = make_consts()
    f = np.ascontiguousarray
    d = dict(c)
    d["w_in"] = f(w_in[0])
    d["wconvT"] = f(w_conv[0].reshape(4, 12, 128).transpose(2, 1, 0))
    d["bgT"] = f(b_gate[0].reshape(16, 128).T)
    d["alog"] = f(np.broadcast_to(a_log[0][None, :], (128, 4)))
    d["dtb"] = f(np.broadcast_to(dt_bias[0][None, :], (128, 4)))
    d["normw"] = f(dn_norm_w[0].reshape(128, 1))
    d["w_up_sb"] = f(w_up_sb[0].reshape(8, 64, D).transpose(1, 0, 2))
    d["w_up_dn"] = f(w_up_dn[0].reshape(4, 128, D).transpose(1, 0, 2))
    d["w_out"] = f(w_out[0].reshape(8, 128, D).transpose(1, 0, 2))
    d["ln1g"] = f(ln1_g[0].reshape(8, 128).T)
    d["ln1b"] = f(ln1_b[0].reshape(8, 128).T)
    d["peer_wq"] = f(peer_wq[0].reshape(8, 128, 2048).transpose(1, 0, 2))
    d["keysT"] = f(peer_keys[0].reshape(16, 128, 128).transpose(2, 0, 1))
    d["peer_u"] = f(peer_u[0])
    d["peer_v"] = f(peer_v[0])
    d["ln2g"] = f(np.broadcast_to(ln2_g[0][None, :], (128, D)))
    d["ln2b"] = f(np.broadcast_to(ln2_b[0][None, :], (128, D)))
    return d


def _sample_inputs(c, x_sample, cache_sb_k, cache_sb_v, state_dn_ssm, state_dn_conv):
    f = np.ascontiguousarray
    sl = slice(4 * c, 4 * c + 4)
    d = {}
    d["xT_s"] = f(x_sample[sl].reshape(128, D).T)
    d["kTc"] = f(cache_sb_k[0, sl].transpose(0, 3, 2, 1))
    d["vc"] = f(cache_sb_v[0, sl].reshape(4, PAST, 512))
    d["S0"] = f(state_dn_ssm[0, sl].transpose(0, 2, 1, 3))
    d["convT"] = f(state_dn_conv[0, sl].reshape(4, 3, 12, 128).transpose(3, 2, 0, 1))
    return d


_PROG = {}
STOP = None


def kernel(x_prompt, x_sample, cache_sb_k, cache_sb_v, state_dn_ssm, state_dn_conv,
           w_in, b_gate, w_conv, a_log, dt_bias, dn_norm_w, w_up_sb, w_up_dn, w_out,
           ln1_g, ln1_b, peer_wq, peer_keys, peer_u, peer_v, ln2_g, ln2_b):
    args = [np.asarray(a, dtype=np.float32) for a in (x_prompt, x_sample, cache_sb_k, cache_sb_v, state_dn_ssm, state_dn_conv,
            w_in, b_gate, w_conv, a_log, dt_bias, dn_norm_w, w_up_sb, w_up_dn, w_out,
            ln1_g, ln1_b, peer_wq, peer_keys, peer_u, peer_v, ln2_g, ln2_b)]
    (x_prompt, x_sample, cache_sb_k, cache_sb_v, state_dn_ssm, state_dn_conv,
     w_in, b_gate, w_conv, a_log, dt_bias, dn_norm_w, w_up_sb, w_up_dn, w_out,
     ln1_g, ln1_b, peer_wq, peer_keys, peer_u, peer_v, ln2_g, ln2_b) = args
    if "nc" not in _PROG:
        _PROG["nc"] = build_program(stop=STOP)[0]
    nc = _PROG["nc"]
    shared = _shared_inputs(w_in, b_gate, w_conv, a_log, dt_bias, dn_norm_w, w_up_sb, w_up_dn, w_out, ln1_g, ln1_b,
                            peer_wq, peer_keys, peer_u, peer_v, ln2_g, ln2_b)
    in_maps = []
    for c in range(NCORE):
        b, r = c // 4, c % 4
        d = dict(shared)
        d.update(_sample_inputs(c, x_sample, cache_sb_k, cache_sb_v, state_dn_ssm, state_dn_conv))
        sh = (3 - r) * WP
        xt = np.zeros((D, SEQ), np.float32)
        xt[:, sh:] = x_prompt[b, :SEQ - sh].T
        d["xT_p"] = xt
        kv = np.ones((128, 4), np.float32)
        kv[:, :3 - r] = 0.0
        d["kval"] = kv
        if STOP == 'dn':
            d["peer_u"] = d["peer_u"][:8]
            d["peer_v"] = d["peer_v"][:8]
        in_maps.append(d)
    res = run_bass_kernel_spmd(nc, in_maps, core_ids=list(range(NCORE))).results
    B = 2
    y_p = np.zeros((B, SEQ, D), np.float32)
    kn_p = np.zeros((1, B, SEQ, 8, 64), np.float32)
    vn_p = np.zeros((1, B, SEQ, 8, 64), np.float32)
    y_s = np.zeros((32, 32, D), np.float32)
    kn_s = np.zeros((1, 32, 32, 8, 64), np.float32)
    vn_s = np.zeros((1, 32, 32, 8, 64), np.float32)
    S_p = np.zeros((1, B, 4, 128, 128), np.float32)
    S_s = np.zeros((1, 32, 4, 128, 128), np.float32)
    cv_p = np.zeros((1, B, 3, 1536), np.float32)
    cv_s = np.zeros((1, 32, 3, 1536), np.float32)
    for c in range(NCORE):
        b, r = c // 4, c % 4
        o = res[c]
        for m in range(NT_FULL // 4):
            t0 = (4 * m + r) * WP
            y_p[b, t0:t0 + WP] = o["y_p"][m * WP:(m + 1) * WP]
            kn_p[0, b, t0:t0 + WP] = o["kn_p"][m * WP:(m + 1) * WP].reshape(WP, 8, 64)
            vn_p[0, b, t0:t0 + WP] = o["vn_p"][m * WP:(m + 1) * WP].reshape(WP, 8, 64)
        if r == 3:
            S_p[0, b] = o["S_p"].transpose(1, 0, 2)
            cv_p[0, b] = o["cv_p"]
        y_s[4 * c:4 * c + 4] = o["y_s"].reshape(4, 32, D)
        kn_s[0, 4 * c:4 * c + 4] = o["kn_s"].reshape(4, 32, 8, 64)
        vn_s[0, 4 * c:4 * c + 4] = o["vn_s"].reshape(4, 32, 8, 64)
        S_s[0, 4 * c:4 * c + 4] = o["S_s"].transpose(0, 2, 1, 3)
        cv_s[0, 4 * c:4 * c + 4] = o["cv_s"]
    return (y_p, y_s, kn_p, vn_p, kn_s, vn_s, S_p, S_s, cv_p, cv_s)
```
